# Optimizing a Trainium2 kernel written in Bass

```python
import math
import jax, jax.numpy as jnp
from jax import lax
import numpy as np

D_MODEL = 1024
BATCH = 4
SEQ = 8192
DEPTH = 4

N_MIXERS = 2
N_ATTN_LAYERS = (DEPTH + 1) // 2
N_SSD_LAYERS = DEPTH // 2
N_HEADS = 16
N_KV_HEADS = 4
HEAD_DIM = 64
GQA_GROUP = N_HEADS // N_KV_HEADS
WINDOW = 128
ATTN_BLOCK = 128
ROT_DIM = HEAD_DIM // 4
ROPE_THETA = 500000.0
Q_DIM = N_HEADS * HEAD_DIM
KV_DIM = N_KV_HEADS * HEAD_DIM
QKV_DIM = Q_DIM + 2 * KV_DIM
SSD_EXPAND = 2
D_INNER = SSD_EXPAND * D_MODEL
SSD_HEAD_DIM = 64
SSD_HEADS = D_INNER // SSD_HEAD_DIM
SSD_GROUPS = 8
HEADS_PER_GROUP = SSD_HEADS // SSD_GROUPS
D_STATE = 128
SSD_CONV = 5
SSD_CHUNK = 128
CONV_DIM = D_INNER + 2 * SSD_GROUPS * D_STATE
SSD_IN_DIM = D_INNER + CONV_DIM + 2 * SSD_HEADS
D_FF = 2816
FFN_CONV = 3
EPS = 1e-6

kernel_name = "bidir_swa_sink_ssd_convffn_hybrid"


def rmsnorm(x, g):
    xf = x.astype(jnp.float32)
    y = xf * lax.rsqrt(jnp.mean(xf * xf, axis=-1, keepdims=True) + EPS)
    return (y * g.astype(jnp.float32)).astype(x.dtype)


def dwconv_centred(x, w, b):
    k_w = w.shape[0]
    pad = k_w // 2
    s = x.shape[1]
    xp = jnp.pad(x, ((0, 0), (pad, pad), (0, 0)))
    y = b + xp[:, 0:s] * w[0]
    for k in range(1, k_w):
        y = y + xp[:, k:k + s] * w[k]
    return y


def rope_partial(t, cos, sin):
    half = ROT_DIM // 2
    c = cos[None, :, None, :]
    s = sin[None, :, None, :]
    t1 = t[..., :half].astype(jnp.float32)
    t2 = t[..., half:ROT_DIM].astype(jnp.float32)
    rot = jnp.concatenate([t1 * c - t2 * s, t2 * c + t1 * s], axis=-1).astype(t.dtype)
    return jnp.concatenate([rot, t[..., ROT_DIM:]], axis=-1)


def window_attention(x, norm_g, w_qkv, q_g, k_g, sink, w_o, cos, sin):
    bsz, s_len, _ = x.shape
    nb = s_len // ATTN_BLOCK
    h = rmsnorm(x, norm_g)
    qkv = h @ w_qkv
    q = qkv[..., :Q_DIM].reshape(bsz, s_len, N_HEADS, HEAD_DIM)
    k = qkv[..., Q_DIM:Q_DIM + KV_DIM].reshape(bsz, s_len, N_KV_HEADS, HEAD_DIM)
    v = qkv[..., Q_DIM + KV_DIM:].reshape(bsz, s_len, N_KV_HEADS, HEAD_DIM)
    q = rope_partial(rmsnorm(q, q_g), cos, sin)
    k = rope_partial(rmsnorm(k, k_g), cos, sin)
    q = q.reshape(bsz, nb, ATTN_BLOCK, N_KV_HEADS, GQA_GROUP, HEAD_DIM)

    def band(t):
        tp = jnp.pad(t, ((0, 0), (ATTN_BLOCK, ATTN_BLOCK), (0, 0), (0, 0)))
        tp = tp.reshape(bsz, nb + 2, ATTN_BLOCK, N_KV_HEADS, HEAD_DIM)
        return jnp.concatenate([tp[:, :-2], tp[:, 1:-1], tp[:, 2:]], axis=2)

    kb = band(k)
    vb = band(v)
    scale = HEAD_DIM ** -0.5
    sc = jnp.einsum('bnqkgd,bntkd->bnkgqt', q, kb).astype(jnp.float32) * scale
    blk = jnp.arange(nb)[:, None] * ATTN_BLOCK
    qpos = blk + jnp.arange(ATTN_BLOCK)[None, :]
    kpos = blk - ATTN_BLOCK + jnp.arange(3 * ATTN_BLOCK)[None, :]
    valid = (jnp.abs(qpos[:, :, None] - kpos[:, None, :]) <= WINDOW) \
        & (kpos >= 0)[:, None, :] & (kpos < s_len)[:, None, :]
    sc = jnp.where(valid[None, :, None, None], sc, -1e30)
    sink_l = sink.astype(jnp.float32).reshape(N_KV_HEADS, GQA_GROUP)[None, None, :, :, None, None]
    m = jnp.maximum(jnp.max(sc, axis=-1, keepdims=True), sink_l)
    p = jnp.exp(sc - m)
    denom = jnp.sum(p, axis=-1, keepdims=True) + jnp.exp(sink_l - m)
    p = (p / denom).astype(v.dtype)
    o = jnp.einsum('bnkgqt,bntkd->bnqkgd', p, vb).reshape(bsz, s_len, Q_DIM)
    return o @ w_o


def ssd_chunked(x, dt, a_diag, bm, cm):
    b, s_len, _, p = x.shape
    c = s_len // SSD_CHUNK
    xf = x.astype(jnp.float32)
    xdt = (xf * dt[..., None]).reshape(b, c, SSD_CHUNK, SSD_GROUPS, HEADS_PER_GROUP, p)
    a = (dt * a_diag).reshape(b, c, SSD_CHUNK, SSD_GROUPS, HEADS_PER_GROUP)
    bc = bm.astype(jnp.float32).reshape(b, c, SSD_CHUNK, SSD_GROUPS, D_STATE)
    cc = cm.astype(jnp.float32).reshape(b, c, SSD_CHUNK, SSD_GROUPS, D_STATE)
    a_cum = jnp.cumsum(a, axis=2)
    diff = a_cum[:, :, :, None] - a_cum[:, :, None, :]
    lower = jnp.tril(jnp.ones((SSD_CHUNK, SSD_CHUNK), dtype=bool))[:, :, None, None]
    decay_mat = jnp.exp(jnp.where(lower, diff, -jnp.inf))
    cb = jnp.einsum('bclgn,bcsgn->bclsg', cc, bc)
    y_diag = jnp.einsum('bclsgr,bcsgrp->bclgrp', cb[..., None] * decay_mat, xdt)
    decay_to_end = jnp.exp(a_cum[:, :, -1:] - a_cum)
    states = jnp.einsum('bclgn,bclgrp->bcgrpn', bc, xdt * decay_to_end[..., None])
    chunk_decay = jnp.exp(a_cum[:, :, -1])

    def step(h, inp):
        st, dc = inp
        return dc[..., None, None] * h + st, h

    h0 = jnp.zeros((b, SSD_GROUPS, HEADS_PER_GROUP, p, D_STATE), jnp.float32)
    _, prev = lax.scan(step, h0, (jnp.moveaxis(states, 1, 0), jnp.moveaxis(chunk_decay, 1, 0)))
    prev = jnp.moveaxis(prev, 0, 1)
    y_off = jnp.einsum('bclgn,bcgrpn->bclgrp', cc, prev) * jnp.exp(a_cum)[..., None]
    return (y_diag + y_off).reshape(b, s_len, SSD_HEADS, p)


def ssd_mixer(x, norm_g, w_in, conv_w, conv_b, dt_bias, a_log, d_skip, gate_g, w_out):
    bsz, s_len, _ = x.shape
    h = rmsnorm(x, norm_g)
    zxbcdt = h @ w_in
    z = zxbcdt[..., :D_INNER]
    xbc = zxbcdt[..., D_INNER:D_INNER + CONV_DIM]
    dt_raw = zxbcdt[..., D_INNER + CONV_DIM:]
    xbc = jax.nn.silu(dwconv_centred(xbc, conv_w, conv_b))
    gn = SSD_GROUPS * D_STATE
    xs = xbc[..., :D_INNER].reshape(bsz, s_len, SSD_HEADS, SSD_HEAD_DIM)
    bm = xbc[..., D_INNER:D_INNER + gn].reshape(bsz, s_len, SSD_GROUPS, D_STATE)
    cm = xbc[..., D_INNER + gn:].reshape(bsz, s_len, SSD_GROUPS, D_STATE)
    dt = jax.nn.softplus(dt_raw.astype(jnp.float32).reshape(bsz, s_len, 2, SSD_HEADS)
                         + dt_bias.astype(jnp.float32))
    a_diag = -jnp.exp(a_log.astype(jnp.float32))
    flip = lambda t: jnp.flip(t, axis=1)
    y_fwd = ssd_chunked(xs, dt[:, :, 0], a_diag[0], bm, cm)
    y_bwd = flip(ssd_chunked(flip(xs), flip(dt[:, :, 1]), a_diag[1], flip(bm), flip(cm)))
    y = y_fwd + y_bwd + xs.astype(jnp.float32) * d_skip.astype(jnp.float32)[:, None]
    y = y.reshape(bsz, s_len, D_INNER) * jax.nn.silu(z.astype(jnp.float32))
    yg = y.reshape(bsz, s_len, SSD_GROUPS, D_INNER // SSD_GROUPS)
    yg = yg * lax.rsqrt(jnp.mean(yg * yg, axis=-1, keepdims=True) + EPS)
    y = yg.reshape(bsz, s_len, D_INNER) * gate_g.astype(jnp.float32)
    return y.astype(x.dtype) @ w_out


def conv_ffn(x, norm_g, w_up, conv_w, conv_b, w_down):
    h = rmsnorm(x, norm_g) @ w_up
    h = dwconv_centred(h, conv_w, conv_b)
    gate = h[..., :D_FF]
    val = h[..., D_FF:]
    return (jax.nn.silu(gate) * val) @ w_down


def setup_inputs(seed: int = 0) -> dict:
    key = jax.random.key(seed)
    ks = jax.random.split(key, 24)
    f32 = jnp.float32

    def nrm(k, shape, scale):
        return jax.random.normal(k, shape, f32) * scale

    na, ns = N_ATTN_LAYERS, N_SSD_LAYERS
    dt0 = jnp.exp(jax.random.uniform(ks[11], (ns, 2, SSD_HEADS), f32,
                                     minval=math.log(1e-3), maxval=math.log(1e-1)))
    return {
        "x": nrm(ks[0], (BATCH, SEQ, D_MODEL), 1.0),
        "attn_norm": 1.0 + nrm(ks[1], (na, D_MODEL), 0.02),
        "attn_w_qkv": nrm(ks[2], (na, D_MODEL, QKV_DIM), D_MODEL ** -0.5),
        "attn_q_norm": 1.0 + nrm(ks[3], (na, HEAD_DIM), 0.02),
        "attn_k_norm": 1.0 + nrm(ks[4], (na, HEAD_DIM), 0.02),
        "attn_sink": nrm(ks[5], (na, N_HEADS), 0.5),
        "attn_w_o": nrm(ks[6], (na, Q_DIM, D_MODEL), Q_DIM ** -0.5),
        "ssd_norm": 1.0 + nrm(ks[7], (ns, D_MODEL), 0.02),
        "ssd_w_in": nrm(ks[8], (ns, D_MODEL, SSD_IN_DIM), D_MODEL ** -0.5),
        "ssd_conv_w": nrm(ks[9], (ns, SSD_CONV, CONV_DIM), SSD_CONV ** -0.5),
        "ssd_conv_b": nrm(ks[10], (ns, CONV_DIM), 0.02),
        "ssd_dt_bias": dt0 + jnp.log(-jnp.expm1(-dt0)),
        "ssd_a_log": jnp.log(jax.random.uniform(ks[12], (ns, 2, SSD_HEADS), f32, minval=1.0, maxval=16.0)),
        "ssd_d": 1.0 + nrm(ks[13], (ns, SSD_HEADS), 0.1),
        "ssd_gate_norm": 1.0 + nrm(ks[14], (ns, D_INNER), 0.02),
        "ssd_w_out": nrm(ks[15], (ns, D_INNER, D_MODEL), D_INNER ** -0.5),
        "ffn_norm": 1.0 + nrm(ks[16], (DEPTH, D_MODEL), 0.02),
        "ffn_w_up": nrm(ks[17], (DEPTH, D_MODEL, 2 * D_FF), D_MODEL ** -0.5),
        "ffn_conv_w": nrm(ks[18], (DEPTH, FFN_CONV, 2 * D_FF), FFN_CONV ** -0.5),
        "ffn_conv_b": nrm(ks[19], (DEPTH, 2 * D_FF), 0.02),
        "ffn_w_down": nrm(ks[20], (DEPTH, D_FF, D_MODEL), D_FF ** -0.5),
    }


def reference(x, attn_norm, attn_w_qkv, attn_q_norm, attn_k_norm, attn_sink, attn_w_o,
              ssd_norm, ssd_w_in, ssd_conv_w, ssd_conv_b, ssd_dt_bias, ssd_a_log, ssd_d,
              ssd_gate_norm, ssd_w_out,
              ffn_norm, ffn_w_up, ffn_conv_w, ffn_conv_b, ffn_w_down):
    s_len = x.shape[1]
    pos = jnp.arange(s_len, dtype=jnp.float32)
    inv_freq = ROPE_THETA ** (-(jnp.arange(0, ROT_DIM, 2, dtype=jnp.float32) / ROT_DIM))
    ang = pos[:, None] * inv_freq[None, :]
    cos, sin = jnp.cos(ang), jnp.sin(ang)
    for i in range(DEPTH):
        j = i // N_MIXERS
        if i % N_MIXERS == 0:
            x = x + window_attention(x, attn_norm[j], attn_w_qkv[j], attn_q_norm[j],
                                     attn_k_norm[j], attn_sink[j], attn_w_o[j], cos, sin)
        else:
            x = x + ssd_mixer(x, ssd_norm[j], ssd_w_in[j], ssd_conv_w[j], ssd_conv_b[j],
                              ssd_dt_bias[j], ssd_a_log[j], ssd_d[j], ssd_gate_norm[j], ssd_w_out[j])
        x = x + conv_ffn(x, ffn_norm[i], ffn_w_up[i], ffn_conv_w[i], ffn_conv_b[i], ffn_w_down[i])
    return x
```

```python
import numpy as np
from contextlib import ExitStack
import concourse.bass as bass
import concourse.mybir as mybir
from concourse.bass_utils import run_bass_kernel_spmd

F32 = mybir.dt.float32
BF16 = mybir.dt.bfloat16
AF = mybir.ActivationFunctionType
ALU = mybir.AluOpType

D = 1024
KC = D // 128
DFF = 2816
NFC = 2 * DFF // 128
NGC = DFF // 128
EPS = 1e-6
PAD = 128


class Sem:
    def __init__(self, handle, name):
        self.h = handle
        self.name = name
        self.count = 0


class Res:
    def __init__(self, name, excl=False):
        self.name = name
        self.excl = excl
        self.last_w = None
        self.readers = {}
        self.dsem = None


class Ctx:
    ENG = ("pe", "act", "dve", "pool", "sp")

    def __init__(self, nc, es):
        self.nc = nc
        self.es = es
        self.top_es = es
        self.fill0 = nc.gpsimd.to_reg(0.0)
        self.fillneg = nc.gpsimd.to_reg(-30000.0)
        self.eng = {"pe": nc.tensor, "act": nc.scalar, "dve": nc.vector, "pool": nc.gpsimd, "sp": nc.sync}
        self.sems = {}
        self.waited = {e: {} for e in self.ENG}
        self.gen = 0
        self._new_sems()
        self.uid = 0

    def _new_sems(self):
        self.gen += 1
        for e in self.ENG:
            self.sems[e] = Sem(self.es.enter_context(self.nc.semaphore(f"s_{e}_{self.gen}")), e)
        if not hasattr(self, "dfree"):
            self.dfree, self.dused, self.nd, self.dfresh = [], [], 0, []

    def name(self, base):
        self.uid += 1
        return f"{base}_{self.uid}"

    def sbuf(self, name, shape, dtype):
        t = self.es.enter_context(self.nc.sbuf_tensor(self.name(name), list(shape), dtype))
        return t

    def psum(self, name, shape, dtype):
        t = self.es.enter_context(self.nc.psum_tensor(self.name(name), list(shape), dtype))
        return t

    def _deps(self, eng, reads, writes):
        deps = []
        for r in reads:
            if r.last_w is not None:
                s, v, e = r.last_w
                if not (e == eng and eng == "pe"):
                    deps.append((s, v))
            if r.excl:
                for e, (s, v) in r.readers.items():
                    if e != eng:
                        deps.append((s, v))
        for w in writes:
            if w.last_w is not None:
                s, v, e = w.last_w
                if e != eng or eng != "pe":
                    deps.append((s, v))
            for e, (s, v) in w.readers.items():
                if e != eng or eng != "pe":
                    deps.append((s, v))
        return deps

    def _wait(self, eng, deps):
        wd = self.waited[eng]
        best = {}
        for s, v in deps:
            if wd.get(s, 0) >= v:
                continue
            if best.get(s, 0) < v:
                best[s] = v
        for s, v in best.items():
            assert v <= s.count, f"wait on un-emitted signal {s.name} {v}>{s.count} (engine {eng})"
            self.eng[eng].wait_ge(s.h, v)
            wd[s] = v

    def _stamp(self, eng, stamp, reads, writes):
        s, v = stamp
        for r in reads:
            r.readers[eng] = (s, v)
        for w in writes:
            w.last_w = (s, v, eng)
            w.readers = {}

    def op(self, eng, fn, reads=(), writes=(), signal=True):
        reads = [r for r in reads if r is not None]
        writes = [w for w in writes if w is not None]
        self._wait(eng, self._deps(eng, reads, writes))
        ins = fn(self.eng[eng])
        s = self.sems[eng]
        if signal:
            s.count += 1
            ins.then_inc(s.h, 1)
            stamp = (s, s.count)
        else:
            stamp = (s, s.count + 1)
        self._stamp(eng, stamp, reads, writes)
        return ins

    def _dsem(self, res, fresh=False):
        if res.dsem is None and fresh:
            self.nd += 1
            res.dsem = Sem(self.top_es.enter_context(self.nc.semaphore(f"s_dma_{self.nd}")), f"dma{self.nd}")
            self.dfresh.append(res.dsem)
        if res.dsem is None:
            while self.dfree and self.dfree[-1].count > 30000:
                self.dfree.pop()
            if self.dfree:
                res.dsem = self.dfree.pop()
            else:
                self.nd += 1
                res.dsem = Sem(self.top_es.enter_context(self.nc.semaphore(f"s_dma_{self.nd}")), f"dma{self.nd}")
            self.dused.append(res.dsem)
        return res.dsem

    def dma(self, queue, out, in_, reads=(), writes=(), slot=None):
        reads = [r for r in reads if r is not None]
        writes = [w for w in writes if w is not None]
        self._wait(queue, self._deps("dma", reads, writes))
        ins = self.eng[queue].dma_start(out=out, in_=in_)
        s = self._dsem(slot, fresh=(queue == "pool"))
        s.count += 16
        ins.then_inc(s.h, 16)
        self._stamp("dma", (s, s.count), reads, writes)
        return ins

    def barrier(self, fresh=True):
        for e in self.ENG:
            deps = [(s, s.count) for s in list(self.sems.values()) + self.dused + self.dfresh if s.count > 0]
            self._wait(e, deps)
        self.dfree.extend(self.dused)
        self.dused = []
        self.dfresh = []
        if fresh:
            old = dict(self.sems)
            self._new_sems()
            self._old = old


class Common:
    def __init__(self, cx):
        nc = cx.nc
        self.ones_mean = cx.sbuf("ones_mean", [128, 128], F32)
        self.r_ones = Res("ones_mean")
        cx.op("pool", lambda e: e.memset(self.ones_mean[:], 1.0 / D), writes=[self.r_ones])
        self.eps = cx.sbuf("eps", [128, 1], F32)
        self.r_eps = Res("eps")
        cx.op("pool", lambda e: e.memset(self.eps[:], EPS), writes=[self.r_eps])
        self.one = cx.sbuf("one", [128, 1], F32)
        self.r_one = Res("one")
        cx.op("pool", lambda e: e.memset(self.one[:], 1.0), writes=[self.r_one])


def rmsnorm_fm(cx, cm, xw, r_xw, W, g_ap, r_g, hn, r_hn, ps, r_ps, sq, r_sq, rstd, r_rstd):
    for k in range(KC):
        b = k % 2
        cx.op("act", lambda e, k=k, b=b: e.activation(out=sq[b][:, :W], in_=xw[:, k, :W], func=AF.Square),
              reads=[r_xw], writes=[r_sq[b]])
        cx.op("pe", lambda e, k=k, b=b: e.matmul(ps[:, :W], cm.ones_mean[:], sq[b][:, :W],
                                                    start=(k == 0), stop=(k == KC - 1)),
              reads=[cm.r_ones, r_sq[b]], writes=[r_ps], signal=True)
    cx.op("act", lambda e: e.activation(out=rstd[:, :W], in_=ps[:, :W], func=AF.Ln, bias=cm.eps[:, 0:1]),
          reads=[r_ps, cm.r_eps], writes=[r_rstd])
    cx.op("act", lambda e: e.activation(out=rstd[:, :W], in_=rstd[:, :W], func=AF.Exp, scale=-0.5),
          reads=[r_rstd], writes=[r_rstd])
    for k in range(KC):
        cx.op("dve", lambda e, k=k: e.scalar_tensor_tensor(out=hn[:, k, :W], in0=xw[:, k, :W], scalar=g_ap(k),
                                                        in1=rstd[:, :W], op0=ALU.mult, op1=ALU.mult),
              reads=[r_xw, r_rstd, r_g], writes=[r_hn[k]])


def load_weight_bf16(cx, dst, r_dst, src_ap, nk, split=1):
    for k in range(nk):
        cx.dma("pool", dst[:, k, :], src_ap[:, k, :], writes=[r_dst], slot=r_dst)


def ffn_layer(cx, cm, T, x_in, x_out, w):
    nc = cx.nc
    es = ExitStack()
    old_es = cx.es
    cx.es = es
    wup = cx.sbuf("wup", [128, KC, 2 * DFF], BF16); r_wup = Res("wup")
    wdn = cx.sbuf("wdn", [128, NGC, D], BF16); r_wdn = Res("wdn")
    cwb = cx.sbuf("cwb", [128, NFC, 4], F32); r_cwb = Res("cwb")
    gn = cx.sbuf("gn", [128, KC], F32); r_gn = Res("gn")
    xws = [cx.sbuf(f"xw{i}", [128, KC, 512], F32) for i in range(2)]; r_xws = [Res("xw0"), Res("xw1")]
    hn = cx.sbuf("hn", [128, KC, 512], BF16); r_hn = [Res(f"hn{k}") for k in range(KC)]
    gT = cx.sbuf("gT", [128, NGC, 512], BF16); r_gT = [Res(f"gT{c}") for c in range(NGC)]
    rstd = cx.sbuf("rstd", [128, 512], F32); r_rstd = Res("rstd")
    NT = 2
    t1 = [[cx.sbuf(f"t1_{i}_{j}", [128, 512], F32) for j in range(2)] for i in range(NT)]
    r_t1 = [[Res(f"t1_{i}_{j}") for j in range(2)] for i in range(NT)]
    sq = [t1[0][0], t1[0][1]]; r_sq = [r_t1[0][0], r_t1[0][1]]
    ps_st = cx.psum("ps_st", [128, 512], F32); r_ps_st = Res("ps_st", excl=True)
    ps_gv = [[cx.psum(f"ps_gv{i}{j}", [128, 512], F32) for j in range(2)] for i in range(2)]
    r_ps_gv = [[Res(f"ps_gv{i}{j}", excl=True) for j in range(2)] for i in range(2)]
    ps_o = [cx.psum(f"ps_o{i}", [128, 512], F32) for i in range(2)]
    r_ps_o = [Res(f"ps_o{i}", excl=True) for i in range(2)]

    cx.dma("sp", cwb[:], w["cwb"], writes=[r_cwb], slot=r_cwb)
    cx.dma("sp", gn[:], w["gn"], writes=[r_gn], slot=r_gn)
    load_weight_bf16(cx, wup, r_wup, w["wup"], KC)
    load_weight_bf16(cx, wdn, r_wdn, w["wdn"], NGC)

    x_in_v = x_in.rearrange("(k p) t -> p k t", p=128)
    x_out_v = x_out.rearrange("(k p) t -> p k t", p=128)
    STEP = 510
    tiles = [(t0, min(STEP, T - t0)) for t0 in range(0, T, STEP)]

    def load(i):
        t0, n_out = tiles[i]
        W = n_out + 2
        cx.dma("sp", xws[i % 2][:, :, :W], x_in_v[:, :, PAD + t0 - 1:PAD + t0 - 1 + W], reads=[w["r_xin"]], writes=[r_xws[i % 2]], slot=r_xws[i % 2])

    def norm(i):
        t0, n_out = tiles[i]
        rmsnorm_fm(cx, cm, xws[i % 2], r_xws[i % 2], n_out + 2, lambda k: gn[:, k:k + 1], r_gn, hn, r_hn, ps_st, r_ps_st, sq, r_sq, rstd, r_rstd)

    load(0)
    if len(tiles) > 1:
        load(1)
    norm(0)
    pair_i = 0
    for ti, (t0, n_out) in enumerate(tiles):
        W = n_out + 2
        xw = xws[ti % 2]; r_xw = r_xws[ti % 2]
        for cg in range(NGC):
            pb = pair_i % 2
            tb = pair_i % NT
            pair_i += 1
            for j, c in enumerate((cg, cg + NGC)):
                ps = ps_gv[pb][j]; r_ps = r_ps_gv[pb][j]
                for k in range(KC):
                    cx.op("pe", lambda e, k=k, c=c, ps=ps: e.matmul(ps[:, :W], wup[:, k, c * 128:(c + 1) * 128], hn[:, k, :W],
                                                                   start=(k == 0), stop=(k == KC - 1)),
                          reads=[r_wup, r_hn[k]], writes=[r_ps], signal=(k == KC - 1))
            for j, c in enumerate((cg, cg + NGC)):
                ps = ps_gv[pb][j]; r_ps = r_ps_gv[pb][j]
                tt = t1[tb][j]; r_tt = r_t1[tb][j]
                cx.op("act", lambda e, c=c, ps=ps, tt=tt: e.activation(out=tt[:, :n_out], in_=ps[:, 1:1 + n_out], func=AF.Identity,
                                                                        bias=cwb[:, c, 3:4], scale=cwb[:, c, 1:2]),
                      reads=[r_ps, r_cwb], writes=[r_tt])
                cx.op("dve", lambda e, c=c, ps=ps, tt=tt: e.scalar_tensor_tensor(out=tt[:, :n_out], in0=ps[:, 0:n_out], scalar=cwb[:, c, 0:1],
                                                                               in1=tt[:, :n_out], op0=ALU.mult, op1=ALU.add),
                      reads=[r_ps, r_cwb, r_tt], writes=[r_tt])
                cx.op("dve", lambda e, c=c, ps=ps, tt=tt: e.scalar_tensor_tensor(out=tt[:, :n_out], in0=ps[:, 2:2 + n_out], scalar=cwb[:, c, 2:3],
                                                                               in1=tt[:, :n_out], op0=ALU.mult, op1=ALU.add),
                      reads=[r_ps, r_cwb, r_tt], writes=[r_tt])
            tg = t1[tb][0]; tv = t1[tb][1]
            cx.op("act", lambda e, tg=tg: e.activation(out=tg[:, :n_out], in_=tg[:, :n_out], func=AF.Silu),
                  reads=[r_t1[tb][0]], writes=[r_t1[tb][0]])
            cx.op("pool", lambda e, tg=tg, tv=tv, cg=cg: e.tensor_tensor(out=gT[:, cg, :n_out], in0=tg[:, :n_out], in1=tv[:, :n_out], op=ALU.mult),
                  reads=[r_t1[tb][0], r_t1[tb][1]], writes=[r_gT[cg]])
        if ti + 1 < len(tiles):
            norm(ti + 1)
        for m in range(KC):
            ob = m % 2
            for c in range(NGC):
                cx.op("pe", lambda e, c=c, m=m, ob=ob: e.matmul(ps_o[ob][:, :n_out], wdn[:, c, m * 128:(m + 1) * 128], gT[:, c, :n_out],
                                                                 start=(c == 0), stop=(c == NGC - 1)),
                      reads=[r_wdn, r_gT[c]], writes=[r_ps_o[ob]], signal=(c == NGC - 1))
            cx.op("dve", lambda e, m=m, ob=ob: e.tensor_tensor(out=xw[:, m, 1:1 + n_out], in0=ps_o[ob][:, :n_out], in1=xw[:, m, 1:1 + n_out], op=ALU.add),
                  reads=[r_ps_o[ob], r_xw], writes=[r_xw])
        cx.dma("sp", x_out_v[:, :, PAD + t0:PAD + t0 + n_out], xw[:, :, 1:1 + n_out], reads=[r_xw], writes=[w["r_xout"]], slot=r_xw)
        if ti + 2 < len(tiles):
            load(ti + 2)
    cx.barrier()
    cx.es = old_es
    es.close()


NH, NKV, HD = 16, 4, 64
QD = NH * HD
WQKV = QD + 2 * NKV * HD + NKV * HD


def attn_layer(cx, cm, T, x_in, x_out, w):
    es = ExitStack(); old_es = cx.es; cx.es = es
    NB = T // 128
    wq = cx.sbuf("wqkv", [128, KC, WQKV], BF16); r_wq = Res("wqkv")
    wo = cx.sbuf("wo", [128, KC, D], BF16); r_wo = Res("wo")
    gn = cx.sbuf("gn", [128, KC], F32); r_gn = Res("gn")
    gqk = cx.sbuf("gqk", [128, 2], F32); r_gqk = Res("gqk")
    esk = cx.sbuf("esk", [128, NH], F32); r_esk = Res("esk")
    bm = cx.sbuf("bm", [128, 128], F32); r_bm = Res("bm")
    rmf = cx.sbuf("rmf", [128, 128], F32); r_rmf = Res("rmf")
    rm = cx.sbuf("rm", [128, 128], BF16); r_rm = Res("rm")
    onesb = cx.sbuf("onesb", [128, 128], BF16); r_onesb = Res("onesb")
    kT = cx.sbuf("kT", [128, NKV, T], BF16); r_kT = [Res(f"kT{i}") for i in range(T // 512)]
    V = cx.sbuf("V", [128, NB, NKV * HD], BF16); r_V = [Res(f"V{i}") for i in range(NB)]
    vv = cx.sbuf("vv", [128, 4, NKV, 2, HD], BF16); r_vv = [Res(f"vv{i}") for i in range(4)]
    xw = cx.sbuf("xw", [128, KC, 512], F32); r_xw = Res("xw")
    hn = cx.sbuf("hn", [128, KC, 512], BF16); r_hn = [Res(f"hn{k}") for k in range(KC)]
    qT = cx.sbuf("qT", [128, KC, 512], BF16); r_qT = [Res(f"qT{k}") for k in range(KC)]
    oT = hn; r_oT = r_hn
    cs = cx.sbuf("cs", [128, 2, 512], F32); r_cs = Res("cs")
    sq = [cx.sbuf(f"sq{i}", [128, 512], F32) for i in range(2)]; r_sq = [Res("sq0"), Res("sq1")]
    rstd = cx.sbuf("rstd", [128, 512], F32); r_rstd = Res("rstd")
    qnb = cx.sbuf("qnb", [128, 512], BF16); r_qnb = Res("qnb")
    ta = cx.sbuf("ta", [128, 512], F32); r_ta = Res("ta")
    tb = cx.sbuf("tb", [128, 512], F32); r_tb = Res("tb")
    PT = [cx.sbuf(f"PT{i}", [128, 512], BF16) for i in range(4)]; r_PT = [Res(f"PT{i}") for i in range(4)]
    rd = cx.sbuf("rd", [128, 512], F32); r_rd = Res("rd")
    ps_p1 = cx.psum("ps_p0", [128, 512], F32); r_ps_p1 = Res("ps_p0", excl=True)
    ps_p = [ps_p1, ps_p1]; r_ps_p = [r_ps_p1, r_ps_p1]
    ps_a = cx.psum("ps_a", [128, 512], F32); r_ps_a = Res("ps_a", excl=True)
    ps_b = ps_a; r_ps_b = r_ps_a
    ps_s = [[cx.psum(f"ps_s{i}{hf}", [128, 512], F32) for hf in range(2)] for i in range(2)]
    r_ps_s = [[Res(f"ps_s{i}{hf}", excl=True) for hf in range(2)] for i in range(2)]
    ps_o = cx.psum("ps_o", [128, 512], F32); r_ps_o = Res("ps_o", excl=True)
    ps_d = cx.psum("ps_d", [128, 512], F32); r_ps_d = Res("ps_d", excl=True)

    for dst, r, key in ((gn, r_gn, "gn"), (gqk, r_gqk, "gqk"), (esk, r_esk, "sink"), (bm, r_bm, "bm"), (rmf, r_rmf, "rm")):
        cx.dma("sp", dst[:], w[key], writes=[r], slot=r)
    load_weight_bf16(cx, wq, r_wq, w["wqkv"], KC)
    load_weight_bf16(cx, wo, r_wo, w["wo"], KC)
    cx.op("act", lambda e: e.activation(out=esk[:], in_=esk[:], func=AF.Exp), reads=[r_esk], writes=[r_esk])
    cx.op("dve", lambda e: e.tensor_copy(rm[:], rmf[:]), reads=[r_rmf], writes=[r_rm])
    cx.op("pool", lambda e: e.memset(onesb[:], 1.0), writes=[r_onesb])

    x_in_v = x_in.rearrange("(k p) t -> p k t", p=128)
    x_out_v = x_out.rearrange("(k p) t -> p k t", p=128)
    pp = [0]

    def load_norm(t0):
        cx.dma("sp", xw[:, :, :], x_in_v[:, :, PAD + t0:PAD + t0 + 512], reads=[w["r_xin"]], writes=[r_xw], slot=r_xw)
        cx.dma("sp", cs[:, 0, :], w["cos"][:, t0:t0 + 512], writes=[r_cs], slot=r_cs)
        cx.dma("sp", cs[:, 1, :], w["sin"][:, t0:t0 + 512], writes=[r_cs], slot=r_cs)
        rmsnorm_fm(cx, cm, xw, r_xw, 512, lambda k: gn[:, k:k + 1], r_gn, hn, r_hn, ps_a, r_ps_a, sq, r_sq, rstd, r_rstd)

    def proj_fm(col0):
        b = pp[0] % 2; pp[0] += 1
        for k in range(KC):
            cx.op("pe", lambda e, k=k: e.matmul(ps_p[b][:, :], wq[:, k, col0:col0 + 128], hn[:, k, :],
                                                start=(k == 0), stop=(k == KC - 1)),
                  reads=[r_wq, r_hn[k]], writes=[r_ps_p[b]], signal=(k == KC - 1))
        return ps_p[b], r_ps_p[b]

    def headnorm_rope(ps, r_ps, gcol, out_ap, r_out):
        cx.op("act", lambda e: e.activation(out=sq[0][:, :], in_=ps[:, :], func=AF.Square), reads=[r_ps], writes=[r_sq[0]])
        cx.op("pe", lambda e: e.matmul(ps_a[:, :], bm[:], sq[0][:, :], start=True, stop=True),
              reads=[r_bm, r_sq[0]], writes=[r_ps_a])
        cx.op("act", lambda e: e.activation(out=rstd[:, :], in_=ps_a[:, :], func=AF.Ln, bias=cm.eps[:, 0:1]),
              reads=[r_ps_a, cm.r_eps], writes=[r_rstd])
        cx.op("act", lambda e: e.activation(out=rstd[:, :], in_=rstd[:, :], func=AF.Exp, scale=-0.5),
              reads=[r_rstd], writes=[r_rstd])
        cx.op("dve", lambda e: e.scalar_tensor_tensor(out=qnb[:, :], in0=ps[:, :], scalar=gqk[:, gcol:gcol + 1], in1=rstd[:, :],
                                                      op0=ALU.mult, op1=ALU.mult),
              reads=[r_ps, r_gqk, r_rstd], writes=[r_qnb])
        cx.op("pe", lambda e: e.matmul(ps_b[:, :], rm[:], qnb[:, :], start=True, stop=True),
              reads=[r_rm, r_qnb], writes=[r_ps_b])
        cx.op("dve", lambda e: e.tensor_tensor(out=ta[:, :], in0=qnb[:, :], in1=cs[:, 0, :], op=ALU.mult),
              reads=[r_qnb, r_cs], writes=[r_ta])
        cx.op("dve", lambda e: e.tensor_tensor(out=tb[:, :], in0=ps_b[:, :], in1=cs[:, 1, :], op=ALU.mult),
              reads=[r_ps_b, r_cs], writes=[r_tb])
        cx.op("pool", lambda e: e.tensor_tensor(out=out_ap, in0=ta[:, :], in1=tb[:, :], op=ALU.add),
              reads=[r_ta, r_tb], writes=[r_out])

    for ti in range(T // 512):
        t0 = ti * 512
        load_norm(t0)
        for j in range(NKV):
            ps, r_ps = proj_fm(QD + j * 128)
            headnorm_rope(ps, r_ps, 1, kT[:, j, t0:t0 + 512], r_kT[ti])
        for bl in range(4):
            b = pp[0] % 2; pp[0] += 1
            for k in range(KC):
                cx.op("pe", lambda e, k=k: e.matmul(ps_p[b][:, :256], hn[:, k, bl * 128:(bl + 1) * 128], wq[:, k, QD + 512:QD + 768],
                                                    start=(k == 0), stop=(k == KC - 1)),
                      reads=[r_wq, r_hn[k]], writes=[r_ps_p[b]], signal=(k == KC - 1))
            cx.op("act", lambda e: e.copy(out=V[:, ti * 4 + bl, :], in_=ps_p[b][:, :256]), reads=[r_ps_p[b]], writes=[r_V[ti * 4 + bl]])

    vv_have = {}
    sc = [0]
    for ti in range(T // 512):
        t0 = ti * 512
        load_norm(t0)
        for c in range(KC):
            ps, r_ps = proj_fm(c * 128)
            headnorm_rope(ps, r_ps, 0, qT[:, c, :], r_qT[c])
        for bl in range(4):
            nb = ti * 4 + bl
            qs = slice(bl * 128, (bl + 1) * 128)
            kbs = [kb for kb in (nb - 1, nb, nb + 1) if 0 <= kb < NB]
            for kb in kbs:
                if vv_have.get(kb % 4) != kb:
                    for d2 in range(2):
                        cx.op("pool", lambda e, d2=d2: e.tensor_copy(vv[:, kb % 4, :, d2, :], V[:, kb, :].rearrange("p (j d) -> p j d", j=NKV)),
                              reads=[r_V[kb]], writes=[r_vv[kb % 4]])
                    vv_have[kb % 4] = kb
            for j in range(NKV):
                pts = []
                for kb in kbs:
                    sb = sc[0] % 2; pb = sc[0] % 4; sc[0] += 1
                    for half in range(2):
                        rows = slice(half * 64, half * 64 + 64)
                        cx.op("pe", lambda e: e.matmul(ps_s[sb][half][:, :256], kT[rows, j, kb * 128:(kb + 1) * 128],
                                                       qT[rows, 2 * j:2 * j + 2, qs], start=True, stop=True),
                              reads=[r_kT[kb // 4], r_qT[2 * j], r_qT[2 * j + 1]], writes=[r_ps_s[sb][half]])
                    for half in range(2):
                        cx.op("act", lambda e: e.activation(out=PT[pb][:, half * 256:(half + 1) * 256], in_=ps_s[sb][half][:, :256], func=AF.Exp, scale=HD ** -0.5),
                              reads=[r_ps_s[sb][half]], writes=[r_PT[pb]])
                    if kb != nb:
                        sgn = 1 if kb < nb else -1
                        cx.op("pool", lambda e: e.affine_select(out=PT[pb][:, :].rearrange("p (a q) -> p a q", a=4),
                                                                in_=PT[pb][:, :].rearrange("p (a q) -> p a q", a=4),
                                                                pattern=[[0, 4], [-sgn, 128]], compare_op=ALU.is_ge, fill=cx.fill0,
                                                                base=0, channel_multiplier=sgn),
                              reads=[r_PT[pb]], writes=[r_PT[pb]])
                    pts.append((kb, pb))
                for i, (kb, pb) in enumerate(pts):
                    cx.op("pe", lambda e: e.matmul(ps_o[:, :], vv[:, kb % 4, j, :, :], PT[pb][:, :], start=(i == 0), stop=(i == len(pts) - 1)),
                          reads=[r_vv[kb % 4], r_PT[pb]], writes=[r_ps_o], signal=(i == len(pts) - 1))
                for i, (kb, pb) in enumerate(pts):
                    cx.op("pe", lambda e: e.matmul(ps_d[:, :], onesb[:], PT[pb][:, :], start=(i == 0), stop=(i == len(pts) - 1)),
                          reads=[r_onesb, r_PT[pb]], writes=[r_ps_d], signal=(i == len(pts) - 1))
                heads = (4 * j, 4 * j + 2, 4 * j + 1, 4 * j + 3)
                for a, h in enumerate(heads):
                    cx.op("dve", lambda e: e.tensor_scalar(out=rd[:, a * 128:(a + 1) * 128], in0=ps_d[:, a * 128:(a + 1) * 128],
                                                           scalar1=esk[:, h:h + 1], scalar2=None, op0=ALU.add),
                          reads=[r_ps_d, r_esk], writes=[r_rd])
                cx.op("act", lambda e: e.activation(out=rd[:, :], in_=rd[:, :], func=AF.Ln), reads=[r_rd], writes=[r_rd])
                cx.op("act", lambda e: e.activation(out=rd[:, :], in_=rd[:, :], func=AF.Exp, scale=-1.0), reads=[r_rd], writes=[r_rd])
                for half in range(2):
                    rows = slice(half * 64, half * 64 + 64)
                    cx.op("dve", lambda e: e.tensor_tensor(out=oT[rows, 2 * j:2 * j + 2, qs],
                                                           in0=ps_o[rows, half * 256:(half + 1) * 256].rearrange("p (a q) -> p a q", a=2),
                                                           in1=rd[rows, half * 256:(half + 1) * 256].rearrange("p (a q) -> p a q", a=2), op=ALU.mult),
                          reads=[r_ps_o, r_rd], writes=[r_oT[2 * j], r_oT[2 * j + 1]])
        for m in range(KC):
            b = pp[0] % 2; pp[0] += 1
            for c in range(KC):
                cx.op("pe", lambda e, c=c: e.matmul(ps_p[b][:, :], wo[:, c, m * 128:(m + 1) * 128], oT[:, c, :],
                                                    start=(c == 0), stop=(c == KC - 1)),
                      reads=[r_wo, r_oT[c]], writes=[r_ps_p[b]], signal=(c == KC - 1))
            cx.op("dve", lambda e: e.tensor_tensor(out=xw[:, m, :], in0=ps_p[b][:, :], in1=xw[:, m, :], op=ALU.add),
                  reads=[r_ps_p[b], r_xw], writes=[r_xw])
        cx.dma("sp", x_out_v[:, :, PAD + t0:PAD + t0 + 512], xw[:, :, :], reads=[r_xw], writes=[w["r_xout"]], slot=r_xw)
    cx.barrier()
    cx.es = old_es
    es.close()


DI, NHS, NG, DS = 2048, 32, 8, 128
CONVD = DI + 2 * NG * DS
SIN = DI + CONVD + 2 * NHS


def ssd_layer(cx, cm, T, x_in, x_out, w, scr):
    NCH = T // 128
    x_in_v = x_in.rearrange("(k p) t -> p k t", p=128)
    x_out_v = x_out.rearrange("(k p) t -> p k t", p=128)
    es = ExitStack(); old_es = cx.es; cx.es = es
    win = cx.sbuf("win", [128, KC, SIN], BF16); r_win = Res("win")
    cwb = cx.sbuf("cwb", [128, 32, 6], F32); r_cwb = Res("cwb")
    gn = cx.sbuf("gn", [128, KC], F32); r_gn = Res("gn")
    dtp = cx.sbuf("dtp", [64, 2], F32); r_dtp = Res("dtp")
    idb = cx.sbuf("idb", [128, 128], BF16); r_idb = Res("idb")
    idf = cx.sbuf("idf", [128, 128], F32); r_idf = Res("idf")
    xw = cx.sbuf("xw", [128, KC, 512], F32); r_xw = Res("xw")
    hn = cx.sbuf("hn", [128, KC, 512], BF16); r_hn = [Res(f"hn{k}") for k in range(KC)]
    sq = [cx.sbuf(f"sq{i}", [128, 512], F32) for i in range(2)]; r_sq = [Res("sq0"), Res("sq1")]
    rstd = cx.sbuf("rstd", [128, 512], F32); r_rstd = Res("rstd")
    tcv = [cx.sbuf(f"tcv{i}", [128, 384], F32) for i in range(2)]; r_tcv = [Res(f"tcv{i}") for i in range(2)]
    xbc = [cx.sbuf(f"xbc{i}", [128, 384], BF16) for i in range(2)]; r_xbc = [Res(f"xbc{i}") for i in range(2)]
    tok = cx.sbuf("tok", [128, 3, DI + NG * DS], BF16); r_tok = Res("tok")
    szt = cx.sbuf("szt", [128, DI], BF16); r_szt = Res("szt")
    aT = cx.sbuf("aT", [64, 384], F32); r_aT = Res("aT")
    dT = cx.sbuf("dT", [64, 384], F32); r_dT = Res("dT")
    acT = cx.sbuf("acT", [64, 384], F32); r_acT = Res("acT")
    onesf = cx.sbuf("onesf", [64, 384], F32); r_onesf = Res("onesf")
    tk2 = cx.sbuf("tk2", [128, 2, 64], F32); r_tk2 = Res("tk2")
    tot1 = cx.sbuf("tot1", [64, 1], F32); r_tot1 = Res("tot1")
    ps_st = cx.psum("ps_st", [128, 512], F32); r_ps_st = Res("ps_st", excl=True)
    ps_c = [cx.psum(f"ps_c{i}", [128, 512], F32) for i in range(2)]; r_ps_c = [Res(f"ps_c{i}", excl=True) for i in range(2)]
    ps_t = [cx.psum(f"ps_t{i}", [128, 1024], BF16) for i in range(2)]; r_ps_t = [Res(f"ps_t{i}", excl=True) for i in range(2)]
    ps_z = [cx.psum(f"ps_z{i}", [128, 512], F32) for i in range(2)]; r_ps_z = [Res(f"ps_z{i}", excl=True) for i in range(2)]
    ps_f = cx.psum("ps_f", [128, 512], F32); r_ps_f = Res("ps_f", excl=True)

    for dst, r, key in ((cwb, r_cwb, "cwb"), (gn, r_gn, "gn"), (dtp, r_dtp, "dtp"), (idf, r_idf, "ident")):
        cx.dma("sp", dst[:], w[key], writes=[r], slot=r)
    load_weight_bf16(cx, win, r_win, w["win"], KC)
    cx.op("dve", lambda e: e.tensor_copy(idb[:], idf[:]), reads=[r_idf], writes=[r_idb])
    cx.op("pool", lambda e: e.memset(onesf[:], 1.0), writes=[r_onesf])
    cx.op("act", lambda e: e.activation(out=dtp[:, 1:2], in_=dtp[:, 1:2], func=AF.Exp), reads=[r_dtp], writes=[r_dtp])
    cx.op("dve", lambda e: e.tensor_scalar(out=dtp[:, 1:2], in0=dtp[:, 1:2], scalar1=-1.0, scalar2=None, op0=ALU.mult),
          reads=[r_dtp], writes=[r_dtp])
    cc = [0]
    for t0 in range(0, T, 384):
        n_out = min(384, T - t0); W = n_out + 4; nblk = n_out // 128
        cx.dma("sp", xw[:, :, :W], x_in_v[:, :, PAD + t0 - 2:PAD + t0 - 2 + W], reads=[w["r_xin"]], writes=[r_xw], slot=r_xw)
        rmsnorm_fm(cx, cm, xw, r_xw, W, lambda k: gn[:, k:k + 1], r_gn, hn, r_hn, ps_st, r_ps_st, sq, r_sq, rstd, r_rstd)
        for c in range(32):
            b = cc[0] % 2; cc[0] += 1
            col0 = DI + c * 128
            for k in range(KC):
                cx.op("pe", lambda e, k=k: e.matmul(ps_c[b][:, :W], win[:, k, col0:col0 + 128], hn[:, k, :W], start=(k == 0), stop=(k == KC - 1)),
                      reads=[r_win, r_hn[k]], writes=[r_ps_c[b]], signal=(k == KC - 1))
            cx.op("act", lambda e: e.activation(out=tcv[b][:, :n_out], in_=ps_c[b][:, 2:2 + n_out], func=AF.Identity,
                                                bias=cwb[:, c, 5:6], scale=cwb[:, c, 2:3]),
                  reads=[r_ps_c[b], r_cwb], writes=[r_tcv[b]])
            for kk in (0, 1, 3, 4):
                cx.op("dve", lambda e, kk=kk: e.scalar_tensor_tensor(out=tcv[b][:, :n_out], in0=ps_c[b][:, kk:kk + n_out], scalar=cwb[:, c, kk:kk + 1],
                                                                     in1=tcv[b][:, :n_out], op0=ALU.mult, op1=ALU.add),
                      reads=[r_ps_c[b], r_cwb, r_tcv[b]], writes=[r_tcv[b]])
            cx.op("act", lambda e: e.activation(out=xbc[b][:, :n_out], in_=tcv[b][:, :n_out], func=AF.Silu), reads=[r_tcv[b]], writes=[r_xbc[b]])
            if c < 24:
                for bl in range(nblk):
                    cx.op("pe", lambda e, bl=bl: e.transpose(ps_t[b][:, bl * 128:(bl + 1) * 128], xbc[b][:, bl * 128:(bl + 1) * 128], idb[:]),
                          reads=[r_xbc[b], r_idb], writes=[r_ps_t[b]])
                cx.op("dve", lambda e: e.tensor_copy(tok[:, :nblk, c * 128:(c + 1) * 128], ps_t[b][:, :nblk * 128].rearrange("p (a q) -> p a q", a=nblk)),
                      reads=[r_ps_t[b]], writes=[r_tok])
            if c >= 16:
                dst = scr["BT"] if c < 24 else scr["CT"]
                g = (c - 16) % 8
                cx.dma("sp", dst[g * 128:(g + 1) * 128, t0:t0 + n_out], xbc[b][:, :n_out], reads=[r_xbc[b]], writes=[scr["r_BC"]], slot=r_xbc[b])
        for bl in range(nblk):
            r0 = t0 + bl * 128
            cx.dma("sp", scr["xtok"][r0:r0 + 128, :], tok[:, bl, :DI], reads=[r_tok], writes=[scr["r_tok"]], slot=r_tok)
            cx.dma("sp", scr["Btok"][r0:r0 + 128, :], tok[:, bl, DI:], reads=[r_tok], writes=[scr["r_tok"]], slot=r_tok)
        for bl in range(nblk):
            r0 = t0 + bl * 128
            for ct in range(4):
                b = cc[0] % 2; cc[0] += 1
                for k in range(KC):
                    cx.op("pe", lambda e, k=k: e.matmul(ps_z[b][:, :], hn[:, k, 2 + bl * 128:2 + (bl + 1) * 128], win[:, k, ct * 512:(ct + 1) * 512],
                                                        start=(k == 0), stop=(k == KC - 1)),
                          reads=[r_win, r_hn[k]], writes=[r_ps_z[b]], signal=(k == KC - 1))
                cx.op("act", lambda e: e.activation(out=szt[:, ct * 512:(ct + 1) * 512], in_=ps_z[b][:, :], func=AF.Silu),
                      reads=[r_ps_z[b]], writes=[r_szt])
            cx.dma("sp", scr["sz"][r0:r0 + 128, :], szt[:, :], reads=[r_szt], writes=[scr["r_sz"]], slot=r_szt)
        for k in range(KC):
            cx.op("pe", lambda e, k=k: e.matmul(ps_f[:64, :n_out], win[:, k, DI + CONVD:SIN], hn[:, k, 2:2 + n_out], start=(k == 0), stop=(k == KC - 1)),
                  reads=[r_win, r_hn[k]], writes=[r_ps_f], signal=(k == KC - 1))
        cx.op("act", lambda e: e.activation(out=dT[:, :n_out], in_=ps_f[:64, :n_out], func=AF.Exp, bias=dtp[:, 0:1]),
              reads=[r_ps_f, r_dtp], writes=[r_dT])
        cx.op("act", lambda e: e.activation(out=dT[:, :n_out], in_=dT[:, :n_out], func=AF.Ln, bias=cm.one[:64, 0:1]),
              reads=[r_dT, cm.r_one], writes=[r_dT])
        cx.op("dve", lambda e: e.tensor_scalar(out=aT[:, :n_out], in0=dT[:, :n_out], scalar1=dtp[:, 1:2], scalar2=None, op0=ALU.mult),
              reads=[r_dT, r_dtp], writes=[r_aT])
        for bl in range(nblk):
            cs_ = slice(bl * 128, (bl + 1) * 128)
            cx.op("dve", lambda e: e.tensor_tensor_scan(out=acT[:, cs_], data0=onesf[:, cs_], data1=aT[:, cs_], initial=0.0, op0=ALU.mult, op1=ALU.add),
                  reads=[r_onesf, r_aT], writes=[r_acT])
            cx.op("dve", lambda e: e.tensor_copy(tot1[32:64, :], acT[32:64, bl * 128 + 127:bl * 128 + 128]), reads=[r_acT], writes=[r_tot1])
            cx.op("dve", lambda e: e.scalar_tensor_tensor(out=acT[32:64, cs_], in0=acT[32:64, cs_], scalar=-1.0, in1=aT[32:64, cs_], op0=ALU.mult, op1=ALU.add),
                  reads=[r_acT, r_aT], writes=[r_acT])
            cx.op("dve", lambda e: e.tensor_scalar(out=acT[32:64, cs_], in0=acT[32:64, cs_], scalar1=tot1[32:64, 0:1], scalar2=None, op0=ALU.add),
                  reads=[r_acT, r_tot1], writes=[r_acT])
            r0 = t0 + bl * 128
            for i, src in enumerate((acT, dT)):
                cx.op("pe", lambda e, src=src: e.transpose(ps_f[:, 128 + i * 64:128 + (i + 1) * 64], src[:, cs_], idf[:64, :64]),
                      reads=[r_acT, r_dT, r_idf], writes=[r_ps_f])
            cx.op("act", lambda e: e.copy(out=tk2[:, :, :], in_=ps_f[:, 128:256].rearrange("p (a q) -> p a q", a=2)), reads=[r_ps_f], writes=[r_tk2])
            cx.dma("sp", scr["actok"][r0:r0 + 128, :], tk2[:, 0, :], reads=[r_tk2], writes=[scr["r_ac"]], slot=r_tk2)
            cx.dma("sp", scr["dttok"][r0:r0 + 128, :], tk2[:, 1, :], reads=[r_tk2], writes=[scr["r_ac"]], slot=r_tk2)
        cx.dma("sp", scr["acT"][:, t0:t0 + n_out], acT[:, :n_out], reads=[r_acT], writes=[scr["r_ac"]], slot=r_acT)
    cx.barrier()
    cx.es = old_es
    es.close()
    ssd_phase_b(cx, cm, T, x_in_v, x_out_v, w, scr)


def ssd_phase_b(cx, cm, T, x_in_v, x_out_v, w, scr):
    NCH = T // 128
    es = ExitStack(); old_es = cx.es; cx.es = es
    wout = cx.sbuf("wout", [128, 16, D], BF16); r_wout = Res("wout")
    dbc = cx.sbuf("dbc", [128, DI], F32); r_dbc = Res("dbc")
    gbc = cx.sbuf("gbc", [128, DI], F32); r_gbc = Res("gbc")
    idb = cx.sbuf("idb", [128, 128], BF16); r_idb = Res("idb")
    idf = cx.sbuf("idf", [128, 128], F32); r_idf = Res("idf")
    xt = cx.sbuf("xt", [128, DI], BF16); r_xt = Res("xt")
    bt = cx.sbuf("bt", [128, NG * DS], BF16); r_bt = Res("bt")
    BTs = cx.sbuf("BTs", [128, NG, 128], BF16); r_BTs = Res("BTs")
    CTs = cx.sbuf("CTs", [128, NG, 128], BF16); r_CTs = Res("CTs")
    dtt = cx.sbuf("dtt", [128, 64], F32); r_dtt = Res("dtt")
    act = cx.sbuf("act", [128, 64], F32); r_act = Res("act")
    totb = cx.sbuf("totb", [128, 32], F32); r_totb = Res("totb")
    rowb = cx.sbuf("rowb", [128, 32, 128], F32); r_rowb = Res("rowb")
    sm = cx.sbuf("sm", [128, 4, 32], F32); r_sm = Res("sm")
    cbt = cx.sbuf("cbt", [128, NG, 128], F32); r_cbt = [Res(f"cbt{g}") for g in range(NG)]
    dif = [cx.sbuf(f"dif{i}", [128, 4, 128], F32) for i in range(2)]; r_dif = [Res(f"dif{i}") for i in range(2)]
    MT = [cx.sbuf(f"MT{i}", [128, 4, 128], BF16) for i in range(2)]; r_MT = [Res(f"MT{i}") for i in range(2)]
    Bw = [cx.sbuf(f"Bw{i}", [128, 4, 128], BF16) for i in range(2)]; r_Bw = [Res(f"Bw{i}") for i in range(2)]
    HT = cx.sbuf("HT", [128, NHS, 64], F32); r_HT = [Res(f"HT{g}") for g in range(NG)]
    HTb = cx.sbuf("HTb", [128, NHS, 64], BF16); r_HTb = [Res(f"HTb{g}") for g in range(NG)]
    xdt = cx.sbuf("xdt", [128, DI], BF16); r_xdt = Res("xdt")
    yof = cx.sbuf("yof", [128, 512], F32); r_yof = Res("yof")
    ysb = cx.sbuf("ysb", [128, DI], F32); r_ysb = [Res(f"ysb{q}") for q in range(4)]
    y0 = cx.sbuf("y0", [128, DI], F32); r_y0 = Res("y0")
    szt = cx.sbuf("szt", [128, DI], BF16); r_szt = Res("szt")
    gss = cx.sbuf("gss", [128, NG], F32); r_gss = Res("gss")
    yb = cx.sbuf("yb", [128, DI], BF16); r_yb = Res("yb")
    yT = cx.sbuf("yT", [128, 16, 128], BF16); r_yT = Res("yT")
    xw = cx.sbuf("xw", [128, KC, 128], F32); r_xw = Res("xw")
    ps_cb = cx.psum("ps_cb", [128, 512], F32); r_ps_cb = Res("ps_cb", excl=True)
    ps_y = [cx.psum(f"ps_y{i}", [128, 512], F32) for i in range(2)]; r_ps_y = [Res(f"ps_y{i}", excl=True) for i in range(2)]
    ps_o = [cx.psum(f"ps_of{i}", [128, 512], F32) for i in range(2)]; r_ps_o = [Res(f"ps_of{i}", excl=True) for i in range(2)]
    ps_s = [cx.psum(f"ps_st{i}", [128, 512], F32) for i in range(2)]; r_ps_s = [Res(f"ps_st{i}", excl=True) for i in range(2)]
    ps_t = cx.psum("ps_tr", [128, 1024], BF16); r_ps_t = Res("ps_tr", excl=True)

    for dst, r, key in ((dbc, r_dbc, "dbc"), (gbc, r_gbc, "gbc"), (idf, r_idf, "ident")):
        cx.dma("sp", dst[:], w[key], writes=[r], slot=r)
    load_weight_bf16(cx, wout, r_wout, w["wout"], 16)
    cx.op("dve", lambda e: e.tensor_copy(idb[:], idf[:]), reads=[r_idf], writes=[r_idb])
    BTv = scr["BT"].rearrange("(g p) t -> p g t", p=128)
    CTv = scr["CT"].rearrange("(g p) t -> p g t", p=128)
    hc = [0]
    for d in range(2):
        cx.op("pool", lambda e: e.memset(HT[:, :, :], 0.0), writes=r_HT)
        cx.op("pool", lambda e: e.memset(HTb[:, :, :], 0.0), writes=r_HTb)
        order = range(NCH) if d == 0 else range(NCH - 1, -1, -1)
        dc = slice(d * 32, d * 32 + 32)
        for c in order:
            r0 = c * 128
            cx.dma("sp", xt[:], scr["xtok"][r0:r0 + 128, :], reads=[scr["r_tok"]], writes=[r_xt], slot=r_xt)
            cx.dma("sp", bt[:], scr["Btok"][r0:r0 + 128, :], reads=[scr["r_tok"]], writes=[r_bt], slot=r_bt)
            cx.dma("sp", BTs[:], BTv[:, :, r0:r0 + 128], reads=[scr["r_BC"]], writes=[r_BTs], slot=r_BTs)
            cx.dma("sp", CTs[:], CTv[:, :, r0:r0 + 128], reads=[scr["r_BC"]], writes=[r_CTs], slot=r_CTs)
            cx.dma("sp", dtt[:], scr["dttok"][r0:r0 + 128, :], reads=[scr["r_ac"]], writes=[r_dtt], slot=r_dtt)
            cx.dma("sp", act[:], scr["actok"][r0:r0 + 128, :], reads=[scr["r_ac"]], writes=[r_act], slot=r_act)
            rl = r0 + 127 if d == 0 else r0
            cx.dma("sp", totb[:], scr["actok"][rl:rl + 1, dc].partition_broadcast(128), reads=[scr["r_ac"]], writes=[r_totb], slot=r_totb)
            cx.dma("sp", rowb[:], scr["acT"][d * 32:d * 32 + 32, r0:r0 + 128].partition_broadcast(128), reads=[scr["r_ac"]], writes=[r_rowb], slot=r_rowb)
            cx.op("dve", lambda e: e.tensor_scalar(out=sm[:, 0, :], in0=act[:, dc], scalar1=-1.0, scalar2=None, op0=ALU.mult), reads=[r_act], writes=[r_sm])
            cx.op("dve", lambda e: e.tensor_tensor(out=sm[:, 1, :], in0=totb[:, :], in1=act[:, dc], op=ALU.subtract), reads=[r_totb, r_act, r_sm], writes=[r_sm])
            cx.op("act", lambda e: e.activation(out=sm[:, 1, :], in_=sm[:, 1, :], func=AF.Exp), reads=[r_sm], writes=[r_sm])
            cx.op("act", lambda e: e.activation(out=sm[:, 2, :], in_=totb[:, :], func=AF.Exp), reads=[r_totb, r_sm], writes=[r_sm])
            cx.op("act", lambda e: e.activation(out=sm[:, 3, :], in_=act[:, dc], func=AF.Exp), reads=[r_act, r_sm], writes=[r_sm])
            for g in range(NG):
                cx.op("pe", lambda e: e.matmul(ps_cb[:, :128], BTs[:, g, :], CTs[:, g, :], start=True, stop=True),
                      reads=[r_BTs, r_CTs], writes=[r_ps_cb])
                cx.op("act", lambda e: e.copy(out=cbt[:, g, :], in_=ps_cb[:, :128]), reads=[r_ps_cb], writes=[r_cbt[g]])
            cx.op("dve", lambda e: e.tensor_tensor(out=xdt[:, :].rearrange("p (h q) -> p h q", h=NHS), in0=xt[:, :].rearrange("p (h q) -> p h q", h=NHS),
                                                   in1=dtt[:, dc].unsqueeze(2).to_broadcast([128, NHS, 64]), op=ALU.mult),
                  reads=[r_xt, r_dtt], writes=[r_xdt])
            sgn = 1 if d == 0 else -1
            for q in range(4):
                yb_, ob_ = ps_y[q % 2], ps_o[q % 2]
                r_yb_, r_ob_ = r_ps_y[q % 2], r_ps_o[q % 2]
                for gg in range(2):
                    g = q * 2 + gg; i2 = hc[0] % 2; hc[0] += 1
                    hs = slice(4 * g, 4 * g + 4)
                    B4 = [128, 4, 128]
                    cx.op("dve", lambda e: e.tensor_tensor(out=dif[i2][:, :, :], in0=rowb[:, hs, :], in1=sm[:, 0, hs].unsqueeze(2).to_broadcast(B4), op=ALU.add),
                          reads=[r_rowb, r_sm], writes=[r_dif[i2]])
                    cx.op("pool", lambda e: e.affine_select(out=dif[i2][:, :, :], in_=dif[i2][:, :, :], pattern=[[0, 4], [sgn, 128]], compare_op=ALU.is_ge,
                                                            fill=cx.fillneg, base=0, channel_multiplier=-sgn),
                          reads=[r_dif[i2]], writes=[r_dif[i2]])
                    cx.op("act", lambda e: e.activation(out=dif[i2][:, :, :], in_=dif[i2][:, :, :], func=AF.Exp), reads=[r_dif[i2]], writes=[r_dif[i2]])
                    cx.op("dve", lambda e: e.tensor_tensor(out=MT[i2][:, :, :], in0=dif[i2][:, :, :], in1=cbt[:, g, :].unsqueeze(1).to_broadcast(B4), op=ALU.mult),
                          reads=[r_dif[i2], r_cbt[g]], writes=[r_MT[i2]])
                    cx.op("dve", lambda e: e.tensor_tensor(out=Bw[i2][:, :, :], in0=bt[:, g * 128:(g + 1) * 128].unsqueeze(1).to_broadcast(B4),
                                                           in1=sm[:, 1, hs].unsqueeze(2).to_broadcast(B4), op=ALU.mult),
                          reads=[r_bt, r_sm], writes=[r_Bw[i2]])
                    sb = ps_s[i2]; r_sb = r_ps_s[i2]
                    for hq in range(4):
                        h = 4 * g + hq; hh = gg * 4 + hq
                        cx.op("pe", lambda e: e.matmul(yb_[:, hh * 64:(hh + 1) * 64], MT[i2][:, hq, :], xdt[:, h * 64:(h + 1) * 64], start=True, stop=True),
                              reads=[r_MT[i2], r_xdt], writes=[r_yb_])
                        cx.op("pe", lambda e: e.matmul(ob_[:, hh * 64:(hh + 1) * 64], CTs[:, g, :], HTb[:, h, :], start=True, stop=True),
                              reads=[r_CTs, r_HTb[g]], writes=[r_ob_])
                        cx.op("pe", lambda e: e.matmul(sb[:, hq * 64:(hq + 1) * 64], Bw[i2][:, hq, :], xdt[:, h * 64:(h + 1) * 64], start=True, stop=True),
                              reads=[r_Bw[i2], r_xdt], writes=[r_sb])
                    cx.op("dve", lambda e: e.tensor_tensor(out=HT[:, hs, :], in0=HT[:, hs, :], in1=sm[:, 2, hs].unsqueeze(2).to_broadcast([128, 4, 64]), op=ALU.mult),
                          reads=[r_HT[g], r_sm], writes=[r_HT[g]])
                    cx.op("dve", lambda e: e.tensor_tensor(out=HT[:, hs, :], in0=sb[:, :256].rearrange("p (h q) -> p h q", h=4), in1=HT[:, hs, :], op=ALU.add),
                          reads=[r_HT[g], r_sb], writes=[r_HT[g]])
                    cx.op("pool", lambda e: e.tensor_copy(HTb[:, hs, :], HT[:, hs, :]), reads=[r_HT[g]], writes=[r_HTb[g]])
                qs = slice(q * 512, (q + 1) * 512)
                h8 = slice(q * 8, q * 8 + 8)
                cx.op("act", lambda e: e.copy(out=ysb[:, qs], in_=yb_[:, :]), reads=[r_yb_], writes=[r_ysb[q]])
                cx.op("dve", lambda e: e.tensor_tensor(out=yof[:, :].rearrange("p (h q) -> p h q", h=8), in0=ob_[:, :].rearrange("p (h q) -> p h q", h=8),
                                                       in1=sm[:, 3, h8].unsqueeze(2).to_broadcast([128, 8, 64]), op=ALU.mult),
                      reads=[r_ob_, r_sm], writes=[r_yof])
                cx.op("pool", lambda e: e.tensor_tensor(out=ysb[:, qs], in0=ysb[:, qs], in1=yof[:, :], op=ALU.add), reads=[r_yof, r_ysb[q]], writes=[r_ysb[q]])
            if d == 0:
                cx.dma("sp", scr["y0"][r0:r0 + 128, :], ysb[:, :], reads=r_ysb, writes=[scr["r_y0"]], slot=r_ysb[0])
                continue
            cx.dma("sp", y0[:], scr["y0"][r0:r0 + 128, :], reads=[scr["r_y0"]], writes=[r_y0], slot=r_y0)
            cx.dma("sp", szt[:], scr["sz"][r0:r0 + 128, :], reads=[scr["r_sz"]], writes=[r_szt], slot=r_szt)
            cx.dma("sp", xw[:], x_in_v[:, :, PAD + r0:PAD + r0 + 128], reads=[w["r_xin"]], writes=[r_xw], slot=r_xw)
            cx.op("pool", lambda e: e.tensor_tensor(out=y0[:, :], in0=y0[:, :], in1=ysb[:, :], op=ALU.add), reads=[r_y0] + r_ysb, writes=[r_y0])
            cx.op("dve", lambda e: e.tensor_tensor(out=ysb[:, :], in0=xt[:, :], in1=dbc[:, :], op=ALU.mult), reads=[r_xt, r_dbc] + r_ysb, writes=r_ysb)
            cx.op("pool", lambda e: e.tensor_tensor(out=y0[:, :], in0=y0[:, :], in1=ysb[:, :], op=ALU.add), reads=[r_y0] + r_ysb, writes=[r_y0])
            cx.op("dve", lambda e: e.tensor_tensor(out=y0[:, :], in0=y0[:, :], in1=szt[:, :], op=ALU.mult), reads=[r_y0, r_szt], writes=[r_y0])
            cx.op("pool", lambda e: e.tensor_tensor(out=ysb[:, :], in0=y0[:, :], in1=y0[:, :], op=ALU.mult), reads=[r_y0] + r_ysb, writes=r_ysb)
            cx.op("dve", lambda e: e.tensor_reduce(out=gss[:, :], in_=ysb[:, :].rearrange("p (g f) -> p g f", g=NG), axis=mybir.AxisListType.X, op=ALU.add),
                  reads=r_ysb, writes=[r_gss])
            cx.op("act", lambda e: e.activation(out=gss[:, :], in_=gss[:, :], func=AF.Ln, bias=cm.eps[:, 0:1], scale=1.0 / 256), reads=[r_gss, cm.r_eps], writes=[r_gss])
            cx.op("act", lambda e: e.activation(out=gss[:, :], in_=gss[:, :], func=AF.Exp, scale=-0.5), reads=[r_gss], writes=[r_gss])
            for g in range(NG):
                gs = slice(g * 256, (g + 1) * 256)
                cx.op("dve", lambda e: e.scalar_tensor_tensor(out=yb[:, gs], in0=y0[:, gs], scalar=gss[:, g:g + 1], in1=gbc[:, gs], op0=ALU.mult, op1=ALU.mult),
                      reads=[r_y0, r_gss, r_gbc], writes=[r_yb])
            for half in range(2):
                for cq in range(8):
                    cch = half * 8 + cq
                    cx.op("pe", lambda e: e.transpose(ps_t[:, cq * 128:(cq + 1) * 128], yb[:, cch * 128:(cch + 1) * 128], idb[:]),
                          reads=[r_yb, r_idb], writes=[r_ps_t])
                cx.op("act", lambda e: e.copy(out=yT[:, half * 8:(half + 1) * 8, :], in_=ps_t[:, :].rearrange("p (a q) -> p a q", a=8)),
                      reads=[r_ps_t], writes=[r_yT])
            for m in range(KC):
                pb = ps_y[m % 2]; r_pb = r_ps_y[m % 2]
                for cch in range(16):
                    cx.op("pe", lambda e, cch=cch: e.matmul(pb[:, :128], wout[:, cch, m * 128:(m + 1) * 128], yT[:, cch, :], start=(cch == 0), stop=(cch == 15)),
                          reads=[r_wout, r_yT], writes=[r_pb], signal=(cch == 15))
                cx.op("dve", lambda e: e.tensor_tensor(out=xw[:, m, :], in0=pb[:, :128], in1=xw[:, m, :], op=ALU.add), reads=[r_pb, r_xw], writes=[r_xw])
            cx.dma("sp", x_out_v[:, :, PAD + r0:PAD + r0 + 128], xw[:, :, :], reads=[r_xw], writes=[w["r_xout"]], slot=r_xw)
    cx.barrier()
    cx.es = old_es
    es.close()


def make_ssd_scratch(nc, T):
    d = lambda n, s, dt: nc.dram_tensor(n, s, dt, kind="Internal").ap()
    return dict(xtok=d("s_xtok", [T, DI], BF16), Btok=d("s_btok", [T, NG * DS], BF16), BT=d("s_BT", [NG * DS, T], BF16),
                CT=d("s_CT", [NG * DS, T], BF16), sz=d("s_sz", [T, DI], BF16), actok=d("s_actok", [T, 64], F32),
                dttok=d("s_dttok", [T, 64], F32), acT=d("s_acT", [64, T], F32), y0=d("s_y0", [T, DI], F32),
                r_tok=Res("s_tok"), r_BC=Res("s_BC"), r_sz=Res("s_sz"), r_ac=Res("s_ac"), r_y0=Res("s_y0"))


DEPTH = 4
ROT = 16


def build_program(T):
    TP = T + 2 * PAD
    nc = bass.Bass("TRN2", target_bir_lowering=False)
    di = lambda n, s: nc.dram_tensor(n, list(s), F32, kind="ExternalInput").ap()
    xin = di("xin", [D, TP])
    xout = nc.dram_tensor("xout", [D, TP], F32, kind="ExternalOutput").ap()
    xmid = nc.dram_tensor("xmid", [D, TP], F32, kind="Internal").ap()
    bm = di("bm", [128, 128]); rm = di("rm", [128, 128]); ident = di("ident", [128, 128])
    cos = di("cos", [128, T]); sin = di("sin", [128, T])
    A = {}
    for j in range(2):
        A[j] = dict(wqkv=di(f"a{j}_wqkv", [D, WQKV]), wo=di(f"a{j}_wo", [D, D]), gn=di(f"a{j}_gn", [128, KC]), gqk=di(f"a{j}_gqk", [128, 2]),
                    sink=di(f"a{j}_sink", [128, NH]))
    S = {}
    for j in range(2):
        S[j] = dict(win=di(f"s{j}_win", [D, SIN]), wout=di(f"s{j}_wout", [DI, D]), gn=di(f"s{j}_gn", [128, KC]), cwb=di(f"s{j}_cwb", [128, 32, 6]),
                    dtp=di(f"s{j}_dtp", [64, 2]), dbc=di(f"s{j}_dbc", [128, DI]), gbc=di(f"s{j}_gbc", [128, DI]))
    Fw = {}
    for i in range(DEPTH):
        Fw[i] = dict(wup=di(f"f{i}_wup", [D, 2 * DFF]), wdn=di(f"f{i}_wdn", [DFF, D]), cwb=di(f"f{i}_cwb", [128, NFC, 4]), gn=di(f"f{i}_gn", [128, KC]))
    scr = make_ssd_scratch(nc, T)
    with ExitStack() as es:
        cx = Ctx(nc, es)
        cm = Common(cx)
        r = {"xin": Res("xin"), "xout": Res("xout"), "xmid": Res("xmid")}
        aps = {"xin": xin, "xout": xout, "xmid": xmid}
        zes = ExitStack(); cx.es = zes
        zt = cx.sbuf("zt", [128, KC, PAD], F32); r_zt = Res("zt")
        cx.op("pool", lambda e: e.memset(zt[:], 0.0), writes=[r_zt])
        for nm in ("xout", "xmid"):
            v = aps[nm].rearrange("(k p) t -> p k t", p=128)
            cx.dma("sp", v[:, :, 0:PAD], zt[:], reads=[r_zt], writes=[r[nm]], slot=r_zt)
            cx.dma("sp", v[:, :, PAD + T:PAD + T + PAD], zt[:], reads=[r_zt], writes=[r[nm]], slot=r_zt)
        cx.barrier()
        cx.es = es
        zes.close()
        seq = ["xin"] + ["xmid", "xout"] * DEPTH
        step = 0
        for i in range(DEPTH):
            j = i // 2
            src, dst = seq[step], seq[step + 1]; step += 1
            if i % 2 == 0:
                w = dict(wqkv=A[j]["wqkv"].rearrange("(k p) n -> p k n", p=128), wo=A[j]["wo"].rearrange("(k p) n -> p k n", p=128),
                         gn=A[j]["gn"], gqk=A[j]["gqk"], sink=A[j]["sink"], bm=bm, rm=rm, cos=cos, sin=sin, r_xin=r[src], r_xout=r[dst])
                attn_layer(cx, cm, T, aps[src], aps[dst], w)
            else:
                w = dict(win=S[j]["win"].rearrange("(k p) n -> p k n", p=128), wout=S[j]["wout"].rearrange("(k p) n -> p k n", p=128),
                         gn=S[j]["gn"], cwb=S[j]["cwb"], dtp=S[j]["dtp"], dbc=S[j]["dbc"], gbc=S[j]["gbc"], ident=ident, r_xin=r[src], r_xout=r[dst])
                ssd_layer(cx, cm, T, aps[src], aps[dst], w, scr)
            src, dst = seq[step], seq[step + 1]; step += 1
            w = dict(wup=Fw[i]["wup"].rearrange("(k p) n -> p k n", p=128), wdn=Fw[i]["wdn"].rearrange("(c p) n -> p c n", p=128),
                     cwb=Fw[i]["cwb"], gn=Fw[i]["gn"], r_xin=r[src], r_xout=r[dst])
            ffn_layer(cx, cm, T, aps[src], aps[dst], w)
        assert dst == "xout"
        cx.barrier(fresh=False)
    return nc


def host_layout(inp, T):
    f = lambda a: np.ascontiguousarray(np.asarray(a, dtype=np.float32))
    col = lambda v: f(np.asarray(v).reshape(KC, 128).T)
    m = {}
    m["bm"] = f(np.kron(np.eye(2), np.full((64, 64), 1.0 / 64)))
    rm = np.zeros((128, 128), np.float32)
    for blk in (0, 64):
        for q in range(8):
            rm[blk + q + 8, blk + q] = -1.0
            rm[blk + q, blk + q + 8] = 1.0
    m["rm"] = rm
    m["ident"] = np.eye(128, dtype=np.float32)
    pos = np.arange(T, dtype=np.float32)
    inv_freq = (np.float32(500000.0) ** (-(np.arange(0, ROT, 2, dtype=np.float32) / np.float32(ROT)))).astype(np.float32)
    ang = (pos[:, None] * inv_freq[None, :]).astype(np.float32)
    cosv, sinv = np.cos(ang).astype(np.float32), np.sin(ang).astype(np.float32)
    cosT = np.ones((128, T), np.float32); sinT = np.zeros((128, T), np.float32)
    for blk in (0, 64):
        for q in range(16):
            cosT[blk + q] = cosv[:, q % 8]; sinT[blk + q] = sinv[:, q % 8]
    m["cos"], m["sin"] = cosT, sinT
    for j in range(2):
        wq = np.asarray(inp["attn_w_qkv"][j])
        m[f"a{j}_wqkv"] = f(np.concatenate([wq[:, :QD]] + [np.tile(wq[:, QD + k * 64:QD + (k + 1) * 64], (1, 2)) for k in range(NKV)] + [wq[:, QD + 256:]], axis=1))
        m[f"a{j}_wo"] = f(inp["attn_w_o"][j])
        m[f"a{j}_gn"] = col(inp["attn_norm"][j])
        m[f"a{j}_gqk"] = f(np.stack([np.tile(np.asarray(inp["attn_q_norm"][j]), 2), np.tile(np.asarray(inp["attn_k_norm"][j]), 2)], 1))
        m[f"a{j}_sink"] = f(np.tile(np.asarray(inp["attn_sink"][j])[None, :], (128, 1)))
        m[f"s{j}_win"] = f(inp["ssd_w_in"][j])
        m[f"s{j}_wout"] = f(inp["ssd_w_out"][j])
        m[f"s{j}_gn"] = col(inp["ssd_norm"][j])
        cwb = np.zeros((128, 32, 6), np.float32)
        cwb[:, :, 0:5] = np.asarray(inp["ssd_conv_w"][j]).reshape(5, 32, 128).transpose(2, 1, 0)
        cwb[:, :, 5] = np.asarray(inp["ssd_conv_b"][j]).reshape(32, 128).T
        m[f"s{j}_cwb"] = cwb
        m[f"s{j}_dtp"] = f(np.stack([np.asarray(inp["ssd_dt_bias"][j]).reshape(64), np.asarray(inp["ssd_a_log"][j]).reshape(64)], 1))
        m[f"s{j}_dbc"] = f(np.tile(np.repeat(np.asarray(inp["ssd_d"][j]), 64)[None, :], (128, 1)))
        m[f"s{j}_gbc"] = f(np.tile(np.asarray(inp["ssd_gate_norm"][j])[None, :], (128, 1)))
    for i in range(DEPTH):
        m[f"f{i}_wup"] = f(inp["ffn_w_up"][i])
        m[f"f{i}_wdn"] = f(inp["ffn_w_down"][i])
        cwb = np.zeros((128, NFC, 4), np.float32)
        cwb[:, :, 0:3] = np.asarray(inp["ffn_conv_w"][i]).reshape(3, NFC, 128).transpose(2, 1, 0)
        cwb[:, :, 3] = np.asarray(inp["ffn_conv_b"][i]).reshape(NFC, 128).T
        m[f"f{i}_cwb"] = cwb
        m[f"f{i}_gn"] = col(inp["ffn_norm"][i])
    return m


def run_module(inp, n_cores=8):
    x = np.asarray(inp["x"], dtype=np.float32)
    B, T, _ = x.shape
    nc = build_program(T)
    shared = host_layout(inp, T)
    in_maps = []
    for c in range(n_cores):
        b = c % B
        xin = np.zeros((D, T + 2 * PAD), np.float32)
        xin[:, PAD:PAD + T] = x[b].T
        mp = dict(shared); mp["xin"] = xin
        in_maps.append(mp)
    res = run_bass_kernel_spmd(nc, in_maps, core_ids=list(range(n_cores)))
    out = np.stack([np.ascontiguousarray(res.results[b]["xout"][:, PAD:PAD + T].T) for b in range(B)], 0)
    return out.astype(np.float32)


def kernel(**inputs):
    return run_module(inputs)
```

```python
import numpy as np
from contextlib import ExitStack
import concourse.bass as bass
import concourse.mybir as mybir
from concourse.bass_utils import run_bass_kernel_spmd

F32 = mybir.dt.float32
BF16 = mybir.dt.bfloat16
AF = mybir.ActivationFunctionType
ALU = mybir.AluOpType

D = 1024
KC = D // 128
DFF = 2816
NFC = 2 * DFF // 128
NGC = DFF // 128
EPS = 1e-6
PAD = 128


class Sem:
    def __init__(self, handle, name):
        self.h = handle
        self.name = name
        self.count = 0


class Res:
    def __init__(self, name, excl=False):
        self.name = name
        self.excl = excl
        self.last_w = None
        self.readers = {}
        self.dsem = None


class Ctx:
    ENG = ("pe", "act", "dve", "pool", "sp")

    def __init__(self, nc, es):
        self.nc = nc
        self.es = es
        self.top_es = es
        self.fill0 = nc.gpsimd.to_reg(0.0)
        self.fillneg = nc.gpsimd.to_reg(-30000.0)
        self.eng = {"pe": nc.tensor, "act": nc.scalar, "dve": nc.vector, "pool": nc.gpsimd, "sp": nc.sync}
        self.sems = {}
        self.waited = {e: {} for e in self.ENG}
        self.gen = 0
        self._new_sems()
        self.uid = 0

    def _new_sems(self):
        self.gen += 1
        for e in self.ENG:
            self.sems[e] = Sem(self.es.enter_context(self.nc.semaphore(f"s_{e}_{self.gen}")), e)
        if not hasattr(self, "dfree"):
            self.dfree, self.dused, self.nd, self.dfresh = [], [], 0, []

    def name(self, base):
        self.uid += 1
        return f"{base}_{self.uid}"

    def sbuf(self, name, shape, dtype):
        t = self.es.enter_context(self.nc.sbuf_tensor(self.name(name), list(shape), dtype))
        return t

    def psum(self, name, shape, dtype):
        t = self.es.enter_context(self.nc.psum_tensor(self.name(name), list(shape), dtype))
        return t

    def _deps(self, eng, reads, writes):
        deps = []
        for r in reads:
            if r.last_w is not None:
                s, v, e = r.last_w
                if not (e == eng and eng == "pe"):
                    deps.append((s, v))
            if r.excl:
                for e, (s, v) in r.readers.items():
                    if e != eng:
                        deps.append((s, v))
        for w in writes:
            if w.last_w is not None:
                s, v, e = w.last_w
                if e != eng or eng != "pe":
                    deps.append((s, v))
            for e, (s, v) in w.readers.items():
                if e != eng or eng != "pe":
                    deps.append((s, v))
        return deps

    def _wait(self, eng, deps):
        wd = self.waited[eng]
        best = {}
        for s, v in deps:
            if wd.get(s, 0) >= v:
                continue
            if best.get(s, 0) < v:
                best[s] = v
        for s, v in best.items():
            assert v <= s.count, f"wait on un-emitted signal {s.name} {v}>{s.count} (engine {eng})"
            self.eng[eng].wait_ge(s.h, v)
            wd[s] = v

    def _stamp(self, eng, stamp, reads, writes):
        s, v = stamp
        for r in reads:
            r.readers[eng] = (s, v)
        for w in writes:
            w.last_w = (s, v, eng)
            w.readers = {}

    def op(self, eng, fn, reads=(), writes=(), signal=True):
        reads = [r for r in reads if r is not None]
        writes = [w for w in writes if w is not None]
        self._wait(eng, self._deps(eng, reads, writes))
        ins = fn(self.eng[eng])
        s = self.sems[eng]
        if signal:
            s.count += 1
            ins.then_inc(s.h, 1)
            stamp = (s, s.count)
        else:
            stamp = (s, s.count + 1)
        self._stamp(eng, stamp, reads, writes)
        return ins

    def _dsem(self, res, fresh=False):
        if res.dsem is None and fresh:
            self.nd += 1
            res.dsem = Sem(self.top_es.enter_context(self.nc.semaphore(f"s_dma_{self.nd}")), f"dma{self.nd}")
            self.dfresh.append(res.dsem)
        if res.dsem is None:
            while self.dfree and self.dfree[-1].count > 30000:
                self.dfree.pop()
            if self.dfree:
                res.dsem = self.dfree.pop()
            else:
                self.nd += 1
                res.dsem = Sem(self.top_es.enter_context(self.nc.semaphore(f"s_dma_{self.nd}")), f"dma{self.nd}")
            self.dused.append(res.dsem)
        return res.dsem

    def dma(self, queue, out, in_, reads=(), writes=(), slot=None):
        reads = [r for r in reads if r is not None]
        writes = [w for w in writes if w is not None]
        self._wait(queue, self._deps("dma", reads, writes))
        ins = self.eng[queue].dma_start(out=out, in_=in_)
        s = self._dsem(slot, fresh=(queue == "pool"))
        s.count += 16
        ins.then_inc(s.h, 16)
        self._stamp("dma", (s, s.count), reads, writes)
        return ins

    def barrier(self, fresh=True):
        for e in self.ENG:
            deps = [(s, s.count) for s in list(self.sems.values()) + self.dused + self.dfresh if s.count > 0]
            self._wait(e, deps)
        self.dfree.extend(self.dused)
        self.dused = []
        self.dfresh = []
        if fresh:
            old = dict(self.sems)
            self._new_sems()
            self._old = old


class Common:
    def __init__(self, cx):
        nc = cx.nc
        self.ones_mean = cx.sbuf("ones_mean", [128, 128], F32)
        self.r_ones = Res("ones_mean")
        cx.op("pool", lambda e: e.memset(self.ones_mean[:], 1.0 / D), writes=[self.r_ones])
        self.eps = cx.sbuf("eps", [128, 1], F32)
        self.r_eps = Res("eps")
        cx.op("pool", lambda e: e.memset(self.eps[:], EPS), writes=[self.r_eps])
        self.one = cx.sbuf("one", [128, 1], F32)
        self.r_one = Res("one")
        cx.op("pool", lambda e: e.memset(self.one[:], 1.0), writes=[self.r_one])


def rmsnorm_fm(cx, cm, xw, r_xw, W, g_ap, r_g, hn, r_hn, ps, r_ps, sq, r_sq, rstd, r_rstd):
    for k in range(KC):
        b = k % 2
        cx.op("act", lambda e, k=k, b=b: e.activation(out=sq[b][:, :W], in_=xw[:, k, :W], func=AF.Square),
              reads=[r_xw], writes=[r_sq[b]])
        cx.op("pe", lambda e, k=k, b=b: e.matmul(ps[:, :W], cm.ones_mean[:], sq[b][:, :W],
                                                    start=(k == 0), stop=(k == KC - 1)),
              reads=[cm.r_ones, r_sq[b]], writes=[r_ps], signal=True)
    cx.op("act", lambda e: e.activation(out=rstd[:, :W], in_=ps[:, :W], func=AF.Ln, bias=cm.eps[:, 0:1]),
          reads=[r_ps, cm.r_eps], writes=[r_rstd])
    cx.op("act", lambda e: e.activation(out=rstd[:, :W], in_=rstd[:, :W], func=AF.Exp, scale=-0.5),
          reads=[r_rstd], writes=[r_rstd])
    for k in range(KC):
        cx.op("dve", lambda e, k=k: e.scalar_tensor_tensor(out=hn[:, k, :W], in0=xw[:, k, :W], scalar=g_ap(k),
                                                        in1=rstd[:, :W], op0=ALU.mult, op1=ALU.mult),
              reads=[r_xw, r_rstd, r_g], writes=[r_hn[k]])


def load_weight_bf16(cx, dst, r_dst, src_ap, nk, split=1):
    for k in range(nk):
        cx.dma("pool", dst[:, k, :], src_ap[:, k, :], writes=[r_dst], slot=r_dst)


def ffn_layer(cx, cm, T, x_in, x_out, w):
    nc = cx.nc
    es = ExitStack()
    old_es = cx.es
    cx.es = es
    wup = cx.sbuf("wup", [128, KC, 2 * DFF], BF16); r_wup = Res("wup")
    wdn = cx.sbuf("wdn", [128, NGC, D], BF16); r_wdn = Res("wdn")
    cwb = cx.sbuf("cwb", [128, NFC, 4], F32); r_cwb = Res("cwb")
    gn = cx.sbuf("gn", [128, KC], F32); r_gn = Res("gn")
    xws = [cx.sbuf(f"xw{i}", [128, KC, 512], F32) for i in range(2)]; r_xws = [Res("xw0"), Res("xw1")]
    hn = cx.sbuf("hn", [128, KC, 512], BF16); r_hn = [Res(f"hn{k}") for k in range(KC)]
    gT = cx.sbuf("gT", [128, NGC, 512], BF16); r_gT = [Res(f"gT{c}") for c in range(NGC)]
    rstd = cx.sbuf("rstd", [128, 512], F32); r_rstd = Res("rstd")
    NT = 2
    t1 = [[cx.sbuf(f"t1_{i}_{j}", [128, 512], F32) for j in range(2)] for i in range(NT)]
    r_t1 = [[Res(f"t1_{i}_{j}") for j in range(2)] for i in range(NT)]
    sq = [t1[0][0], t1[0][1]]; r_sq = [r_t1[0][0], r_t1[0][1]]
    ps_st = cx.psum("ps_st", [128, 512], F32); r_ps_st = Res("ps_st", excl=True)
    ps_gv = [[cx.psum(f"ps_gv{i}{j}", [128, 512], F32) for j in range(2)] for i in range(2)]
    r_ps_gv = [[Res(f"ps_gv{i}{j}", excl=True) for j in range(2)] for i in range(2)]
    ps_o = [cx.psum(f"ps_o{i}", [128, 512], F32) for i in range(2)]
    r_ps_o = [Res(f"ps_o{i}", excl=True) for i in range(2)]

    cx.dma("sp", cwb[:], w["cwb"], writes=[r_cwb], slot=r_cwb)
    cx.dma("sp", gn[:], w["gn"], writes=[r_gn], slot=r_gn)
    load_weight_bf16(cx, wup, r_wup, w["wup"], KC)
    load_weight_bf16(cx, wdn, r_wdn, w["wdn"], NGC)

    x_in_v = x_in.rearrange("(k p) t -> p k t", p=128)
    x_out_v = x_out.rearrange("(k p) t -> p k t", p=128)
    STEP = 510
    tiles = [(t0, min(STEP, T - t0)) for t0 in range(0, T, STEP)]

    def load(i):
        t0, n_out = tiles[i]
        W = n_out + 2
        cx.dma("sp", xws[i % 2][:, :, :W], x_in_v[:, :, PAD + t0 - 1:PAD + t0 - 1 + W], reads=[w["r_xin"]], writes=[r_xws[i % 2]], slot=r_xws[i % 2])

    def norm(i):
        t0, n_out = tiles[i]
        rmsnorm_fm(cx, cm, xws[i % 2], r_xws[i % 2], n_out + 2, lambda k: gn[:, k:k + 1], r_gn, hn, r_hn, ps_st, r_ps_st, sq, r_sq, rstd, r_rstd)

    load(0)
    if len(tiles) > 1:
        load(1)
    norm(0)
    pair_i = 0
    for ti, (t0, n_out) in enumerate(tiles):
        W = n_out + 2
        xw = xws[ti % 2]; r_xw = r_xws[ti % 2]
        for cg in range(NGC):
            pb = pair_i % 2
            tb = pair_i % NT
            pair_i += 1
            for j, c in enumerate((cg, cg + NGC)):
                ps = ps_gv[pb][j]; r_ps = r_ps_gv[pb][j]
                for k in range(KC):
                    cx.op("pe", lambda e, k=k, c=c, ps=ps: e.matmul(ps[:, :W], wup[:, k, c * 128:(c + 1) * 128], hn[:, k, :W],
                                                                   start=(k == 0), stop=(k == KC - 1)),
                          reads=[r_wup, r_hn[k]], writes=[r_ps], signal=(k == KC - 1))
            for j, c in enumerate((cg, cg + NGC)):
                ps = ps_gv[pb][j]; r_ps = r_ps_gv[pb][j]
                tt = t1[tb][j]; r_tt = r_t1[tb][j]
                cx.op("act", lambda e, c=c, ps=ps, tt=tt: e.activation(out=tt[:, :n_out], in_=ps[:, 1:1 + n_out], func=AF.Identity,
                                                                        bias=cwb[:, c, 3:4], scale=cwb[:, c, 1:2]),
                      reads=[r_ps, r_cwb], writes=[r_tt])
                cx.op("dve", lambda e, c=c, ps=ps, tt=tt: e.scalar_tensor_tensor(out=tt[:, :n_out], in0=ps[:, 0:n_out], scalar=cwb[:, c, 0:1],
                                                                               in1=tt[:, :n_out], op0=ALU.mult, op1=ALU.add),
                      reads=[r_ps, r_cwb, r_tt], writes=[r_tt])
                cx.op("dve", lambda e, c=c, ps=ps, tt=tt: e.scalar_tensor_tensor(out=tt[:, :n_out], in0=ps[:, 2:2 + n_out], scalar=cwb[:, c, 2:3],
                                                                               in1=tt[:, :n_out], op0=ALU.mult, op1=ALU.add),
                      reads=[r_ps, r_cwb, r_tt], writes=[r_tt])
            tg = t1[tb][0]; tv = t1[tb][1]
            cx.op("act", lambda e, tg=tg: e.activation(out=tg[:, :n_out], in_=tg[:, :n_out], func=AF.Silu),
                  reads=[r_t1[tb][0]], writes=[r_t1[tb][0]])
            cx.op("pool", lambda e, tg=tg, tv=tv, cg=cg: e.tensor_tensor(out=gT[:, cg, :n_out], in0=tg[:, :n_out], in1=tv[:, :n_out], op=ALU.mult),
                  reads=[r_t1[tb][0], r_t1[tb][1]], writes=[r_gT[cg]])
        if ti + 1 < len(tiles):
            norm(ti + 1)
        for m in range(KC):
            ob = m % 2
            for c in range(NGC):
                cx.op("pe", lambda e, c=c, m=m, ob=ob: e.matmul(ps_o[ob][:, :n_out], wdn[:, c, m * 128:(m + 1) * 128], gT[:, c, :n_out],
                                                                 start=(c == 0), stop=(c == NGC - 1)),
                      reads=[r_wdn, r_gT[c]], writes=[r_ps_o[ob]], signal=(c == NGC - 1))
            cx.op("dve", lambda e, m=m, ob=ob: e.tensor_tensor(out=xw[:, m, 1:1 + n_out], in0=ps_o[ob][:, :n_out], in1=xw[:, m, 1:1 + n_out], op=ALU.add),
                  reads=[r_ps_o[ob], r_xw], writes=[r_xw])
        cx.dma("sp", x_out_v[:, :, PAD + t0:PAD + t0 + n_out], xw[:, :, 1:1 + n_out], reads=[r_xw], writes=[w["r_xout"]], slot=r_xw)
        if ti + 2 < len(tiles):
            load(ti + 2)
    cx.barrier()
    cx.es = old_es
    es.close()


NH, NKV, HD = 16, 4, 64
QD = NH * HD
WQKV = QD + 2 * NKV * HD + NKV * HD


def attn_layer(cx, cm, T, x_in, x_out, w):
    es = ExitStack(); old_es = cx.es; cx.es = es
    NB = T // 128
    wq = cx.sbuf("wqkv", [128, KC, WQKV], BF16); r_wq = Res("wqkv")
    wo = cx.sbuf("wo", [128, KC, D], BF16); r_wo = Res("wo")
    gn = cx.sbuf("gn", [128, KC], F32); r_gn = Res("gn")
    gqk = cx.sbuf("gqk", [128, 2], F32); r_gqk = Res("gqk")
    esk = cx.sbuf("esk", [128, NH], F32); r_esk = Res("esk")
    bm = cx.sbuf("bm", [128, 128], F32); r_bm = Res("bm")
    rmf = cx.sbuf("rmf", [128, 128], F32); r_rmf = Res("rmf")
    rm = cx.sbuf("rm", [128, 128], BF16); r_rm = Res("rm")
    onesb = cx.sbuf("onesb", [128, 128], BF16); r_onesb = Res("onesb")
    kT = cx.sbuf("kT", [128, NKV, T], BF16); r_kT = [Res(f"kT{i}") for i in range(T // 512)]
    V = cx.sbuf("V", [128, NB, NKV * HD], BF16); r_V = [Res(f"V{i}") for i in range(NB)]
    vv = cx.sbuf("vv", [128, 4, NKV, 2, HD], BF16); r_vv = [Res(f"vv{i}") for i in range(4)]
    xw = cx.sbuf("xw", [128, KC, 512], F32); r_xw = Res("xw")
    hn = cx.sbuf("hn", [128, KC, 512], BF16); r_hn = [Res(f"hn{k}") for k in range(KC)]
    qT = cx.sbuf("qT", [128, KC, 512], BF16); r_qT = [Res(f"qT{k}") for k in range(KC)]
    oT = hn; r_oT = r_hn
    cs = cx.sbuf("cs", [128, 2, 512], F32); r_cs = Res("cs")
    sq = [cx.sbuf(f"sq{i}", [128, 512], F32) for i in range(2)]; r_sq = [Res("sq0"), Res("sq1")]
    rstd = cx.sbuf("rstd", [128, 512], F32); r_rstd = Res("rstd")
    qnb = cx.sbuf("qnb", [128, 512], BF16); r_qnb = Res("qnb")
    ta = cx.sbuf("ta", [128, 512], F32); r_ta = Res("ta")
    tb = cx.sbuf("tb", [128, 512], F32); r_tb = Res("tb")
    PT = [cx.sbuf(f"PT{i}", [128, 512], BF16) for i in range(4)]; r_PT = [Res(f"PT{i}") for i in range(4)]
    rd = cx.sbuf("rd", [128, 512], F32); r_rd = Res("rd")
    ps_p1 = cx.psum("ps_p0", [128, 512], F32); r_ps_p1 = Res("ps_p0", excl=True)
    ps_p = [ps_p1, ps_p1]; r_ps_p = [r_ps_p1, r_ps_p1]
    ps_a = cx.psum("ps_a", [128, 512], F32); r_ps_a = Res("ps_a", excl=True)
    ps_b = ps_a; r_ps_b = r_ps_a
    ps_s = [[cx.psum(f"ps_s{i}{hf}", [128, 512], F32) for hf in range(2)] for i in range(2)]
    r_ps_s = [[Res(f"ps_s{i}{hf}", excl=True) for hf in range(2)] for i in range(2)]
    ps_o = cx.psum("ps_o", [128, 512], F32); r_ps_o = Res("ps_o", excl=True)
    ps_d = cx.psum("ps_d", [128, 512], F32); r_ps_d = Res("ps_d", excl=True)

    for dst, r, key in ((gn, r_gn, "gn"), (gqk, r_gqk, "gqk"), (esk, r_esk, "sink"), (bm, r_bm, "bm"), (rmf, r_rmf, "rm")):
        cx.dma("sp", dst[:], w[key], writes=[r], slot=r)
    load_weight_bf16(cx, wq, r_wq, w["wqkv"], KC)
    load_weight_bf16(cx, wo, r_wo, w["wo"], KC)
    cx.op("act", lambda e: e.activation(out=esk[:], in_=esk[:], func=AF.Exp), reads=[r_esk], writes=[r_esk])
    cx.op("dve", lambda e: e.tensor_copy(rm[:], rmf[:]), reads=[r_rmf], writes=[r_rm])
    cx.op("pool", lambda e: e.memset(onesb[:], 1.0), writes=[r_onesb])

    x_in_v = x_in.rearrange("(k p) t -> p k t", p=128)
    x_out_v = x_out.rearrange("(k p) t -> p k t", p=128)
    pp = [0]

    def load_norm(t0):
        cx.dma("sp", xw[:, :, :], x_in_v[:, :, PAD + t0:PAD + t0 + 512], reads=[w["r_xin"]], writes=[r_xw], slot=r_xw)
        cx.dma("sp", cs[:, 0, :], w["cos"][:, t0:t0 + 512], writes=[r_cs], slot=r_cs)
        cx.dma("sp", cs[:, 1, :], w["sin"][:, t0:t0 + 512], writes=[r_cs], slot=r_cs)
        rmsnorm_fm(cx, cm, xw, r_xw, 512, lambda k: gn[:, k:k + 1], r_gn, hn, r_hn, ps_a, r_ps_a, sq, r_sq, rstd, r_rstd)

    def proj_fm(col0):
        b = pp[0] % 2; pp[0] += 1
        for k in range(KC):
            cx.op("pe", lambda e, k=k: e.matmul(ps_p[b][:, :], wq[:, k, col0:col0 + 128], hn[:, k, :],
                                                start=(k == 0), stop=(k == KC - 1)),
                  reads=[r_wq, r_hn[k]], writes=[r_ps_p[b]], signal=(k == KC - 1))
        return ps_p[b], r_ps_p[b]

    def headnorm_rope(ps, r_ps, gcol, out_ap, r_out):
        cx.op("act", lambda e: e.activation(out=sq[0][:, :], in_=ps[:, :], func=AF.Square), reads=[r_ps], writes=[r_sq[0]])
        cx.op("pe", lambda e: e.matmul(ps_a[:, :], bm[:], sq[0][:, :], start=True, stop=True),
              reads=[r_bm, r_sq[0]], writes=[r_ps_a])
        cx.op("act", lambda e: e.activation(out=rstd[:, :], in_=ps_a[:, :], func=AF.Ln, bias=cm.eps[:, 0:1]),
              reads=[r_ps_a, cm.r_eps], writes=[r_rstd])
        cx.op("act", lambda e: e.activation(out=rstd[:, :], in_=rstd[:, :], func=AF.Exp, scale=-0.5),
              reads=[r_rstd], writes=[r_rstd])
        cx.op("dve", lambda e: e.scalar_tensor_tensor(out=qnb[:, :], in0=ps[:, :], scalar=gqk[:, gcol:gcol + 1], in1=rstd[:, :],
                                                      op0=ALU.mult, op1=ALU.mult),
              reads=[r_ps, r_gqk, r_rstd], writes=[r_qnb])
        cx.op("pe", lambda e: e.matmul(ps_b[:, :], rm[:], qnb[:, :], start=True, stop=True),
              reads=[r_rm, r_qnb], writes=[r_ps_b])
        cx.op("dve", lambda e: e.tensor_tensor(out=ta[:, :], in0=qnb[:, :], in1=cs[:, 0, :], op=ALU.mult),
              reads=[r_qnb, r_cs], writes=[r_ta])
        cx.op("dve", lambda e: e.tensor_tensor(out=tb[:, :], in0=ps_b[:, :], in1=cs[:, 1, :], op=ALU.mult),
              reads=[r_ps_b, r_cs], writes=[r_tb])
        cx.op("pool", lambda e: e.tensor_tensor(out=out_ap, in0=ta[:, :], in1=tb[:, :], op=ALU.add),
              reads=[r_ta, r_tb], writes=[r_out])

    for ti in range(T // 512):
        t0 = ti * 512
        load_norm(t0)
        for j in range(NKV):
            ps, r_ps = proj_fm(QD + j * 128)
            headnorm_rope(ps, r_ps, 1, kT[:, j, t0:t0 + 512], r_kT[ti])
        for bl in range(4):
            b = pp[0] % 2; pp[0] += 1
            for k in range(KC):
                cx.op("pe", lambda e, k=k: e.matmul(ps_p[b][:, :256], hn[:, k, bl * 128:(bl + 1) * 128], wq[:, k, QD + 512:QD + 768],
                                                    start=(k == 0), stop=(k == KC - 1)),
                      reads=[r_wq, r_hn[k]], writes=[r_ps_p[b]], signal=(k == KC - 1))
            cx.op("act", lambda e: e.copy(out=V[:, ti * 4 + bl, :], in_=ps_p[b][:, :256]), reads=[r_ps_p[b]], writes=[r_V[ti * 4 + bl]])

    vv_have = {}
    sc = [0]
    for ti in range(T // 512):
        t0 = ti * 512
        load_norm(t0)
        for c in range(KC):
            ps, r_ps = proj_fm(c * 128)
            headnorm_rope(ps, r_ps, 0, qT[:, c, :], r_qT[c])
        for bl in range(4):
            nb = ti * 4 + bl
            qs = slice(bl * 128, (bl + 1) * 128)
            kbs = [kb for kb in (nb - 1, nb, nb + 1) if 0 <= kb < NB]
            for kb in kbs:
                if vv_have.get(kb % 4) != kb:
                    for d2 in range(2):
                        cx.op("pool", lambda e, d2=d2: e.tensor_copy(vv[:, kb % 4, :, d2, :], V[:, kb, :].rearrange("p (j d) -> p j d", j=NKV)),
                              reads=[r_V[kb]], writes=[r_vv[kb % 4]])
                    vv_have[kb % 4] = kb
            for j in range(NKV):
                pts = []
                for kb in kbs:
                    sb = sc[0] % 2; pb = sc[0] % 4; sc[0] += 1
                    for half in range(2):
                        rows = slice(half * 64, half * 64 + 64)
                        cx.op("pe", lambda e: e.matmul(ps_s[sb][half][:, :256], kT[rows, j, kb * 128:(kb + 1) * 128],
                                                       qT[rows, 2 * j:2 * j + 2, qs], start=True, stop=True),
                              reads=[r_kT[kb // 4], r_qT[2 * j], r_qT[2 * j + 1]], writes=[r_ps_s[sb][half]])
                    for half in range(2):
                        cx.op("act", lambda e: e.activation(out=PT[pb][:, half * 256:(half + 1) * 256], in_=ps_s[sb][half][:, :256], func=AF.Exp, scale=HD ** -0.5),
                              reads=[r_ps_s[sb][half]], writes=[r_PT[pb]])
                    if kb != nb:
                        sgn = 1 if kb < nb else -1
                        cx.op("pool", lambda e: e.affine_select(out=PT[pb][:, :].rearrange("p (a q) -> p a q", a=4),
                                                                in_=PT[pb][:, :].rearrange("p (a q) -> p a q", a=4),
                                                                pattern=[[0, 4], [-sgn, 128]], compare_op=ALU.is_ge, fill=cx.fill0,
                                                                base=0, channel_multiplier=sgn),
                              reads=[r_PT[pb]], writes=[r_PT[pb]])
                    pts.append((kb, pb))
                for i, (kb, pb) in enumerate(pts):
                    cx.op("pe", lambda e: e.matmul(ps_o[:, :], vv[:, kb % 4, j, :, :], PT[pb][:, :], start=(i == 0), stop=(i == len(pts) - 1)),
                          reads=[r_vv[kb % 4], r_PT[pb]], writes=[r_ps_o], signal=(i == len(pts) - 1))
                for i, (kb, pb) in enumerate(pts):
                    cx.op("pe", lambda e: e.matmul(ps_d[:, :], onesb[:], PT[pb][:, :], start=(i == 0), stop=(i == len(pts) - 1)),
                          reads=[r_onesb, r_PT[pb]], writes=[r_ps_d], signal=(i == len(pts) - 1))
                heads = (4 * j, 4 * j + 2, 4 * j + 1, 4 * j + 3)
                for a, h in enumerate(heads):
                    cx.op("dve", lambda e: e.tensor_scalar(out=rd[:, a * 128:(a + 1) * 128], in0=ps_d[:, a * 128:(a + 1) * 128],
                                                           scalar1=esk[:, h:h + 1], scalar2=None, op0=ALU.add),
                          reads=[r_ps_d, r_esk], writes=[r_rd])
                cx.op("act", lambda e: e.activation(out=rd[:, :], in_=rd[:, :], func=AF.Ln), reads=[r_rd], writes=[r_rd])
                cx.op("act", lambda e: e.activation(out=rd[:, :], in_=rd[:, :], func=AF.Exp, scale=-1.0), reads=[r_rd], writes=[r_rd])
                for half in range(2):
                    rows = slice(half * 64, half * 64 + 64)
                    cx.op("dve", lambda e: e.tensor_tensor(out=oT[rows, 2 * j:2 * j + 2, qs],
                                                           in0=ps_o[rows, half * 256:(half + 1) * 256].rearrange("p (a q) -> p a q", a=2),
                                                           in1=rd[rows, half * 256:(half + 1) * 256].rearrange("p (a q) -> p a q", a=2), op=ALU.mult),
                          reads=[r_ps_o, r_rd], writes=[r_oT[2 * j], r_oT[2 * j + 1]])
        for m in range(KC):
            b = pp[0] % 2; pp[0] += 1
            for c in range(KC):
                cx.op("pe", lambda e, c=c: e.matmul(ps_p[b][:, :], wo[:, c, m * 128:(m + 1) * 128], oT[:, c, :],
                                                    start=(c == 0), stop=(c == KC - 1)),
                      reads=[r_wo, r_oT[c]], writes=[r_ps_p[b]], signal=(c == KC - 1))
            cx.op("dve", lambda e: e.tensor_tensor(out=xw[:, m, :], in0=ps_p[b][:, :], in1=xw[:, m, :], op=ALU.add),
                  reads=[r_ps_p[b], r_xw], writes=[r_xw])
        cx.dma("sp", x_out_v[:, :, PAD + t0:PAD + t0 + 512], xw[:, :, :], reads=[r_xw], writes=[w["r_xout"]], slot=r_xw)
    cx.barrier()
    cx.es = old_es
    es.close()


DI, NHS, NG, DS = 2048, 32, 8, 128
CONVD = DI + 2 * NG * DS
SIN = DI + CONVD + 2 * NHS


def ssd_layer(cx, cm, T, x_in, x_out, w, scr):
    NCH = T // 128
    x_in_v = x_in.rearrange("(k p) t -> p k t", p=128)
    x_out_v = x_out.rearrange("(k p) t -> p k t", p=128)
    es = ExitStack(); old_es = cx.es; cx.es = es
    win = cx.sbuf("win", [128, KC, SIN], BF16); r_win = Res("win")
    cwb = cx.sbuf("cwb", [128, 32, 6], F32); r_cwb = Res("cwb")
    gn = cx.sbuf("gn", [128, KC], F32); r_gn = Res("gn")
    dtp = cx.sbuf("dtp", [64, 2], F32); r_dtp = Res("dtp")
    idb = cx.sbuf("idb", [128, 128], BF16); r_idb = Res("idb")
    idf = cx.sbuf("idf", [128, 128], F32); r_idf = Res("idf")
    xw = cx.sbuf("xw", [128, KC, 512], F32); r_xw = Res("xw")
    hn = cx.sbuf("hn", [128, KC, 512], BF16); r_hn = [Res(f"hn{k}") for k in range(KC)]
    sq = [cx.sbuf(f"sq{i}", [128, 512], F32) for i in range(2)]; r_sq = [Res("sq0"), Res("sq1")]
    rstd = cx.sbuf("rstd", [128, 512], F32); r_rstd = Res("rstd")
    tcv = [cx.sbuf(f"tcv{i}", [128, 384], F32) for i in range(2)]; r_tcv = [Res(f"tcv{i}") for i in range(2)]
    xbc = [cx.sbuf(f"xbc{i}", [128, 384], BF16) for i in range(2)]; r_xbc = [Res(f"xbc{i}") for i in range(2)]
    tok = cx.sbuf("tok", [128, 3, DI + NG * DS], BF16); r_tok = Res("tok")
    szt = cx.sbuf("szt", [128, DI], BF16); r_szt = Res("szt")
    aT = cx.sbuf("aT", [64, 384], F32); r_aT = Res("aT")
    dT = cx.sbuf("dT", [64, 384], F32); r_dT = Res("dT")
    acT = cx.sbuf("acT", [64, 384], F32); r_acT = Res("acT")
    onesf = cx.sbuf("onesf", [64, 384], F32); r_onesf = Res("onesf")
    tk2 = cx.sbuf("tk2", [128, 2, 64], F32); r_tk2 = Res("tk2")
    tot1 = cx.sbuf("tot1", [64, 1], F32); r_tot1 = Res("tot1")
    ps_st = cx.psum("ps_st", [128, 512], F32); r_ps_st = Res("ps_st", excl=True)
    ps_c = [cx.psum(f"ps_c{i}", [128, 512], F32) for i in range(2)]; r_ps_c = [Res(f"ps_c{i}", excl=True) for i in range(2)]
    ps_t = [cx.psum(f"ps_t{i}", [128, 1024], BF16) for i in range(2)]; r_ps_t = [Res(f"ps_t{i}", excl=True) for i in range(2)]
    ps_z = [cx.psum(f"ps_z{i}", [128, 512], F32) for i in range(2)]; r_ps_z = [Res(f"ps_z{i}", excl=True) for i in range(2)]
    ps_f = cx.psum("ps_f", [128, 512], F32); r_ps_f = Res("ps_f", excl=True)

    for dst, r, key in ((cwb, r_cwb, "cwb"), (gn, r_gn, "gn"), (dtp, r_dtp, "dtp"), (idf, r_idf, "ident")):
        cx.dma("sp", dst[:], w[key], writes=[r], slot=r)
    load_weight_bf16(cx, win, r_win, w["win"], KC)
    cx.op("dve", lambda e: e.tensor_copy(idb[:], idf[:]), reads=[r_idf], writes=[r_idb])
    cx.op("pool", lambda e: e.memset(onesf[:], 1.0), writes=[r_onesf])
    cx.op("act", lambda e: e.activation(out=dtp[:, 1:2], in_=dtp[:, 1:2], func=AF.Exp), reads=[r_dtp], writes=[r_dtp])
    cx.op("dve", lambda e: e.tensor_scalar(out=dtp[:, 1:2], in0=dtp[:, 1:2], scalar1=-1.0, scalar2=None, op0=ALU.mult),
          reads=[r_dtp], writes=[r_dtp])
    cc = [0]
    for t0 in range(0, T, 384):
        n_out = min(384, T - t0); W = n_out + 4; nblk = n_out // 128
        cx.dma("sp", xw[:, :, :W], x_in_v[:, :, PAD + t0 - 2:PAD + t0 - 2 + W], reads=[w["r_xin"]], writes=[r_xw], slot=r_xw)
        rmsnorm_fm(cx, cm, xw, r_xw, W, lambda k: gn[:, k:k + 1], r_gn, hn, r_hn, ps_st, r_ps_st, sq, r_sq, rstd, r_rstd)
        for c in range(32):
            b = cc[0] % 2; cc[0] += 1
            col0 = DI + c * 128
            for k in range(KC):
                cx.op("pe", lambda e, k=k: e.matmul(ps_c[b][:, :W], win[:, k, col0:col0 + 128], hn[:, k, :W], start=(k == 0), stop=(k == KC - 1)),
                      reads=[r_win, r_hn[k]], writes=[r_ps_c[b]], signal=(k == KC - 1))
            cx.op("act", lambda e: e.activation(out=tcv[b][:, :n_out], in_=ps_c[b][:, 2:2 + n_out], func=AF.Identity,
                                                bias=cwb[:, c, 5:6], scale=cwb[:, c, 2:3]),
                  reads=[r_ps_c[b], r_cwb], writes=[r_tcv[b]])
            for kk in (0, 1, 3, 4):
                cx.op("dve", lambda e, kk=kk: e.scalar_tensor_tensor(out=tcv[b][:, :n_out], in0=ps_c[b][:, kk:kk + n_out], scalar=cwb[:, c, kk:kk + 1],
                                                                     in1=tcv[b][:, :n_out], op0=ALU.mult, op1=ALU.add),
                      reads=[r_ps_c[b], r_cwb, r_tcv[b]], writes=[r_tcv[b]])
            cx.op("act", lambda e: e.activation(out=xbc[b][:, :n_out], in_=tcv[b][:, :n_out], func=AF.Silu), reads=[r_tcv[b]], writes=[r_xbc[b]])
            if c < 24:
                for bl in range(nblk):
                    cx.op("pe", lambda e, bl=bl: e.transpose(ps_t[b][:, bl * 128:(bl + 1) * 128], xbc[b][:, bl * 128:(bl + 1) * 128], idb[:]),
                          reads=[r_xbc[b], r_idb], writes=[r_ps_t[b]])
                cx.op("dve", lambda e: e.tensor_copy(tok[:, :nblk, c * 128:(c + 1) * 128], ps_t[b][:, :nblk * 128].rearrange("p (a q) -> p a q", a=nblk)),
                      reads=[r_ps_t[b]], writes=[r_tok])
            if c >= 16:
                dst = scr["BT"] if c < 24 else scr["CT"]
                g = (c - 16) % 8
                cx.dma("sp", dst[g * 128:(g + 1) * 128, t0:t0 + n_out], xbc[b][:, :n_out], reads=[r_xbc[b]], writes=[scr["r_BC"]], slot=r_xbc[b])
        for bl in range(nblk):
            r0 = t0 + bl * 128
            cx.dma("sp", scr["xtok"][r0:r0 + 128, :], tok[:, bl, :DI], reads=[r_tok], writes=[scr["r_tok"]], slot=r_tok)
            cx.dma("sp", scr["Btok"][r0:r0 + 128, :], tok[:, bl, DI:], reads=[r_tok], writes=[scr["r_tok"]], slot=r_tok)
        for bl in range(nblk):
            r0 = t0 + bl * 128
            for ct in range(4):
                b = cc[0] % 2; cc[0] += 1
                for k in range(KC):
                    cx.op("pe", lambda e, k=k: e.matmul(ps_z[b][:, :], hn[:, k, 2 + bl * 128:2 + (bl + 1) * 128], win[:, k, ct * 512:(ct + 1) * 512],
                                                        start=(k == 0), stop=(k == KC - 1)),
                          reads=[r_win, r_hn[k]], writes=[r_ps_z[b]], signal=(k == KC - 1))
                cx.op("act", lambda e: e.activation(out=szt[:, ct * 512:(ct + 1) * 512], in_=ps_z[b][:, :], func=AF.Silu),
                      reads=[r_ps_z[b]], writes=[r_szt])
            cx.dma("sp", scr["sz"][r0:r0 + 128, :], szt[:, :], reads=[r_szt], writes=[scr["r_sz"]], slot=r_szt)
        for k in range(KC):
            cx.op("pe", lambda e, k=k: e.matmul(ps_f[:64, :n_out], win[:, k, DI + CONVD:SIN], hn[:, k, 2:2 + n_out], start=(k == 0), stop=(k == KC - 1)),
                  reads=[r_win, r_hn[k]], writes=[r_ps_f], signal=(k == KC - 1))
        cx.op("act", lambda e: e.activation(out=dT[:, :n_out], in_=ps_f[:64, :n_out], func=AF.Exp, bias=dtp[:, 0:1]),
              reads=[r_ps_f, r_dtp], writes=[r_dT])
        cx.op("act", lambda e: e.activation(out=dT[:, :n_out], in_=dT[:, :n_out], func=AF.Ln, bias=cm.one[:64, 0:1]),
              reads=[r_dT, cm.r_one], writes=[r_dT])
        cx.op("dve", lambda e: e.tensor_scalar(out=aT[:, :n_out], in0=dT[:, :n_out], scalar1=dtp[:, 1:2], scalar2=None, op0=ALU.mult),
              reads=[r_dT, r_dtp], writes=[r_aT])
        for bl in range(nblk):
            cs_ = slice(bl * 128, (bl + 1) * 128)
            cx.op("dve", lambda e: e.tensor_tensor_scan(out=acT[:, cs_], data0=onesf[:, cs_], data1=aT[:, cs_], initial=0.0, op0=ALU.mult, op1=ALU.add),
                  reads=[r_onesf, r_aT], writes=[r_acT])
            cx.op("dve", lambda e: e.tensor_copy(tot1[32:64, :], acT[32:64, bl * 128 + 127:bl * 128 + 128]), reads=[r_acT], writes=[r_tot1])
            cx.op("dve", lambda e: e.scalar_tensor_tensor(out=acT[32:64, cs_], in0=acT[32:64, cs_], scalar=-1.0, in1=aT[32:64, cs_], op0=ALU.mult, op1=ALU.add),
                  reads=[r_acT, r_aT], writes=[r_acT])
            cx.op("dve", lambda e: e.tensor_scalar(out=acT[32:64, cs_], in0=acT[32:64, cs_], scalar1=tot1[32:64, 0:1], scalar2=None, op0=ALU.add),
                  reads=[r_acT, r_tot1], writes=[r_acT])
            r0 = t0 + bl * 128
            for i, src in enumerate((acT, dT)):
                cx.op("pe", lambda e, src=src: e.transpose(ps_f[:, 128 + i * 64:128 + (i + 1) * 64], src[:, cs_], idf[:64, :64]),
                      reads=[r_acT, r_dT, r_idf], writes=[r_ps_f])
            cx.op("act", lambda e: e.copy(out=tk2[:, :, :], in_=ps_f[:, 128:256].rearrange("p (a q) -> p a q", a=2)), reads=[r_ps_f], writes=[r_tk2])
            cx.dma("sp", scr["actok"][r0:r0 + 128, :], tk2[:, 0, :], reads=[r_tk2], writes=[scr["r_ac"]], slot=r_tk2)
            cx.dma("sp", scr["dttok"][r0:r0 + 128, :], tk2[:, 1, :], reads=[r_tk2], writes=[scr["r_ac"]], slot=r_tk2)
        cx.dma("sp", scr["acT"][:, t0:t0 + n_out], acT[:, :n_out], reads=[r_acT], writes=[scr["r_ac"]], slot=r_acT)
    cx.barrier()
    cx.es = old_es
    es.close()
    ssd_phase_b(cx, cm, T, x_in_v, x_out_v, w, scr)


def ssd_phase_b(cx, cm, T, x_in_v, x_out_v, w, scr):
    NCH = T // 128
    es = ExitStack(); old_es = cx.es; cx.es = es
    wout = cx.sbuf("wout", [128, 16, D], BF16); r_wout = Res("wout")
    dbc = cx.sbuf("dbc", [128, DI], F32); r_dbc = Res("dbc")
    gbc = cx.sbuf("gbc", [128, DI], F32); r_gbc = Res("gbc")
    idb = cx.sbuf("idb", [128, 128], BF16); r_idb = Res("idb")
    idf = cx.sbuf("idf", [128, 128], F32); r_idf = Res("idf")
    def two(name, shape, dt):
        return [cx.sbuf(f"{name}{i}", shape, dt) for i in range(2)], [Res(f"{name}{i}") for i in range(2)]
    xts, r_xts = two("xt", [128, DI], BF16)
    bts, r_bts = two("bt", [128, NG * DS], BF16)
    BTss, r_BTss = two("BTs", [128, NG, 128], BF16)
    CTss, r_CTss = two("CTs", [128, NG, 128], BF16)
    dtts, r_dtts = two("dtt", [128, 64], F32)
    acts, r_acts = two("act", [128, 64], F32)
    totbs, r_totbs = two("totb", [128, 32], F32)
    rowbs, r_rowbs = two("rowb", [128, 32, 128], F32)
    sm = cx.sbuf("sm", [128, 4, 32], F32); r_sm = Res("sm")
    cbt = cx.sbuf("cbt", [128, NG, 128], F32); r_cbt = [Res(f"cbt{g}") for g in range(NG)]
    dif = [cx.sbuf(f"dif{i}", [128, 4, 128], F32) for i in range(2)]; r_dif = [Res(f"dif{i}") for i in range(2)]
    MT = [cx.sbuf(f"MT{i}", [128, 4, 128], BF16) for i in range(2)]; r_MT = [Res(f"MT{i}") for i in range(2)]
    Bw = [cx.sbuf(f"Bw{i}", [128, 4, 128], BF16) for i in range(2)]; r_Bw = [Res(f"Bw{i}") for i in range(2)]
    HT = cx.sbuf("HT", [128, NHS, 64], F32); r_HT = [Res(f"HT{g}") for g in range(NG)]
    HTb = cx.sbuf("HTb", [128, NHS, 64], BF16); r_HTb = [Res(f"HTb{g}") for g in range(NG)]
    xdt = cx.sbuf("xdt", [128, DI], BF16); r_xdt = Res("xdt")
    yof = cx.sbuf("yof", [128, 512], F32); r_yof = Res("yof")
    dsk = cx.sbuf("dsk", [128, DI], F32); r_dsk = Res("dsk")
    ysb = cx.sbuf("ysb", [128, DI], F32); r_ysb = [Res(f"ysb{q}") for q in range(4)]
    y0 = cx.sbuf("y0", [128, DI], F32); r_y0 = Res("y0")
    szt = cx.sbuf("szt", [128, DI], BF16); r_szt = Res("szt")
    gss = cx.sbuf("gss", [128, NG], F32); r_gss = Res("gss")
    yb = cx.sbuf("yb", [128, DI], BF16); r_yb = Res("yb")
    yT = cx.sbuf("yT", [128, 16, 128], BF16); r_yT = Res("yT")
    xw = cx.sbuf("xw", [128, KC, 128], F32); r_xw = Res("xw")
    ps_cb = cx.psum("ps_cb", [128, 512], F32); r_ps_cb = Res("ps_cb", excl=True)
    ps_y = [cx.psum(f"ps_y{i}", [128, 512], F32) for i in range(2)]; r_ps_y = [Res(f"ps_y{i}", excl=True) for i in range(2)]
    ps_o = [cx.psum(f"ps_of{i}", [128, 512], F32) for i in range(2)]; r_ps_o = [Res(f"ps_of{i}", excl=True) for i in range(2)]
    ps_s = [cx.psum(f"ps_st{i}", [128, 512], F32) for i in range(2)]; r_ps_s = [Res(f"ps_st{i}", excl=True) for i in range(2)]
    ps_t = cx.psum("ps_tr", [128, 1024], BF16); r_ps_t = Res("ps_tr", excl=True)

    for dst, r, key in ((dbc, r_dbc, "dbc"), (gbc, r_gbc, "gbc"), (idf, r_idf, "ident")):
        cx.dma("sp", dst[:], w[key], writes=[r], slot=r)
    load_weight_bf16(cx, wout, r_wout, w["wout"], 16)
    cx.op("dve", lambda e: e.tensor_copy(idb[:], idf[:]), reads=[r_idf], writes=[r_idb])
    BTv = scr["BT"].rearrange("(g p) t -> p g t", p=128)
    CTv = scr["CT"].rearrange("(g p) t -> p g t", p=128)
    hc = [0]
    iters = [(0, c) for c in range(NCH)] + [(1, c) for c in range(NCH - 1, -1, -1)]

    def load_inputs(i):
        d, c = iters[i]; p = i % 2
        r0 = c * 128
        dc = slice(d * 32, d * 32 + 32)
        cx.dma("sp", xts[p][:], scr["xtok"][r0:r0 + 128, :], reads=[scr["r_tok"]], writes=[r_xts[p]], slot=r_xts[p])
        cx.dma("sp", bts[p][:], scr["Btok"][r0:r0 + 128, :], reads=[scr["r_tok"]], writes=[r_bts[p]], slot=r_bts[p])
        cx.dma("sp", BTss[p][:], BTv[:, :, r0:r0 + 128], reads=[scr["r_BC"]], writes=[r_BTss[p]], slot=r_BTss[p])
        cx.dma("sp", CTss[p][:], CTv[:, :, r0:r0 + 128], reads=[scr["r_BC"]], writes=[r_CTss[p]], slot=r_CTss[p])
        cx.dma("sp", dtts[p][:], scr["dttok"][r0:r0 + 128, :], reads=[scr["r_ac"]], writes=[r_dtts[p]], slot=r_dtts[p])
        cx.dma("sp", acts[p][:], scr["actok"][r0:r0 + 128, :], reads=[scr["r_ac"]], writes=[r_acts[p]], slot=r_acts[p])
        rl = r0 + 127 if d == 0 else r0
        cx.dma("sp", totbs[p][:], scr["actok"][rl:rl + 1, dc].partition_broadcast(128), reads=[scr["r_ac"]], writes=[r_totbs[p]], slot=r_totbs[p])
        cx.dma("sp", rowbs[p][:], scr["acT"][d * 32:d * 32 + 32, r0:r0 + 128].partition_broadcast(128), reads=[scr["r_ac"]], writes=[r_rowbs[p]], slot=r_rowbs[p])

    load_inputs(0)
    for it, (d, c) in enumerate(iters):
        if c == (0 if d == 0 else NCH - 1):
            cx.op("pool", lambda e: e.memset(HT[:, :, :], 0.0), writes=r_HT)
            cx.op("pool", lambda e: e.memset(HTb[:, :, :], 0.0), writes=r_HTb)
        dc = slice(d * 32, d * 32 + 32)
        if True:
            r0 = c * 128
            p = it % 2
            xt, r_xt, bt, r_bt, BTs, r_BTs, CTs, r_CTs = xts[p], r_xts[p], bts[p], r_bts[p], BTss[p], r_BTss[p], CTss[p], r_CTss[p]
            dtt, r_dtt, act, r_act, totb, r_totb, rowb, r_rowb = dtts[p], r_dtts[p], acts[p], r_acts[p], totbs[p], r_totbs[p], rowbs[p], r_rowbs[p]
            if it + 1 < len(iters):
                load_inputs(it + 1)
            if d == 1:
                cx.dma("sp", y0[:], scr["y0"][r0:r0 + 128, :], reads=[scr["r_y0"]], writes=[r_y0], slot=r_y0)
                cx.dma("sp", szt[:], scr["sz"][r0:r0 + 128, :], reads=[scr["r_sz"]], writes=[r_szt], slot=r_szt)
                cx.dma("sp", xw[:], x_in_v[:, :, PAD + r0:PAD + r0 + 128], reads=[w["r_xin"]], writes=[r_xw], slot=r_xw)
                cx.op("dve", lambda e: e.tensor_tensor(out=dsk[:, :], in0=xt[:, :], in1=dbc[:, :], op=ALU.mult), reads=[r_xt, r_dbc], writes=[r_dsk])
                cx.op("pool", lambda e: e.tensor_tensor(out=y0[:, :], in0=y0[:, :], in1=dsk[:, :], op=ALU.add), reads=[r_y0, r_dsk], writes=[r_y0])
            cx.op("dve", lambda e: e.tensor_scalar(out=sm[:, 0, :], in0=act[:, dc], scalar1=-1.0, scalar2=None, op0=ALU.mult), reads=[r_act], writes=[r_sm])
            cx.op("dve", lambda e: e.tensor_tensor(out=sm[:, 1, :], in0=totb[:, :], in1=act[:, dc], op=ALU.subtract), reads=[r_totb, r_act, r_sm], writes=[r_sm])
            cx.op("act", lambda e: e.activation(out=sm[:, 1, :], in_=sm[:, 1, :], func=AF.Exp), reads=[r_sm], writes=[r_sm])
            cx.op("act", lambda e: e.activation(out=sm[:, 2, :], in_=totb[:, :], func=AF.Exp), reads=[r_totb, r_sm], writes=[r_sm])
            cx.op("act", lambda e: e.activation(out=sm[:, 3, :], in_=act[:, dc], func=AF.Exp), reads=[r_act, r_sm], writes=[r_sm])
            for g in range(NG):
                cx.op("pe", lambda e: e.matmul(ps_cb[:, :128], BTs[:, g, :], CTs[:, g, :], start=True, stop=True),
                      reads=[r_BTs, r_CTs], writes=[r_ps_cb])
                cx.op("act", lambda e: e.copy(out=cbt[:, g, :], in_=ps_cb[:, :128]), reads=[r_ps_cb], writes=[r_cbt[g]])
            cx.op("dve", lambda e: e.tensor_tensor(out=xdt[:, :].rearrange("p (h q) -> p h q", h=NHS), in0=xt[:, :].rearrange("p (h q) -> p h q", h=NHS),
                                                   in1=dtt[:, dc].unsqueeze(2).to_broadcast([128, NHS, 64]), op=ALU.mult),
                  reads=[r_xt, r_dtt], writes=[r_xdt])
            sgn = 1 if d == 0 else -1
            for q in range(4):
                yb_, ob_ = ps_y[q % 2], ps_o[q % 2]
                r_yb_, r_ob_ = r_ps_y[q % 2], r_ps_o[q % 2]
                for gg in range(2):
                    g = q * 2 + gg; i2 = hc[0] % 2; hc[0] += 1
                    hs = slice(4 * g, 4 * g + 4)
                    B4 = [128, 4, 128]
                    cx.op("dve", lambda e: e.tensor_tensor(out=dif[i2][:, :, :], in0=rowb[:, hs, :], in1=sm[:, 0, hs].unsqueeze(2).to_broadcast(B4), op=ALU.add),
                          reads=[r_rowb, r_sm], writes=[r_dif[i2]])
                    cx.op("pool", lambda e: e.affine_select(out=dif[i2][:, :, :], in_=dif[i2][:, :, :], pattern=[[0, 4], [sgn, 128]], compare_op=ALU.is_ge,
                                                            fill=cx.fillneg, base=0, channel_multiplier=-sgn),
                          reads=[r_dif[i2]], writes=[r_dif[i2]])
                    cx.op("act", lambda e: e.activation(out=dif[i2][:, :, :], in_=dif[i2][:, :, :], func=AF.Exp), reads=[r_dif[i2]], writes=[r_dif[i2]])
                    cx.op("dve", lambda e: e.tensor_tensor(out=MT[i2][:, :, :], in0=dif[i2][:, :, :], in1=cbt[:, g, :].unsqueeze(1).to_broadcast(B4), op=ALU.mult),
                          reads=[r_dif[i2], r_cbt[g]], writes=[r_MT[i2]])
                    cx.op("dve", lambda e: e.tensor_tensor(out=Bw[i2][:, :, :], in0=bt[:, g * 128:(g + 1) * 128].unsqueeze(1).to_broadcast(B4),
                                                           in1=sm[:, 1, hs].unsqueeze(2).to_broadcast(B4), op=ALU.mult),
                          reads=[r_bt, r_sm], writes=[r_Bw[i2]])
                    sb = ps_s[i2]; r_sb = r_ps_s[i2]
                    for hq in range(4):
                        h = 4 * g + hq; hh = gg * 4 + hq
                        cx.op("pe", lambda e: e.matmul(yb_[:, hh * 64:(hh + 1) * 64], MT[i2][:, hq, :], xdt[:, h * 64:(h + 1) * 64], start=True, stop=True),
                              reads=[r_MT[i2], r_xdt], writes=[r_yb_])
                        cx.op("pe", lambda e: e.matmul(ob_[:, hh * 64:(hh + 1) * 64], CTs[:, g, :], HTb[:, h, :], start=True, stop=True),
                              reads=[r_CTs, r_HTb[g]], writes=[r_ob_])
                        cx.op("pe", lambda e: e.matmul(sb[:, hq * 64:(hq + 1) * 64], Bw[i2][:, hq, :], xdt[:, h * 64:(h + 1) * 64], start=True, stop=True),
                              reads=[r_Bw[i2], r_xdt], writes=[r_sb])
                    cx.op("dve", lambda e: e.tensor_tensor(out=HT[:, hs, :], in0=HT[:, hs, :], in1=sm[:, 2, hs].unsqueeze(2).to_broadcast([128, 4, 64]), op=ALU.mult),
                          reads=[r_HT[g], r_sm], writes=[r_HT[g]])
                    cx.op("dve", lambda e: e.tensor_tensor(out=HT[:, hs, :], in0=sb[:, :256].rearrange("p (h q) -> p h q", h=4), in1=HT[:, hs, :], op=ALU.add),
                          reads=[r_HT[g], r_sb], writes=[r_HT[g]])
                    cx.op("pool", lambda e: e.tensor_copy(HTb[:, hs, :], HT[:, hs, :]), reads=[r_HT[g]], writes=[r_HTb[g]])
                qs = slice(q * 512, (q + 1) * 512)
                h8 = slice(q * 8, q * 8 + 8)
                cx.op("act", lambda e: e.copy(out=ysb[:, qs], in_=yb_[:, :]), reads=[r_yb_], writes=[r_ysb[q]])
                cx.op("dve", lambda e: e.tensor_tensor(out=yof[:, :].rearrange("p (h q) -> p h q", h=8), in0=ob_[:, :].rearrange("p (h q) -> p h q", h=8),
                                                       in1=sm[:, 3, h8].unsqueeze(2).to_broadcast([128, 8, 64]), op=ALU.mult),
                      reads=[r_ob_, r_sm], writes=[r_yof])
                cx.op("pool", lambda e: e.tensor_tensor(out=ysb[:, qs], in0=ysb[:, qs], in1=yof[:, :], op=ALU.add), reads=[r_yof, r_ysb[q]], writes=[r_ysb[q]])
            if d == 0:
                cx.dma("sp", scr["y0"][r0:r0 + 128, :], ysb[:, :], reads=r_ysb, writes=[scr["r_y0"]], slot=r_ysb[0])
                continue
            cx.op("pool", lambda e: e.tensor_tensor(out=y0[:, :], in0=y0[:, :], in1=ysb[:, :], op=ALU.add), reads=[r_y0] + r_ysb, writes=[r_y0])
            cx.op("dve", lambda e: e.tensor_tensor(out=y0[:, :], in0=y0[:, :], in1=szt[:, :], op=ALU.mult), reads=[r_y0, r_szt], writes=[r_y0])
            cx.op("pool", lambda e: e.tensor_tensor(out=ysb[:, :], in0=y0[:, :], in1=y0[:, :], op=ALU.mult), reads=[r_y0] + r_ysb, writes=r_ysb)
            cx.op("dve", lambda e: e.tensor_reduce(out=gss[:, :], in_=ysb[:, :].rearrange("p (g f) -> p g f", g=NG), axis=mybir.AxisListType.X, op=ALU.add),
                  reads=r_ysb, writes=[r_gss])
            cx.op("act", lambda e: e.activation(out=gss[:, :], in_=gss[:, :], func=AF.Ln, bias=cm.eps[:, 0:1], scale=1.0 / 256), reads=[r_gss, cm.r_eps], writes=[r_gss])
            cx.op("act", lambda e: e.activation(out=gss[:, :], in_=gss[:, :], func=AF.Exp, scale=-0.5), reads=[r_gss], writes=[r_gss])
            for g in range(NG):
                gs = slice(g * 256, (g + 1) * 256)
                cx.op("dve", lambda e: e.scalar_tensor_tensor(out=yb[:, gs], in0=y0[:, gs], scalar=gss[:, g:g + 1], in1=gbc[:, gs], op0=ALU.mult, op1=ALU.mult),
                      reads=[r_y0, r_gss, r_gbc], writes=[r_yb])
            for half in range(2):
                for cq in range(8):
                    cch = half * 8 + cq
                    cx.op("pe", lambda e: e.transpose(ps_t[:, cq * 128:(cq + 1) * 128], yb[:, cch * 128:(cch + 1) * 128], idb[:]),
                          reads=[r_yb, r_idb], writes=[r_ps_t])
                cx.op("act", lambda e: e.copy(out=yT[:, half * 8:(half + 1) * 8, :], in_=ps_t[:, :].rearrange("p (a q) -> p a q", a=8)),
                      reads=[r_ps_t], writes=[r_yT])
            for m in range(KC):
                pb = ps_y[m % 2]; r_pb = r_ps_y[m % 2]
                for cch in range(16):
                    cx.op("pe", lambda e, cch=cch: e.matmul(pb[:, :128], wout[:, cch, m * 128:(m + 1) * 128], yT[:, cch, :], start=(cch == 0), stop=(cch == 15)),
                          reads=[r_wout, r_yT], writes=[r_pb], signal=(cch == 15))
                cx.op("dve", lambda e: e.tensor_tensor(out=xw[:, m, :], in0=pb[:, :128], in1=xw[:, m, :], op=ALU.add), reads=[r_pb, r_xw], writes=[r_xw])
            cx.dma("sp", x_out_v[:, :, PAD + r0:PAD + r0 + 128], xw[:, :, :], reads=[r_xw], writes=[w["r_xout"]], slot=r_xw)
    cx.barrier()
    cx.es = old_es
    es.close()


def make_ssd_scratch(nc, T):
    d = lambda n, s, dt: nc.dram_tensor(n, s, dt, kind="Internal").ap()
    return dict(xtok=d("s_xtok", [T, DI], BF16), Btok=d("s_btok", [T, NG * DS], BF16), BT=d("s_BT", [NG * DS, T], BF16),
                CT=d("s_CT", [NG * DS, T], BF16), sz=d("s_sz", [T, DI], BF16), actok=d("s_actok", [T, 64], F32),
                dttok=d("s_dttok", [T, 64], F32), acT=d("s_acT", [64, T], F32), y0=d("s_y0", [T, DI], F32),
                r_tok=Res("s_tok"), r_BC=Res("s_BC"), r_sz=Res("s_sz"), r_ac=Res("s_ac"), r_y0=Res("s_y0"))


DEPTH = 4
ROT = 16


def build_program(T):
    TP = T + 2 * PAD
    nc = bass.Bass("TRN2", target_bir_lowering=False)
    di = lambda n, s: nc.dram_tensor(n, list(s), F32, kind="ExternalInput").ap()
    xin = di("xin", [D, TP])
    xout = nc.dram_tensor("xout", [D, TP], F32, kind="ExternalOutput").ap()
    xmid = nc.dram_tensor("xmid", [D, TP], F32, kind="Internal").ap()
    bm = di("bm", [128, 128]); rm = di("rm", [128, 128]); ident = di("ident", [128, 128])
    cos = di("cos", [128, T]); sin = di("sin", [128, T])
    A = {}
    for j in range(2):
        A[j] = dict(wqkv=di(f"a{j}_wqkv", [D, WQKV]), wo=di(f"a{j}_wo", [D, D]), gn=di(f"a{j}_gn", [128, KC]), gqk=di(f"a{j}_gqk", [128, 2]),
                    sink=di(f"a{j}_sink", [128, NH]))
    S = {}
    for j in range(2):
        S[j] = dict(win=di(f"s{j}_win", [D, SIN]), wout=di(f"s{j}_wout", [DI, D]), gn=di(f"s{j}_gn", [128, KC]), cwb=di(f"s{j}_cwb", [128, 32, 6]),
                    dtp=di(f"s{j}_dtp", [64, 2]), dbc=di(f"s{j}_dbc", [128, DI]), gbc=di(f"s{j}_gbc", [128, DI]))
    Fw = {}
    for i in range(DEPTH):
        Fw[i] = dict(wup=di(f"f{i}_wup", [D, 2 * DFF]), wdn=di(f"f{i}_wdn", [DFF, D]), cwb=di(f"f{i}_cwb", [128, NFC, 4]), gn=di(f"f{i}_gn", [128, KC]))
    scr = make_ssd_scratch(nc, T)
    with ExitStack() as es:
        cx = Ctx(nc, es)
        cm = Common(cx)
        r = {"xin": Res("xin"), "xout": Res("xout"), "xmid": Res("xmid")}
        aps = {"xin": xin, "xout": xout, "xmid": xmid}
        zes = ExitStack(); cx.es = zes
        zt = cx.sbuf("zt", [128, KC, PAD], F32); r_zt = Res("zt")
        cx.op("pool", lambda e: e.memset(zt[:], 0.0), writes=[r_zt])
        for nm in ("xout", "xmid"):
            v = aps[nm].rearrange("(k p) t -> p k t", p=128)
            cx.dma("sp", v[:, :, 0:PAD], zt[:], reads=[r_zt], writes=[r[nm]], slot=r_zt)
            cx.dma("sp", v[:, :, PAD + T:PAD + T + PAD], zt[:], reads=[r_zt], writes=[r[nm]], slot=r_zt)
        cx.barrier()
        cx.es = es
        zes.close()
        seq = ["xin"] + ["xmid", "xout"] * DEPTH
        step = 0
        for i in range(DEPTH):
            j = i // 2
            src, dst = seq[step], seq[step + 1]; step += 1
            if i % 2 == 0:
                w = dict(wqkv=A[j]["wqkv"].rearrange("(k p) n -> p k n", p=128), wo=A[j]["wo"].rearrange("(k p) n -> p k n", p=128),
                         gn=A[j]["gn"], gqk=A[j]["gqk"], sink=A[j]["sink"], bm=bm, rm=rm, cos=cos, sin=sin, r_xin=r[src], r_xout=r[dst])
                attn_layer(cx, cm, T, aps[src], aps[dst], w)
            else:
                w = dict(win=S[j]["win"].rearrange("(k p) n -> p k n", p=128), wout=S[j]["wout"].rearrange("(k p) n -> p k n", p=128),
                         gn=S[j]["gn"], cwb=S[j]["cwb"], dtp=S[j]["dtp"], dbc=S[j]["dbc"], gbc=S[j]["gbc"], ident=ident, r_xin=r[src], r_xout=r[dst])
                ssd_layer(cx, cm, T, aps[src], aps[dst], w, scr)
            src, dst = seq[step], seq[step + 1]; step += 1
            w = dict(wup=Fw[i]["wup"].rearrange("(k p) n -> p k n", p=128), wdn=Fw[i]["wdn"].rearrange("(c p) n -> p c n", p=128),
                     cwb=Fw[i]["cwb"], gn=Fw[i]["gn"], r_xin=r[src], r_xout=r[dst])
            ffn_layer(cx, cm, T, aps[src], aps[dst], w)
        assert dst == "xout"
        cx.barrier(fresh=False)
    return nc


def host_layout(inp, T):
    f = lambda a: np.ascontiguousarray(np.asarray(a, dtype=np.float32))
    col = lambda v: f(np.asarray(v).reshape(KC, 128).T)
    m = {}
    m["bm"] = f(np.kron(np.eye(2), np.full((64, 64), 1.0 / 64)))
    rm = np.zeros((128, 128), np.float32)
    for blk in (0, 64):
        for q in range(8):
            rm[blk + q + 8, blk + q] = -1.0
            rm[blk + q, blk + q + 8] = 1.0
    m["rm"] = rm
    m["ident"] = np.eye(128, dtype=np.float32)
    pos = np.arange(T, dtype=np.float32)
    inv_freq = (np.float32(500000.0) ** (-(np.arange(0, ROT, 2, dtype=np.float32) / np.float32(ROT)))).astype(np.float32)
    ang = (pos[:, None] * inv_freq[None, :]).astype(np.float32)
    cosv, sinv = np.cos(ang).astype(np.float32), np.sin(ang).astype(np.float32)
    cosT = np.ones((128, T), np.float32); sinT = np.zeros((128, T), np.float32)
    for blk in (0, 64):
        for q in range(16):
            cosT[blk + q] = cosv[:, q % 8]; sinT[blk + q] = sinv[:, q % 8]
    m["cos"], m["sin"] = cosT, sinT
    for j in range(2):
        wq = np.asarray(inp["attn_w_qkv"][j])
        m[f"a{j}_wqkv"] = f(np.concatenate([wq[:, :QD]] + [np.tile(wq[:, QD + k * 64:QD + (k + 1) * 64], (1, 2)) for k in range(NKV)] + [wq[:, QD + 256:]], axis=1))
        m[f"a{j}_wo"] = f(inp["attn_w_o"][j])
        m[f"a{j}_gn"] = col(inp["attn_norm"][j])
        m[f"a{j}_gqk"] = f(np.stack([np.tile(np.asarray(inp["attn_q_norm"][j]), 2), np.tile(np.asarray(inp["attn_k_norm"][j]), 2)], 1))
        m[f"a{j}_sink"] = f(np.tile(np.asarray(inp["attn_sink"][j])[None, :], (128, 1)))
        m[f"s{j}_win"] = f(inp["ssd_w_in"][j])
        m[f"s{j}_wout"] = f(inp["ssd_w_out"][j])
        m[f"s{j}_gn"] = col(inp["ssd_norm"][j])
        cwb = np.zeros((128, 32, 6), np.float32)
        cwb[:, :, 0:5] = np.asarray(inp["ssd_conv_w"][j]).reshape(5, 32, 128).transpose(2, 1, 0)
        cwb[:, :, 5] = np.asarray(inp["ssd_conv_b"][j]).reshape(32, 128).T
        m[f"s{j}_cwb"] = cwb
        m[f"s{j}_dtp"] = f(np.stack([np.asarray(inp["ssd_dt_bias"][j]).reshape(64), np.asarray(inp["ssd_a_log"][j]).reshape(64)], 1))
        m[f"s{j}_dbc"] = f(np.tile(np.repeat(np.asarray(inp["ssd_d"][j]), 64)[None, :], (128, 1)))
        m[f"s{j}_gbc"] = f(np.tile(np.asarray(inp["ssd_gate_norm"][j])[None, :], (128, 1)))
    for i in range(DEPTH):
        m[f"f{i}_wup"] = f(inp["ffn_w_up"][i])
        m[f"f{i}_wdn"] = f(inp["ffn_w_down"][i])
        cwb = np.zeros((128, NFC, 4), np.float32)
        cwb[:, :, 0:3] = np.asarray(inp["ffn_conv_w"][i]).reshape(3, NFC, 128).transpose(2, 1, 0)
        cwb[:, :, 3] = np.asarray(inp["ffn_conv_b"][i]).reshape(NFC, 128).T
        m[f"f{i}_cwb"] = cwb
        m[f"f{i}_gn"] = col(inp["ffn_norm"][i])
    return m


def run_module(inp, n_cores=8):
    x = np.asarray(inp["x"], dtype=np.float32)
    B, T, _ = x.shape
    nc = build_program(T)
    shared = host_layout(inp, T)
    in_maps = []
    for c in range(n_cores):
        b = c % B
        xin = np.zeros((D, T + 2 * PAD), np.float32)
        xin[:, PAD:PAD + T] = x[b].T
        mp = dict(shared); mp["xin"] = xin
        in_maps.append(mp)
    res = run_bass_kernel_spmd(nc, in_maps, core_ids=list(range(n_cores)))
    out = np.stack([np.ascontiguousarray(res.results[b]["xout"][:, PAD:PAD + T].T) for b in range(B)], 0)
    return out.astype(np.float32)


def kernel(**inputs):
    return run_module(inputs)
```

```python
import numpy as np
from contextlib import ExitStack
import concourse.bass as bass
import concourse.mybir as mybir
from concourse.bass_utils import run_bass_kernel_spmd

F32 = mybir.dt.float32
BF16 = mybir.dt.bfloat16
AF = mybir.ActivationFunctionType
ALU = mybir.AluOpType

D = 1024
KC = D // 128
DFF = 2816
NFC = 2 * DFF // 128
NGC = DFF // 128
EPS = 1e-6
PAD = 128


class Sem:
    def __init__(self, handle, name):
        self.h = handle
        self.name = name
        self.count = 0


class Res:
    def __init__(self, name, excl=False):
        self.name = name
        self.excl = excl
        self.last_w = None
        self.readers = {}
        self.dsem = None


class Ctx:
    ENG = ("pe", "act", "dve", "pool", "sp")

    def __init__(self, nc, es):
        self.nc = nc
        self.es = es
        self.top_es = es
        self.fill0 = nc.gpsimd.to_reg(0.0)
        self.fillneg = nc.gpsimd.to_reg(-30000.0)
        self.eng = {"pe": nc.tensor, "act": nc.scalar, "dve": nc.vector, "pool": nc.gpsimd, "sp": nc.sync}
        self.sems = {}
        self.waited = {e: {} for e in self.ENG}
        self.gen = 0
        self._new_sems()
        self.uid = 0

    def _new_sems(self):
        self.gen += 1
        for e in self.ENG:
            self.sems[e] = Sem(self.es.enter_context(self.nc.semaphore(f"s_{e}_{self.gen}")), e)
        if not hasattr(self, "dfree"):
            self.dfree, self.dused, self.nd, self.dfresh = [], [], 0, []

    def name(self, base):
        self.uid += 1
        return f"{base}_{self.uid}"

    def sbuf(self, name, shape, dtype):
        t = self.es.enter_context(self.nc.sbuf_tensor(self.name(name), list(shape), dtype))
        return t

    def psum(self, name, shape, dtype):
        t = self.es.enter_context(self.nc.psum_tensor(self.name(name), list(shape), dtype))
        return t

    def _deps(self, eng, reads, writes):
        deps = []
        for r in reads:
            if r.last_w is not None:
                s, v, e = r.last_w
                if not (e == eng and eng == "pe"):
                    deps.append((s, v))
            if r.excl:
                for e, (s, v) in r.readers.items():
                    if e != eng:
                        deps.append((s, v))
        for w in writes:
            if w.last_w is not None:
                s, v, e = w.last_w
                if e != eng or eng != "pe":
                    deps.append((s, v))
            for e, (s, v) in w.readers.items():
                if e != eng or eng != "pe":
                    deps.append((s, v))
        return deps

    def _wait(self, eng, deps):
        wd = self.waited[eng]
        best = {}
        for s, v in deps:
            if wd.get(s, 0) >= v:
                continue
            if best.get(s, 0) < v:
                best[s] = v
        for s, v in best.items():
            assert v <= s.count, f"wait on un-emitted signal {s.name} {v}>{s.count} (engine {eng})"
            self.eng[eng].wait_ge(s.h, v)
            wd[s] = v

    def _stamp(self, eng, stamp, reads, writes):
        s, v = stamp
        for r in reads:
            r.readers[eng] = (s, v)
        for w in writes:
            w.last_w = (s, v, eng)
            w.readers = {}

    def op(self, eng, fn, reads=(), writes=(), signal=True):
        reads = [r for r in reads if r is not None]
        writes = [w for w in writes if w is not None]
        self._wait(eng, self._deps(eng, reads, writes))
        ins = fn(self.eng[eng])
        s = self.sems[eng]
        if signal:
            s.count += 1
            ins.then_inc(s.h, 1)
            stamp = (s, s.count)
        else:
            stamp = (s, s.count + 1)
        self._stamp(eng, stamp, reads, writes)
        return ins

    def _dsem(self, res, fresh=False):
        if res.dsem is None and fresh:
            self.nd += 1
            res.dsem = Sem(self.top_es.enter_context(self.nc.semaphore(f"s_dma_{self.nd}")), f"dma{self.nd}")
            self.dfresh.append(res.dsem)
        if res.dsem is None:
            while self.dfree and self.dfree[-1].count > 30000:
                self.dfree.pop()
            if self.dfree:
                res.dsem = self.dfree.pop()
            else:
                self.nd += 1
                res.dsem = Sem(self.top_es.enter_context(self.nc.semaphore(f"s_dma_{self.nd}")), f"dma{self.nd}")
            self.dused.append(res.dsem)
        return res.dsem

    def dma(self, queue, out, in_, reads=(), writes=(), slot=None):
        reads = [r for r in reads if r is not None]
        writes = [w for w in writes if w is not None]
        self._wait(queue, self._deps("dma", reads, writes))
        ins = self.eng[queue].dma_start(out=out, in_=in_)
        s = self._dsem(slot, fresh=(queue == "pool"))
        s.count += 16
        ins.then_inc(s.h, 16)
        self._stamp("dma", (s, s.count), reads, writes)
        return ins

    def barrier(self, fresh=True):
        for e in self.ENG:
            deps = [(s, s.count) for s in list(self.sems.values()) + self.dused + self.dfresh if s.count > 0]
            self._wait(e, deps)
        self.dfree.extend(self.dused)
        self.dused = []
        self.dfresh = []
        if fresh:
            old = dict(self.sems)
            self._new_sems()
            self._old = old


class Common:
    def __init__(self, cx):
        nc = cx.nc
        self.ones_mean = cx.sbuf("ones_mean", [128, 128], F32)
        self.r_ones = Res("ones_mean")
        cx.op("pool", lambda e: e.memset(self.ones_mean[:], 1.0 / D), writes=[self.r_ones])
        self.eps = cx.sbuf("eps", [128, 1], F32)
        self.r_eps = Res("eps")
        cx.op("pool", lambda e: e.memset(self.eps[:], EPS), writes=[self.r_eps])
        self.one = cx.sbuf("one", [128, 1], F32)
        self.r_one = Res("one")
        cx.op("pool", lambda e: e.memset(self.one[:], 1.0), writes=[self.r_one])


def rmsnorm_fm(cx, cm, xw, r_xw, W, g_ap, r_g, hn, r_hn, ps, r_ps, sq, r_sq, rstd, r_rstd):
    for k in range(KC):
        b = k % 2
        cx.op("act", lambda e, k=k, b=b: e.activation(out=sq[b][:, :W], in_=xw[:, k, :W], func=AF.Square),
              reads=[r_xw], writes=[r_sq[b]])
        cx.op("pe", lambda e, k=k, b=b: e.matmul(ps[:, :W], cm.ones_mean[:], sq[b][:, :W],
                                                    start=(k == 0), stop=(k == KC - 1)),
              reads=[cm.r_ones, r_sq[b]], writes=[r_ps], signal=True)
    cx.op("act", lambda e: e.activation(out=rstd[:, :W], in_=ps[:, :W], func=AF.Ln, bias=cm.eps[:, 0:1]),
          reads=[r_ps, cm.r_eps], writes=[r_rstd])
    cx.op("act", lambda e: e.activation(out=rstd[:, :W], in_=rstd[:, :W], func=AF.Exp, scale=-0.5),
          reads=[r_rstd], writes=[r_rstd])
    for k in range(KC):
        cx.op("dve", lambda e, k=k: e.scalar_tensor_tensor(out=hn[:, k, :W], in0=xw[:, k, :W], scalar=g_ap(k),
                                                        in1=rstd[:, :W], op0=ALU.mult, op1=ALU.mult),
              reads=[r_xw, r_rstd, r_g], writes=[r_hn[k]])


def load_weight_bf16(cx, dst, r_dst, src_ap, nk, split=1):
    for k in range(nk):
        cx.dma("pool", dst[:, k, :], src_ap[:, k, :], writes=[r_dst], slot=r_dst)


def ffn_layer(cx, cm, T, x_in, x_out, w):
    nc = cx.nc
    es = ExitStack()
    old_es = cx.es
    cx.es = es
    wup = cx.sbuf("wup", [128, KC, 2 * DFF], BF16); r_wup = Res("wup")
    wdn = cx.sbuf("wdn", [128, NGC, D], BF16); r_wdn = Res("wdn")
    cwb = cx.sbuf("cwb", [128, NFC, 4], F32); r_cwb = Res("cwb")
    gn = cx.sbuf("gn", [128, KC], F32); r_gn = Res("gn")
    xws = [cx.sbuf(f"xw{i}", [128, KC, 512], F32) for i in range(2)]; r_xws = [Res("xw0"), Res("xw1")]
    hn = cx.sbuf("hn", [128, KC, 512], BF16); r_hn = [Res(f"hn{k}") for k in range(KC)]
    gT = cx.sbuf("gT", [128, NGC, 512], BF16); r_gT = [Res(f"gT{c}") for c in range(NGC)]
    rstd = cx.sbuf("rstd", [128, 512], F32); r_rstd = Res("rstd")
    NT = 2
    t1 = [[cx.sbuf(f"t1_{i}_{j}", [128, 512], F32) for j in range(2)] for i in range(NT)]
    r_t1 = [[Res(f"t1_{i}_{j}") for j in range(2)] for i in range(NT)]
    sq = [t1[0][0], t1[0][1]]; r_sq = [r_t1[0][0], r_t1[0][1]]
    ps_st = cx.psum("ps_st", [128, 512], F32); r_ps_st = Res("ps_st", excl=True)
    ps_gv = [[cx.psum(f"ps_gv{i}{j}", [128, 512], F32) for j in range(2)] for i in range(2)]
    r_ps_gv = [[Res(f"ps_gv{i}{j}", excl=True) for j in range(2)] for i in range(2)]
    ps_o = [cx.psum(f"ps_o{i}", [128, 512], F32) for i in range(2)]
    r_ps_o = [Res(f"ps_o{i}", excl=True) for i in range(2)]

    cx.dma("sp", cwb[:], w["cwb"], writes=[r_cwb], slot=r_cwb)
    cx.dma("sp", gn[:], w["gn"], writes=[r_gn], slot=r_gn)
    load_weight_bf16(cx, wup, r_wup, w["wup"], KC)
    load_weight_bf16(cx, wdn, r_wdn, w["wdn"], NGC)

    x_in_v = x_in.rearrange("(k p) t -> p k t", p=128)
    x_out_v = x_out.rearrange("(k p) t -> p k t", p=128)
    STEP = 510
    tiles = [(t0, min(STEP, T - t0)) for t0 in range(0, T, STEP)]

    def load(i):
        t0, n_out = tiles[i]
        W = n_out + 2
        cx.dma("sp", xws[i % 2][:, :, :W], x_in_v[:, :, PAD + t0 - 1:PAD + t0 - 1 + W], reads=[w["r_xin"]], writes=[r_xws[i % 2]], slot=r_xws[i % 2])

    def norm(i):
        t0, n_out = tiles[i]
        rmsnorm_fm(cx, cm, xws[i % 2], r_xws[i % 2], n_out + 2, lambda k: gn[:, k:k + 1], r_gn, hn, r_hn, ps_st, r_ps_st, sq, r_sq, rstd, r_rstd)

    load(0)
    if len(tiles) > 1:
        load(1)
    norm(0)
    pair_i = 0
    for ti, (t0, n_out) in enumerate(tiles):
        W = n_out + 2
        xw = xws[ti % 2]; r_xw = r_xws[ti % 2]
        for cg in range(NGC):
            pb = pair_i % 2
            tb = pair_i % NT
            pair_i += 1
            for j, c in enumerate((cg, cg + NGC)):
                ps = ps_gv[pb][j]; r_ps = r_ps_gv[pb][j]
                for k in range(KC):
                    cx.op("pe", lambda e, k=k, c=c, ps=ps: e.matmul(ps[:, :W], wup[:, k, c * 128:(c + 1) * 128], hn[:, k, :W],
                                                                   start=(k == 0), stop=(k == KC - 1)),
                          reads=[r_wup, r_hn[k]], writes=[r_ps], signal=(k == KC - 1))
            for j, c in enumerate((cg, cg + NGC)):
                ps = ps_gv[pb][j]; r_ps = r_ps_gv[pb][j]
                tt = t1[tb][j]; r_tt = r_t1[tb][j]
                cx.op("act", lambda e, c=c, ps=ps, tt=tt: e.activation(out=tt[:, :n_out], in_=ps[:, 1:1 + n_out], func=AF.Identity,
                                                                        bias=cwb[:, c, 3:4], scale=cwb[:, c, 1:2]),
                      reads=[r_ps, r_cwb], writes=[r_tt])
                cx.op("dve", lambda e, c=c, ps=ps, tt=tt: e.scalar_tensor_tensor(out=tt[:, :n_out], in0=ps[:, 0:n_out], scalar=cwb[:, c, 0:1],
                                                                               in1=tt[:, :n_out], op0=ALU.mult, op1=ALU.add),
                      reads=[r_ps, r_cwb, r_tt], writes=[r_tt])
                cx.op("dve", lambda e, c=c, ps=ps, tt=tt: e.scalar_tensor_tensor(out=tt[:, :n_out], in0=ps[:, 2:2 + n_out], scalar=cwb[:, c, 2:3],
                                                                               in1=tt[:, :n_out], op0=ALU.mult, op1=ALU.add),
                      reads=[r_ps, r_cwb, r_tt], writes=[r_tt])
            tg = t1[tb][0]; tv = t1[tb][1]
            cx.op("act", lambda e, tg=tg: e.activation(out=tg[:, :n_out], in_=tg[:, :n_out], func=AF.Silu),
                  reads=[r_t1[tb][0]], writes=[r_t1[tb][0]])
            cx.op("pool", lambda e, tg=tg, tv=tv, cg=cg: e.tensor_tensor(out=gT[:, cg, :n_out], in0=tg[:, :n_out], in1=tv[:, :n_out], op=ALU.mult),
                  reads=[r_t1[tb][0], r_t1[tb][1]], writes=[r_gT[cg]])
        if ti + 1 < len(tiles):
            norm(ti + 1)
        for m in range(KC):
            ob = m % 2
            for c in range(NGC):
                cx.op("pe", lambda e, c=c, m=m, ob=ob: e.matmul(ps_o[ob][:, :n_out], wdn[:, c, m * 128:(m + 1) * 128], gT[:, c, :n_out],
                                                                 start=(c == 0), stop=(c == NGC - 1)),
                      reads=[r_wdn, r_gT[c]], writes=[r_ps_o[ob]], signal=(c == NGC - 1))
            cx.op("dve", lambda e, m=m, ob=ob: e.tensor_tensor(out=xw[:, m, 1:1 + n_out], in0=ps_o[ob][:, :n_out], in1=xw[:, m, 1:1 + n_out], op=ALU.add),
                  reads=[r_ps_o[ob], r_xw], writes=[r_xw])
        cx.dma("sp", x_out_v[:, :, PAD + t0:PAD + t0 + n_out], xw[:, :, 1:1 + n_out], reads=[r_xw], writes=[w["r_xout"]], slot=r_xw)
        if ti + 2 < len(tiles):
            load(ti + 2)
    cx.barrier()
    cx.es = old_es
    es.close()


NH, NKV, HD = 16, 4, 64
QD = NH * HD
WQKV = QD + 2 * NKV * HD + NKV * HD


def attn_layer(cx, cm, T, x_in, x_out, w):
    es = ExitStack(); old_es = cx.es; cx.es = es
    NB = T // 128
    wq = cx.sbuf("wqkv", [128, KC, WQKV], BF16); r_wq = Res("wqkv")
    wo = cx.sbuf("wo", [128, KC, D], BF16); r_wo = Res("wo")
    gn = cx.sbuf("gn", [128, KC], F32); r_gn = Res("gn")
    gqk = cx.sbuf("gqk", [128, 2], F32); r_gqk = Res("gqk")
    esk = cx.sbuf("esk", [128, NH], F32); r_esk = Res("esk")
    bm = cx.sbuf("bm", [128, 128], F32); r_bm = Res("bm")
    rmf = cx.sbuf("rmf", [128, 128], F32); r_rmf = Res("rmf")
    rm = cx.sbuf("rm", [128, 128], BF16); r_rm = Res("rm")
    onesb = cx.sbuf("onesb", [128, 128], BF16); r_onesb = Res("onesb")
    kT = cx.sbuf("kT", [128, NKV, T], BF16); r_kT = [Res(f"kT{i}") for i in range(T // 512)]
    V = cx.sbuf("V", [128, NB, NKV * HD], BF16); r_V = [Res(f"V{i}") for i in range(NB)]
    vv = cx.sbuf("vv", [128, 4, NKV, 2, HD], BF16); r_vv = [Res(f"vv{i}") for i in range(4)]
    xw = cx.sbuf("xw", [128, KC, 512], F32); r_xw = Res("xw")
    hn = cx.sbuf("hn", [128, KC, 512], BF16); r_hn = [Res(f"hn{k}") for k in range(KC)]
    qT = cx.sbuf("qT", [128, KC, 512], BF16); r_qT = [Res(f"qT{k}") for k in range(KC)]
    oT = hn; r_oT = r_hn
    cs = cx.sbuf("cs", [128, 2, 512], F32); r_cs = Res("cs")
    sq = [cx.sbuf(f"sq{i}", [128, 512], F32) for i in range(2)]; r_sq = [Res("sq0"), Res("sq1")]
    rstd = cx.sbuf("rstd", [128, 512], F32); r_rstd = Res("rstd")
    qnb = cx.sbuf("qnb", [128, 512], BF16); r_qnb = Res("qnb")
    ta = cx.sbuf("ta", [128, 512], F32); r_ta = Res("ta")
    tb = cx.sbuf("tb", [128, 512], F32); r_tb = Res("tb")
    PT = [cx.sbuf(f"PT{i}", [128, 512], BF16) for i in range(4)]; r_PT = [Res(f"PT{i}") for i in range(4)]
    rd = cx.sbuf("rd", [128, 512], F32); r_rd = Res("rd")
    ps_p1 = cx.psum("ps_p0", [128, 512], F32); r_ps_p1 = Res("ps_p0", excl=True)
    ps_p = [ps_p1, ps_p1]; r_ps_p = [r_ps_p1, r_ps_p1]
    ps_a = cx.psum("ps_a", [128, 512], F32); r_ps_a = Res("ps_a", excl=True)
    ps_b = ps_a; r_ps_b = r_ps_a
    ps_s = [[cx.psum(f"ps_s{i}{hf}", [128, 512], F32) for hf in range(2)] for i in range(2)]
    r_ps_s = [[Res(f"ps_s{i}{hf}", excl=True) for hf in range(2)] for i in range(2)]
    ps_o = cx.psum("ps_o", [128, 512], F32); r_ps_o = Res("ps_o", excl=True)
    ps_d = cx.psum("ps_d", [128, 512], F32); r_ps_d = Res("ps_d", excl=True)

    for dst, r, key in ((gn, r_gn, "gn"), (gqk, r_gqk, "gqk"), (esk, r_esk, "sink"), (bm, r_bm, "bm"), (rmf, r_rmf, "rm")):
        cx.dma("sp", dst[:], w[key], writes=[r], slot=r)
    load_weight_bf16(cx, wq, r_wq, w["wqkv"], KC)
    load_weight_bf16(cx, wo, r_wo, w["wo"], KC)
    cx.op("act", lambda e: e.activation(out=esk[:], in_=esk[:], func=AF.Exp), reads=[r_esk], writes=[r_esk])
    cx.op("dve", lambda e: e.tensor_copy(rm[:], rmf[:]), reads=[r_rmf], writes=[r_rm])
    cx.op("pool", lambda e: e.memset(onesb[:], 1.0), writes=[r_onesb])

    x_in_v = x_in.rearrange("(k p) t -> p k t", p=128)
    x_out_v = x_out.rearrange("(k p) t -> p k t", p=128)
    pp = [0]

    def load_norm(t0):
        cx.dma("sp", xw[:, :, :], x_in_v[:, :, PAD + t0:PAD + t0 + 512], reads=[w["r_xin"]], writes=[r_xw], slot=r_xw)
        cx.dma("sp", cs[:, 0, :], w["cos"][:, t0:t0 + 512], writes=[r_cs], slot=r_cs)
        cx.dma("sp", cs[:, 1, :], w["sin"][:, t0:t0 + 512], writes=[r_cs], slot=r_cs)
        rmsnorm_fm(cx, cm, xw, r_xw, 512, lambda k: gn[:, k:k + 1], r_gn, hn, r_hn, ps_a, r_ps_a, sq, r_sq, rstd, r_rstd)

    def proj_fm(col0):
        b = pp[0] % 2; pp[0] += 1
        for k in range(KC):
            cx.op("pe", lambda e, k=k: e.matmul(ps_p[b][:, :], wq[:, k, col0:col0 + 128], hn[:, k, :],
                                                start=(k == 0), stop=(k == KC - 1)),
                  reads=[r_wq, r_hn[k]], writes=[r_ps_p[b]], signal=(k == KC - 1))
        return ps_p[b], r_ps_p[b]

    def headnorm_rope(ps, r_ps, gcol, out_ap, r_out):
        cx.op("act", lambda e: e.activation(out=sq[0][:, :], in_=ps[:, :], func=AF.Square), reads=[r_ps], writes=[r_sq[0]])
        cx.op("pe", lambda e: e.matmul(ps_a[:, :], bm[:], sq[0][:, :], start=True, stop=True),
              reads=[r_bm, r_sq[0]], writes=[r_ps_a])
        cx.op("act", lambda e: e.activation(out=rstd[:, :], in_=ps_a[:, :], func=AF.Ln, bias=cm.eps[:, 0:1]),
              reads=[r_ps_a, cm.r_eps], writes=[r_rstd])
        cx.op("act", lambda e: e.activation(out=rstd[:, :], in_=rstd[:, :], func=AF.Exp, scale=-0.5),
              reads=[r_rstd], writes=[r_rstd])
        cx.op("dve", lambda e: e.scalar_tensor_tensor(out=qnb[:, :], in0=ps[:, :], scalar=gqk[:, gcol:gcol + 1], in1=rstd[:, :],
                                                      op0=ALU.mult, op1=ALU.mult),
              reads=[r_ps, r_gqk, r_rstd], writes=[r_qnb])
        cx.op("pe", lambda e: e.matmul(ps_b[:, :], rm[:], qnb[:, :], start=True, stop=True),
              reads=[r_rm, r_qnb], writes=[r_ps_b])
        cx.op("dve", lambda e: e.tensor_tensor(out=ta[:, :], in0=qnb[:, :], in1=cs[:, 0, :], op=ALU.mult),
              reads=[r_qnb, r_cs], writes=[r_ta])
        cx.op("dve", lambda e: e.tensor_tensor(out=tb[:, :], in0=ps_b[:, :], in1=cs[:, 1, :], op=ALU.mult),
              reads=[r_ps_b, r_cs], writes=[r_tb])
        cx.op("pool", lambda e: e.tensor_tensor(out=out_ap, in0=ta[:, :], in1=tb[:, :], op=ALU.add),
              reads=[r_ta, r_tb], writes=[r_out])

    for ti in range(T // 512):
        t0 = ti * 512
        load_norm(t0)
        for j in range(NKV):
            ps, r_ps = proj_fm(QD + j * 128)
            headnorm_rope(ps, r_ps, 1, kT[:, j, t0:t0 + 512], r_kT[ti])
        for bl in range(4):
            b = pp[0] % 2; pp[0] += 1
            for k in range(KC):
                cx.op("pe", lambda e, k=k: e.matmul(ps_p[b][:, :256], hn[:, k, bl * 128:(bl + 1) * 128], wq[:, k, QD + 512:QD + 768],
                                                    start=(k == 0), stop=(k == KC - 1)),
                      reads=[r_wq, r_hn[k]], writes=[r_ps_p[b]], signal=(k == KC - 1))
            cx.op("act", lambda e: e.copy(out=V[:, ti * 4 + bl, :], in_=ps_p[b][:, :256]), reads=[r_ps_p[b]], writes=[r_V[ti * 4 + bl]])

    vv_have = {}
    sc = [0]
    for ti in range(T // 512):
        t0 = ti * 512
        load_norm(t0)
        for c in range(KC):
            ps, r_ps = proj_fm(c * 128)
            headnorm_rope(ps, r_ps, 0, qT[:, c, :], r_qT[c])
        for bl in range(4):
            nb = ti * 4 + bl
            qs = slice(bl * 128, (bl + 1) * 128)
            kbs = [kb for kb in (nb - 1, nb, nb + 1) if 0 <= kb < NB]
            for kb in kbs:
                if vv_have.get(kb % 4) != kb:
                    for d2 in range(2):
                        cx.op("pool", lambda e, d2=d2: e.tensor_copy(vv[:, kb % 4, :, d2, :], V[:, kb, :].rearrange("p (j d) -> p j d", j=NKV)),
                              reads=[r_V[kb]], writes=[r_vv[kb % 4]])
                    vv_have[kb % 4] = kb
            for j in range(NKV):
                pts = []
                for kb in kbs:
                    sb = sc[0] % 2; pb = sc[0] % 4; sc[0] += 1
                    for half in range(2):
                        rows = slice(half * 64, half * 64 + 64)
                        cx.op("pe", lambda e: e.matmul(ps_s[sb][half][:, :256], kT[rows, j, kb * 128:(kb + 1) * 128],
                                                       qT[rows, 2 * j:2 * j + 2, qs], start=True, stop=True),
                              reads=[r_kT[kb // 4], r_qT[2 * j], r_qT[2 * j + 1]], writes=[r_ps_s[sb][half]])
                    for half in range(2):
                        cx.op("act", lambda e: e.activation(out=PT[pb][:, half * 256:(half + 1) * 256], in_=ps_s[sb][half][:, :256], func=AF.Exp, scale=HD ** -0.5),
                              reads=[r_ps_s[sb][half]], writes=[r_PT[pb]])
                    if kb != nb:
                        sgn = 1 if kb < nb else -1
                        cx.op("pool", lambda e: e.affine_select(out=PT[pb][:, :].rearrange("p (a q) -> p a q", a=4),
                                                                in_=PT[pb][:, :].rearrange("p (a q) -> p a q", a=4),
                                                                pattern=[[0, 4], [-sgn, 128]], compare_op=ALU.is_ge, fill=cx.fill0,
                                                                base=0, channel_multiplier=sgn),
                              reads=[r_PT[pb]], writes=[r_PT[pb]])
                    pts.append((kb, pb))
                for i, (kb, pb) in enumerate(pts):
                    cx.op("pe", lambda e: e.matmul(ps_o[:, :], vv[:, kb % 4, j, :, :], PT[pb][:, :], start=(i == 0), stop=(i == len(pts) - 1)),
                          reads=[r_vv[kb % 4], r_PT[pb]], writes=[r_ps_o], signal=(i == len(pts) - 1))
                for i, (kb, pb) in enumerate(pts):
                    cx.op("pe", lambda e: e.matmul(ps_d[:, :], onesb[:], PT[pb][:, :], start=(i == 0), stop=(i == len(pts) - 1)),
                          reads=[r_onesb, r_PT[pb]], writes=[r_ps_d], signal=(i == len(pts) - 1))
                heads = (4 * j, 4 * j + 2, 4 * j + 1, 4 * j + 3)
                for a, h in enumerate(heads):
                    cx.op("dve", lambda e: e.tensor_scalar(out=rd[:, a * 128:(a + 1) * 128], in0=ps_d[:, a * 128:(a + 1) * 128],
                                                           scalar1=esk[:, h:h + 1], scalar2=None, op0=ALU.add),
                          reads=[r_ps_d, r_esk], writes=[r_rd])
                cx.op("act", lambda e: e.activation(out=rd[:, :], in_=rd[:, :], func=AF.Ln), reads=[r_rd], writes=[r_rd])
                cx.op("act", lambda e: e.activation(out=rd[:, :], in_=rd[:, :], func=AF.Exp, scale=-1.0), reads=[r_rd], writes=[r_rd])
                for half in range(2):
                    rows = slice(half * 64, half * 64 + 64)
                    cx.op("dve", lambda e: e.tensor_tensor(out=oT[rows, 2 * j:2 * j + 2, qs],
                                                           in0=ps_o[rows, half * 256:(half + 1) * 256].rearrange("p (a q) -> p a q", a=2),
                                                           in1=rd[rows, half * 256:(half + 1) * 256].rearrange("p (a q) -> p a q", a=2), op=ALU.mult),
                          reads=[r_ps_o, r_rd], writes=[r_oT[2 * j], r_oT[2 * j + 1]])
        for m in range(KC):
            b = pp[0] % 2; pp[0] += 1
            for c in range(KC):
                cx.op("pe", lambda e, c=c: e.matmul(ps_p[b][:, :], wo[:, c, m * 128:(m + 1) * 128], oT[:, c, :],
                                                    start=(c == 0), stop=(c == KC - 1)),
                      reads=[r_wo, r_oT[c]], writes=[r_ps_p[b]], signal=(c == KC - 1))
            cx.op("dve", lambda e: e.tensor_tensor(out=xw[:, m, :], in0=ps_p[b][:, :], in1=xw[:, m, :], op=ALU.add),
                  reads=[r_ps_p[b], r_xw], writes=[r_xw])
        cx.dma("sp", x_out_v[:, :, PAD + t0:PAD + t0 + 512], xw[:, :, :], reads=[r_xw], writes=[w["r_xout"]], slot=r_xw)
    cx.barrier()
    cx.es = old_es
    es.close()


DI, NHS, NG, DS = 2048, 32, 8, 128
CONVD = DI + 2 * NG * DS
SIN = DI + CONVD + 2 * NHS


def ssd_layer(cx, cm, T, x_in, x_out, w, scr):
    NCH = T // 128
    x_in_v = x_in.rearrange("(k p) t -> p k t", p=128)
    x_out_v = x_out.rearrange("(k p) t -> p k t", p=128)
    es = ExitStack(); old_es = cx.es; cx.es = es
    win = cx.sbuf("win", [128, KC, SIN], BF16); r_win = Res("win")
    cwb = cx.sbuf("cwb", [128, 32, 6], F32); r_cwb = Res("cwb")
    gn = cx.sbuf("gn", [128, KC], F32); r_gn = Res("gn")
    dtp = cx.sbuf("dtp", [64, 2], F32); r_dtp = Res("dtp")
    idb = cx.sbuf("idb", [128, 128], BF16); r_idb = Res("idb")
    idf = cx.sbuf("idf", [128, 128], F32); r_idf = Res("idf")
    xw = cx.sbuf("xw", [128, KC, 512], F32); r_xw = Res("xw")
    hn = cx.sbuf("hn", [128, KC, 512], BF16); r_hn = [Res(f"hn{k}") for k in range(KC)]
    sq = [cx.sbuf(f"sq{i}", [128, 512], F32) for i in range(2)]; r_sq = [Res("sq0"), Res("sq1")]
    rstd = cx.sbuf("rstd", [128, 512], F32); r_rstd = Res("rstd")
    tcv = [cx.sbuf(f"tcv{i}", [128, 384], F32) for i in range(2)]; r_tcv = [Res(f"tcv{i}") for i in range(2)]
    xbc = [cx.sbuf(f"xbc{i}", [128, 384], BF16) for i in range(2)]; r_xbc = [Res(f"xbc{i}") for i in range(2)]
    tok = cx.sbuf("tok", [128, 3, DI + NG * DS], BF16); r_tok = Res("tok")
    szt = cx.sbuf("szt", [128, DI], BF16); r_szt = Res("szt")
    aT = cx.sbuf("aT", [64, 384], F32); r_aT = Res("aT")
    dT = cx.sbuf("dT", [64, 384], F32); r_dT = Res("dT")
    acT = cx.sbuf("acT", [64, 384], F32); r_acT = Res("acT")
    onesf = cx.sbuf("onesf", [64, 384], F32); r_onesf = Res("onesf")
    tk2 = cx.sbuf("tk2", [128, 2, 64], F32); r_tk2 = Res("tk2")
    tot1 = cx.sbuf("tot1", [64, 1], F32); r_tot1 = Res("tot1")
    ps_st = cx.psum("ps_st", [128, 512], F32); r_ps_st = Res("ps_st", excl=True)
    ps_c = [cx.psum(f"ps_c{i}", [128, 512], F32) for i in range(2)]; r_ps_c = [Res(f"ps_c{i}", excl=True) for i in range(2)]
    ps_t = [cx.psum(f"ps_t{i}", [128, 1024], BF16) for i in range(2)]; r_ps_t = [Res(f"ps_t{i}", excl=True) for i in range(2)]
    ps_z = [cx.psum(f"ps_z{i}", [128, 512], F32) for i in range(2)]; r_ps_z = [Res(f"ps_z{i}", excl=True) for i in range(2)]
    ps_f = cx.psum("ps_f", [128, 512], F32); r_ps_f = Res("ps_f", excl=True)

    for dst, r, key in ((cwb, r_cwb, "cwb"), (gn, r_gn, "gn"), (dtp, r_dtp, "dtp"), (idf, r_idf, "ident")):
        cx.dma("sp", dst[:], w[key], writes=[r], slot=r)
    load_weight_bf16(cx, win, r_win, w["win"], KC)
    cx.op("dve", lambda e: e.tensor_copy(idb[:], idf[:]), reads=[r_idf], writes=[r_idb])
    cx.op("pool", lambda e: e.memset(onesf[:], 1.0), writes=[r_onesf])
    cx.op("act", lambda e: e.activation(out=dtp[:, 1:2], in_=dtp[:, 1:2], func=AF.Exp), reads=[r_dtp], writes=[r_dtp])
    cx.op("dve", lambda e: e.tensor_scalar(out=dtp[:, 1:2], in0=dtp[:, 1:2], scalar1=-1.0, scalar2=None, op0=ALU.mult),
          reads=[r_dtp], writes=[r_dtp])
    cc = [0]
    for t0 in range(0, T, 384):
        n_out = min(384, T - t0); W = n_out + 4; nblk = n_out // 128
        cx.dma("sp", xw[:, :, :W], x_in_v[:, :, PAD + t0 - 2:PAD + t0 - 2 + W], reads=[w["r_xin"]], writes=[r_xw], slot=r_xw)
        rmsnorm_fm(cx, cm, xw, r_xw, W, lambda k: gn[:, k:k + 1], r_gn, hn, r_hn, ps_st, r_ps_st, sq, r_sq, rstd, r_rstd)
        for c in range(32):
            b = cc[0] % 2; cc[0] += 1
            col0 = DI + c * 128
            for k in range(KC):
                cx.op("pe", lambda e, k=k: e.matmul(ps_c[b][:, :W], win[:, k, col0:col0 + 128], hn[:, k, :W], start=(k == 0), stop=(k == KC - 1)),
                      reads=[r_win, r_hn[k]], writes=[r_ps_c[b]], signal=(k == KC - 1))
            cx.op("act", lambda e: e.activation(out=tcv[b][:, :n_out], in_=ps_c[b][:, 2:2 + n_out], func=AF.Identity,
                                                bias=cwb[:, c, 5:6], scale=cwb[:, c, 2:3]),
                  reads=[r_ps_c[b], r_cwb], writes=[r_tcv[b]])
            for kk in (0, 1, 3, 4):
                cx.op("dve", lambda e, kk=kk: e.scalar_tensor_tensor(out=tcv[b][:, :n_out], in0=ps_c[b][:, kk:kk + n_out], scalar=cwb[:, c, kk:kk + 1],
                                                                     in1=tcv[b][:, :n_out], op0=ALU.mult, op1=ALU.add),
                      reads=[r_ps_c[b], r_cwb, r_tcv[b]], writes=[r_tcv[b]])
            cx.op("act", lambda e: e.activation(out=xbc[b][:, :n_out], in_=tcv[b][:, :n_out], func=AF.Silu), reads=[r_tcv[b]], writes=[r_xbc[b]])
            if c < 24:
                for bl in range(nblk):
                    cx.op("pe", lambda e, bl=bl: e.transpose(ps_t[b][:, bl * 128:(bl + 1) * 128], xbc[b][:, bl * 128:(bl + 1) * 128], idb[:]),
                          reads=[r_xbc[b], r_idb], writes=[r_ps_t[b]])
                cx.op("dve", lambda e: e.tensor_copy(tok[:, :nblk, c * 128:(c + 1) * 128], ps_t[b][:, :nblk * 128].rearrange("p (a q) -> p a q", a=nblk)),
                      reads=[r_ps_t[b]], writes=[r_tok])
            if c >= 16:
                dst = scr["BT"] if c < 24 else scr["CT"]
                g = (c - 16) % 8
                cx.dma("sp", dst[g * 128:(g + 1) * 128, t0:t0 + n_out], xbc[b][:, :n_out], reads=[r_xbc[b]], writes=[scr["r_BC"]], slot=r_xbc[b])
        for bl in range(nblk):
            r0 = t0 + bl * 128
            cx.dma("sp", scr["xtok"][r0:r0 + 128, :], tok[:, bl, :DI], reads=[r_tok], writes=[scr["r_tok"]], slot=r_tok)
            cx.dma("sp", scr["Btok"][r0:r0 + 128, :], tok[:, bl, DI:], reads=[r_tok], writes=[scr["r_tok"]], slot=r_tok)
        for bl in range(nblk):
            r0 = t0 + bl * 128
            for ct in range(4):
                b = cc[0] % 2; cc[0] += 1
                for k in range(KC):
                    cx.op("pe", lambda e, k=k: e.matmul(ps_z[b][:, :], hn[:, k, 2 + bl * 128:2 + (bl + 1) * 128], win[:, k, ct * 512:(ct + 1) * 512],
                                                        start=(k == 0), stop=(k == KC - 1)),
                          reads=[r_win, r_hn[k]], writes=[r_ps_z[b]], signal=(k == KC - 1))
                cx.op("act", lambda e: e.activation(out=szt[:, ct * 512:(ct + 1) * 512], in_=ps_z[b][:, :], func=AF.Silu),
                      reads=[r_ps_z[b]], writes=[r_szt])
            cx.dma("sp", scr["sz"][r0:r0 + 128, :], szt[:, :], reads=[r_szt], writes=[scr["r_sz"]], slot=r_szt)
        for k in range(KC):
            cx.op("pe", lambda e, k=k: e.matmul(ps_f[:64, :n_out], win[:, k, DI + CONVD:SIN], hn[:, k, 2:2 + n_out], start=(k == 0), stop=(k == KC - 1)),
                  reads=[r_win, r_hn[k]], writes=[r_ps_f], signal=(k == KC - 1))
        cx.op("act", lambda e: e.activation(out=dT[:, :n_out], in_=ps_f[:64, :n_out], func=AF.Exp, bias=dtp[:, 0:1]),
              reads=[r_ps_f, r_dtp], writes=[r_dT])
        cx.op("act", lambda e: e.activation(out=dT[:, :n_out], in_=dT[:, :n_out], func=AF.Ln, bias=cm.one[:64, 0:1]),
              reads=[r_dT, cm.r_one], writes=[r_dT])
        cx.op("dve", lambda e: e.tensor_scalar(out=aT[:, :n_out], in0=dT[:, :n_out], scalar1=dtp[:, 1:2], scalar2=None, op0=ALU.mult),
              reads=[r_dT, r_dtp], writes=[r_aT])
        for bl in range(nblk):
            cs_ = slice(bl * 128, (bl + 1) * 128)
            cx.op("dve", lambda e: e.tensor_tensor_scan(out=acT[:, cs_], data0=onesf[:, cs_], data1=aT[:, cs_], initial=0.0, op0=ALU.mult, op1=ALU.add),
                  reads=[r_onesf, r_aT], writes=[r_acT])
            cx.op("dve", lambda e: e.tensor_copy(tot1[32:64, :], acT[32:64, bl * 128 + 127:bl * 128 + 128]), reads=[r_acT], writes=[r_tot1])
            cx.op("dve", lambda e: e.scalar_tensor_tensor(out=acT[32:64, cs_], in0=acT[32:64, cs_], scalar=-1.0, in1=aT[32:64, cs_], op0=ALU.mult, op1=ALU.add),
                  reads=[r_acT, r_aT], writes=[r_acT])
            cx.op("dve", lambda e: e.tensor_scalar(out=acT[32:64, cs_], in0=acT[32:64, cs_], scalar1=tot1[32:64, 0:1], scalar2=None, op0=ALU.add),
                  reads=[r_acT, r_tot1], writes=[r_acT])
            r0 = t0 + bl * 128
            for i, src in enumerate((acT, dT)):
                cx.op("pe", lambda e, src=src: e.transpose(ps_f[:, 128 + i * 64:128 + (i + 1) * 64], src[:, cs_], idf[:64, :64]),
                      reads=[r_acT, r_dT, r_idf], writes=[r_ps_f])
            cx.op("act", lambda e: e.copy(out=tk2[:, :, :], in_=ps_f[:, 128:256].rearrange("p (a q) -> p a q", a=2)), reads=[r_ps_f], writes=[r_tk2])
            cx.dma("sp", scr["actok"][r0:r0 + 128, :], tk2[:, 0, :], reads=[r_tk2], writes=[scr["r_ac"]], slot=r_tk2)
            cx.dma("sp", scr["dttok"][r0:r0 + 128, :], tk2[:, 1, :], reads=[r_tk2], writes=[scr["r_ac"]], slot=r_tk2)
        cx.dma("sp", scr["acT"][:, t0:t0 + n_out], acT[:, :n_out], reads=[r_acT], writes=[scr["r_ac"]], slot=r_acT)
    cx.barrier()
    cx.es = old_es
    es.close()
    ssd_phase_b(cx, cm, T, x_in_v, x_out_v, w, scr)


def ssd_phase_b(cx, cm, T, x_in_v, x_out_v, w, scr):
    NCH = T // 128
    es = ExitStack(); old_es = cx.es; cx.es = es
    wout = cx.sbuf("wout", [128, 16, D], BF16); r_wout = Res("wout")
    dbc = cx.sbuf("dbc", [128, DI], F32); r_dbc = Res("dbc")
    gbc = cx.sbuf("gbc", [128, DI], F32); r_gbc = Res("gbc")
    idb = cx.sbuf("idb", [128, 128], BF16); r_idb = Res("idb")
    idf = cx.sbuf("idf", [128, 128], F32); r_idf = Res("idf")
    def two(name, shape, dt):
        return [cx.sbuf(f"{name}{i}", shape, dt) for i in range(2)], [Res(f"{name}{i}") for i in range(2)]
    xts, r_xts = two("xt", [128, DI], BF16)
    bts, r_bts = two("bt", [128, NG * DS], BF16)
    BTss, r_BTss = two("BTs", [128, NG, 128], BF16)
    CTss, r_CTss = two("CTs", [128, NG, 128], BF16)
    dtts, r_dtts = two("dtt", [128, 64], F32)
    acts, r_acts = two("act", [128, 64], F32)
    totbs, r_totbs = two("totb", [128, 32], F32)
    rowbs, r_rowbs = two("rowb", [128, 32, 128], F32)
    sm = cx.sbuf("sm", [128, 4, 32], F32); r_sm = Res("sm")
    cbt = cx.sbuf("cbt", [128, NG, 128], F32); r_cbt = [Res(f"cbt{g}") for g in range(NG)]
    dif = [cx.sbuf(f"dif{i}", [128, 4, 128], F32) for i in range(2)]; r_dif = [Res(f"dif{i}") for i in range(2)]
    MT = [cx.sbuf(f"MT{i}", [128, 4, 128], BF16) for i in range(2)]; r_MT = [Res(f"MT{i}") for i in range(2)]
    Bw = [cx.sbuf(f"Bw{i}", [128, 4, 128], BF16) for i in range(2)]; r_Bw = [Res(f"Bw{i}") for i in range(2)]
    HT = cx.sbuf("HT", [128, NHS, 64], F32); r_HT = [Res(f"HT{g}") for g in range(NG)]
    HTb = cx.sbuf("HTb", [128, NHS, 64], BF16); r_HTb = [Res(f"HTb{g}") for g in range(NG)]
    xdt = cx.sbuf("xdt", [128, DI], BF16); r_xdt = Res("xdt")
    yof = cx.sbuf("yof", [128, 512], F32); r_yof = Res("yof")
    dsk = cx.sbuf("dsk", [128, DI], F32); r_dsk = Res("dsk")
    ysb = cx.sbuf("ysb", [128, DI], F32); r_ysb = [Res(f"ysb{q}") for q in range(4)]
    y0 = cx.sbuf("y0", [128, DI], F32); r_y0 = Res("y0")
    szt = cx.sbuf("szt", [128, DI], BF16); r_szt = Res("szt")
    gss = cx.sbuf("gss", [128, NG], F32); r_gss = Res("gss")
    yb = cx.sbuf("yb", [128, DI], BF16); r_yb = Res("yb")
    yT = cx.sbuf("yT", [128, 16, 128], BF16); r_yT = Res("yT")
    xw = cx.sbuf("xw", [128, KC, 128], F32); r_xw = Res("xw")
    ps_cb = cx.psum("ps_cb", [128, 512], F32); r_ps_cb = Res("ps_cb", excl=True)
    ps_y = [cx.psum(f"ps_y{i}", [128, 512], F32) for i in range(2)]; r_ps_y = [Res(f"ps_y{i}", excl=True) for i in range(2)]
    ps_o = [cx.psum(f"ps_of{i}", [128, 512], F32) for i in range(2)]; r_ps_o = [Res(f"ps_of{i}", excl=True) for i in range(2)]
    ps_s = [cx.psum(f"ps_st{i}", [128, 512], F32) for i in range(2)]; r_ps_s = [Res(f"ps_st{i}", excl=True) for i in range(2)]
    ps_t = cx.psum("ps_tr", [128, 1024], BF16); r_ps_t = Res("ps_tr", excl=True)

    for dst, r, key in ((dbc, r_dbc, "dbc"), (gbc, r_gbc, "gbc"), (idf, r_idf, "ident")):
        cx.dma("sp", dst[:], w[key], writes=[r], slot=r)
    load_weight_bf16(cx, wout, r_wout, w["wout"], 16)
    cx.op("dve", lambda e: e.tensor_copy(idb[:], idf[:]), reads=[r_idf], writes=[r_idb])
    BTv = scr["BT"].rearrange("(g p) t -> p g t", p=128)
    CTv = scr["CT"].rearrange("(g p) t -> p g t", p=128)
    hc = [0]
    iters = [(0, c) for c in range(NCH)] + [(1, c) for c in range(NCH - 1, -1, -1)]

    def load_inputs(i):
        d, c = iters[i]; p = i % 2
        r0 = c * 128
        dc = slice(d * 32, d * 32 + 32)
        cx.dma("sp", xts[p][:], scr["xtok"][r0:r0 + 128, :], reads=[scr["r_tok"]], writes=[r_xts[p]], slot=r_xts[p])
        cx.dma("sp", bts[p][:], scr["Btok"][r0:r0 + 128, :], reads=[scr["r_tok"]], writes=[r_bts[p]], slot=r_bts[p])
        cx.dma("sp", BTss[p][:], BTv[:, :, r0:r0 + 128], reads=[scr["r_BC"]], writes=[r_BTss[p]], slot=r_BTss[p])
        cx.dma("sp", CTss[p][:], CTv[:, :, r0:r0 + 128], reads=[scr["r_BC"]], writes=[r_CTss[p]], slot=r_CTss[p])
        cx.dma("sp", dtts[p][:], scr["dttok"][r0:r0 + 128, :], reads=[scr["r_ac"]], writes=[r_dtts[p]], slot=r_dtts[p])
        cx.dma("sp", acts[p][:], scr["actok"][r0:r0 + 128, :], reads=[scr["r_ac"]], writes=[r_acts[p]], slot=r_acts[p])
        rl = r0 + 127 if d == 0 else r0
        cx.dma("sp", totbs[p][:], scr["actok"][rl:rl + 1, dc].partition_broadcast(128), reads=[scr["r_ac"]], writes=[r_totbs[p]], slot=r_totbs[p])
        cx.dma("sp", rowbs[p][:], scr["acT"][d * 32:d * 32 + 32, r0:r0 + 128].partition_broadcast(128), reads=[scr["r_ac"]], writes=[r_rowbs[p]], slot=r_rowbs[p])

    load_inputs(0)
    for it, (d, c) in enumerate(iters):
        if c == (0 if d == 0 else NCH - 1):
            cx.op("pool", lambda e: e.memset(HT[:, :, :], 0.0), writes=r_HT)
            cx.op("pool", lambda e: e.memset(HTb[:, :, :], 0.0), writes=r_HTb)
        dc = slice(d * 32, d * 32 + 32)
        if True:
            r0 = c * 128
            p = it % 2
            xt, r_xt, bt, r_bt, BTs, r_BTs, CTs, r_CTs = xts[p], r_xts[p], bts[p], r_bts[p], BTss[p], r_BTss[p], CTss[p], r_CTss[p]
            dtt, r_dtt, act, r_act, totb, r_totb, rowb, r_rowb = dtts[p], r_dtts[p], acts[p], r_acts[p], totbs[p], r_totbs[p], rowbs[p], r_rowbs[p]
            if it + 1 < len(iters):
                load_inputs(it + 1)
            if d == 1:
                cx.dma("sp", y0[:], scr["y0"][r0:r0 + 128, :], reads=[scr["r_y0"]], writes=[r_y0], slot=r_y0)
                cx.dma("sp", szt[:], scr["sz"][r0:r0 + 128, :], reads=[scr["r_sz"]], writes=[r_szt], slot=r_szt)
                cx.dma("sp", xw[:], x_in_v[:, :, PAD + r0:PAD + r0 + 128], reads=[w["r_xin"]], writes=[r_xw], slot=r_xw)
                cx.op("dve", lambda e: e.tensor_tensor(out=dsk[:, :], in0=xt[:, :], in1=dbc[:, :], op=ALU.mult), reads=[r_xt, r_dbc], writes=[r_dsk])
                cx.op("pool", lambda e: e.tensor_tensor(out=y0[:, :], in0=y0[:, :], in1=dsk[:, :], op=ALU.add), reads=[r_y0, r_dsk], writes=[r_y0])
            cx.op("dve", lambda e: e.tensor_scalar(out=sm[:, 0, :], in0=act[:, dc], scalar1=-1.0, scalar2=None, op0=ALU.mult), reads=[r_act], writes=[r_sm])
            cx.op("dve", lambda e: e.tensor_tensor(out=sm[:, 1, :], in0=totb[:, :], in1=act[:, dc], op=ALU.subtract), reads=[r_totb, r_act, r_sm], writes=[r_sm])
            cx.op("act", lambda e: e.activation(out=sm[:, 1, :], in_=sm[:, 1, :], func=AF.Exp), reads=[r_sm], writes=[r_sm])
            cx.op("act", lambda e: e.activation(out=sm[:, 2, :], in_=totb[:, :], func=AF.Exp), reads=[r_totb, r_sm], writes=[r_sm])
            cx.op("act", lambda e: e.activation(out=sm[:, 3, :], in_=act[:, dc], func=AF.Exp), reads=[r_act, r_sm], writes=[r_sm])
            for g4 in range(2):
                for gq in range(4):
                    g = g4 * 4 + gq
                    cx.op("pe", lambda e: e.matmul(ps_cb[:, gq * 128:(gq + 1) * 128], BTs[:, g, :], CTs[:, g, :], start=True, stop=True),
                          reads=[r_BTs, r_CTs], writes=[r_ps_cb])
                cx.op("act", lambda e: e.copy(out=cbt[:, g4 * 4:g4 * 4 + 4, :], in_=ps_cb[:, :].rearrange("p (g l) -> p g l", g=4)),
                      reads=[r_ps_cb], writes=r_cbt[g4 * 4:g4 * 4 + 4])
            cx.op("dve", lambda e: e.tensor_tensor(out=xdt[:, :].rearrange("p (h q) -> p h q", h=NHS), in0=xt[:, :].rearrange("p (h q) -> p h q", h=NHS),
                                                   in1=dtt[:, dc].unsqueeze(2).to_broadcast([128, NHS, 64]), op=ALU.mult),
                  reads=[r_xt, r_dtt], writes=[r_xdt])
            sgn = 1 if d == 0 else -1
            B4 = [128, 4, 128]

            def stage1(g):
                i2 = g % 2; hs = slice(4 * g, 4 * g + 4)
                cx.op("dve", lambda e: e.tensor_tensor(out=dif[i2][:, :, :], in0=rowb[:, hs, :], in1=sm[:, 0, hs].unsqueeze(2).to_broadcast(B4), op=ALU.add),
                      reads=[r_rowb, r_sm], writes=[r_dif[i2]])
                cx.op("pool", lambda e: e.affine_select(out=dif[i2][:, :, :], in_=dif[i2][:, :, :], pattern=[[0, 4], [sgn, 128]], compare_op=ALU.is_ge,
                                                        fill=cx.fillneg, base=0, channel_multiplier=-sgn),
                      reads=[r_dif[i2]], writes=[r_dif[i2]])
                cx.op("act", lambda e: e.activation(out=dif[i2][:, :, :], in_=dif[i2][:, :, :], func=AF.Exp), reads=[r_dif[i2]], writes=[r_dif[i2]])
                cx.op("dve", lambda e: e.tensor_tensor(out=Bw[i2][:, :, :], in0=bt[:, g * 128:(g + 1) * 128].unsqueeze(1).to_broadcast(B4),
                                                       in1=sm[:, 1, hs].unsqueeze(2).to_broadcast(B4), op=ALU.mult),
                      reads=[r_bt, r_sm], writes=[r_Bw[i2]])

            def stage2(g):
                i2 = g % 2; hs = slice(4 * g, 4 * g + 4); q = g // 2; gg = g % 2
                yb_, ob_ = ps_y[q % 2], ps_o[q % 2]
                r_yb_, r_ob_ = r_ps_y[q % 2], r_ps_o[q % 2]
                sb = ps_s[i2]; r_sb = r_ps_s[i2]
                for hq in range(4):
                    h = 4 * g + hq; hh = gg * 4 + hq
                    cx.op("pe", lambda e: e.matmul(ob_[:, hh * 64:(hh + 1) * 64], CTs[:, g, :], HTb[:, h, :], start=True, stop=True),
                          reads=[r_CTs, r_HTb[g]], writes=[r_ob_])
                    cx.op("pe", lambda e: e.matmul(sb[:, hq * 64:(hq + 1) * 64], Bw[i2][:, hq, :], xdt[:, h * 64:(h + 1) * 64], start=True, stop=True),
                          reads=[r_Bw[i2], r_xdt], writes=[r_sb])
                cx.op("dve", lambda e: e.tensor_tensor(out=MT[i2][:, :, :], in0=dif[i2][:, :, :], in1=cbt[:, g, :].unsqueeze(1).to_broadcast(B4), op=ALU.mult),
                      reads=[r_dif[i2], r_cbt[g]], writes=[r_MT[i2]])
                cx.op("dve", lambda e: e.tensor_tensor(out=HT[:, hs, :], in0=HT[:, hs, :], in1=sm[:, 2, hs].unsqueeze(2).to_broadcast([128, 4, 64]), op=ALU.mult),
                      reads=[r_HT[g], r_sm], writes=[r_HT[g]])
                for hq in range(4):
                    h = 4 * g + hq; hh = gg * 4 + hq
                    cx.op("pe", lambda e: e.matmul(yb_[:, hh * 64:(hh + 1) * 64], MT[i2][:, hq, :], xdt[:, h * 64:(h + 1) * 64], start=True, stop=True),
                          reads=[r_MT[i2], r_xdt], writes=[r_yb_])
                cx.op("dve", lambda e: e.tensor_tensor(out=HT[:, hs, :], in0=sb[:, :256].rearrange("p (h q) -> p h q", h=4), in1=HT[:, hs, :], op=ALU.add),
                      reads=[r_HT[g], r_sb], writes=[r_HT[g]])
                cx.op("pool", lambda e: e.tensor_copy(HTb[:, hs, :], HT[:, hs, :]), reads=[r_HT[g]], writes=[r_HTb[g]])
                if gg == 1:
                    qs = slice(q * 512, (q + 1) * 512)
                    h8 = slice(q * 8, q * 8 + 8)
                    cx.op("act", lambda e: e.copy(out=ysb[:, qs], in_=yb_[:, :]), reads=[r_yb_], writes=[r_ysb[q]])
                    cx.op("dve", lambda e: e.tensor_tensor(out=yof[:, :].rearrange("p (h q) -> p h q", h=8), in0=ob_[:, :].rearrange("p (h q) -> p h q", h=8),
                                                           in1=sm[:, 3, h8].unsqueeze(2).to_broadcast([128, 8, 64]), op=ALU.mult),
                          reads=[r_ob_, r_sm], writes=[r_yof])
                    cx.op("pool", lambda e: e.tensor_tensor(out=ysb[:, qs], in0=ysb[:, qs], in1=yof[:, :], op=ALU.add), reads=[r_yof, r_ysb[q]], writes=[r_ysb[q]])

            stage1(0)
            for g in range(NG):
                if g + 1 < NG:
                    stage1(g + 1)
                stage2(g)
            if d == 0:
                cx.dma("sp", scr["y0"][r0:r0 + 128, :], ysb[:, :], reads=r_ysb, writes=[scr["r_y0"]], slot=r_ysb[0])
                continue
            cx.op("pool", lambda e: e.tensor_tensor(out=y0[:, :], in0=y0[:, :], in1=ysb[:, :], op=ALU.add), reads=[r_y0] + r_ysb, writes=[r_y0])
            cx.op("dve", lambda e: e.tensor_tensor(out=y0[:, :], in0=y0[:, :], in1=szt[:, :], op=ALU.mult), reads=[r_y0, r_szt], writes=[r_y0])
            cx.op("pool", lambda e: e.tensor_tensor(out=ysb[:, :], in0=y0[:, :], in1=y0[:, :], op=ALU.mult), reads=[r_y0] + r_ysb, writes=r_ysb)
            cx.op("dve", lambda e: e.tensor_reduce(out=gss[:, :], in_=ysb[:, :].rearrange("p (g f) -> p g f", g=NG), axis=mybir.AxisListType.X, op=ALU.add),
                  reads=r_ysb, writes=[r_gss])
            cx.op("act", lambda e: e.activation(out=gss[:, :], in_=gss[:, :], func=AF.Ln, bias=cm.eps[:, 0:1], scale=1.0 / 256), reads=[r_gss, cm.r_eps], writes=[r_gss])
            cx.op("act", lambda e: e.activation(out=gss[:, :], in_=gss[:, :], func=AF.Exp, scale=-0.5), reads=[r_gss], writes=[r_gss])
            for g in range(NG):
                gs = slice(g * 256, (g + 1) * 256)
                cx.op("dve", lambda e: e.scalar_tensor_tensor(out=yb[:, gs], in0=y0[:, gs], scalar=gss[:, g:g + 1], in1=gbc[:, gs], op0=ALU.mult, op1=ALU.mult),
                      reads=[r_y0, r_gss, r_gbc], writes=[r_yb])
            for half in range(2):
                for cq in range(8):
                    cch = half * 8 + cq
                    cx.op("pe", lambda e: e.transpose(ps_t[:, cq * 128:(cq + 1) * 128], yb[:, cch * 128:(cch + 1) * 128], idb[:]),
                          reads=[r_yb, r_idb], writes=[r_ps_t])
                cx.op("act", lambda e: e.copy(out=yT[:, half * 8:(half + 1) * 8, :], in_=ps_t[:, :].rearrange("p (a q) -> p a q", a=8)),
                      reads=[r_ps_t], writes=[r_yT])
            for m in range(KC):
                pb = ps_y[m % 2]; r_pb = r_ps_y[m % 2]
                for cch in range(16):
                    cx.op("pe", lambda e, cch=cch: e.matmul(pb[:, :128], wout[:, cch, m * 128:(m + 1) * 128], yT[:, cch, :], start=(cch == 0), stop=(cch == 15)),
                          reads=[r_wout, r_yT], writes=[r_pb], signal=(cch == 15))
                cx.op("dve", lambda e: e.tensor_tensor(out=xw[:, m, :], in0=pb[:, :128], in1=xw[:, m, :], op=ALU.add), reads=[r_pb, r_xw], writes=[r_xw])
            cx.dma("sp", x_out_v[:, :, PAD + r0:PAD + r0 + 128], xw[:, :, :], reads=[r_xw], writes=[w["r_xout"]], slot=r_xw)
    cx.barrier()
    cx.es = old_es
    es.close()


def make_ssd_scratch(nc, T):
    d = lambda n, s, dt: nc.dram_tensor(n, s, dt, kind="Internal").ap()
    return dict(xtok=d("s_xtok", [T, DI], BF16), Btok=d("s_btok", [T, NG * DS], BF16), BT=d("s_BT", [NG * DS, T], BF16),
                CT=d("s_CT", [NG * DS, T], BF16), sz=d("s_sz", [T, DI], BF16), actok=d("s_actok", [T, 64], F32),
                dttok=d("s_dttok", [T, 64], F32), acT=d("s_acT", [64, T], F32), y0=d("s_y0", [T, DI], F32),
                r_tok=Res("s_tok"), r_BC=Res("s_BC"), r_sz=Res("s_sz"), r_ac=Res("s_ac"), r_y0=Res("s_y0"))


DEPTH = 4
ROT = 16


def build_program(T):
    TP = T + 2 * PAD
    nc = bass.Bass("TRN2", target_bir_lowering=False)
    di = lambda n, s: nc.dram_tensor(n, list(s), F32, kind="ExternalInput").ap()
    xin = di("xin", [D, TP])
    xout = nc.dram_tensor("xout", [D, TP], F32, kind="ExternalOutput").ap()
    xmid = nc.dram_tensor("xmid", [D, TP], F32, kind="Internal").ap()
    bm = di("bm", [128, 128]); rm = di("rm", [128, 128]); ident = di("ident", [128, 128])
    cos = di("cos", [128, T]); sin = di("sin", [128, T])
    A = {}
    for j in range(2):
        A[j] = dict(wqkv=di(f"a{j}_wqkv", [D, WQKV]), wo=di(f"a{j}_wo", [D, D]), gn=di(f"a{j}_gn", [128, KC]), gqk=di(f"a{j}_gqk", [128, 2]),
                    sink=di(f"a{j}_sink", [128, NH]))
    S = {}
    for j in range(2):
        S[j] = dict(win=di(f"s{j}_win", [D, SIN]), wout=di(f"s{j}_wout", [DI, D]), gn=di(f"s{j}_gn", [128, KC]), cwb=di(f"s{j}_cwb", [128, 32, 6]),
                    dtp=di(f"s{j}_dtp", [64, 2]), dbc=di(f"s{j}_dbc", [128, DI]), gbc=di(f"s{j}_gbc", [128, DI]))
    Fw = {}
    for i in range(DEPTH):
        Fw[i] = dict(wup=di(f"f{i}_wup", [D, 2 * DFF]), wdn=di(f"f{i}_wdn", [DFF, D]), cwb=di(f"f{i}_cwb", [128, NFC, 4]), gn=di(f"f{i}_gn", [128, KC]))
    scr = make_ssd_scratch(nc, T)
    with ExitStack() as es:
        cx = Ctx(nc, es)
        cm = Common(cx)
        r = {"xin": Res("xin"), "xout": Res("xout"), "xmid": Res("xmid")}
        aps = {"xin": xin, "xout": xout, "xmid": xmid}
        zes = ExitStack(); cx.es = zes
        zt = cx.sbuf("zt", [128, KC, PAD], F32); r_zt = Res("zt")
        cx.op("pool", lambda e: e.memset(zt[:], 0.0), writes=[r_zt])
        for nm in ("xout", "xmid"):
            v = aps[nm].rearrange("(k p) t -> p k t", p=128)
            cx.dma("sp", v[:, :, 0:PAD], zt[:], reads=[r_zt], writes=[r[nm]], slot=r_zt)
            cx.dma("sp", v[:, :, PAD + T:PAD + T + PAD], zt[:], reads=[r_zt], writes=[r[nm]], slot=r_zt)
        cx.barrier()
        cx.es = es
        zes.close()
        seq = ["xin"] + ["xmid", "xout"] * DEPTH
        step = 0
        for i in range(DEPTH):
            j = i // 2
            src, dst = seq[step], seq[step + 1]; step += 1
            if i % 2 == 0:
                w = dict(wqkv=A[j]["wqkv"].rearrange("(k p) n -> p k n", p=128), wo=A[j]["wo"].rearrange("(k p) n -> p k n", p=128),
                         gn=A[j]["gn"], gqk=A[j]["gqk"], sink=A[j]["sink"], bm=bm, rm=rm, cos=cos, sin=sin, r_xin=r[src], r_xout=r[dst])
                attn_layer(cx, cm, T, aps[src], aps[dst], w)
            else:
                w = dict(win=S[j]["win"].rearrange("(k p) n -> p k n", p=128), wout=S[j]["wout"].rearrange("(k p) n -> p k n", p=128),
                         gn=S[j]["gn"], cwb=S[j]["cwb"], dtp=S[j]["dtp"], dbc=S[j]["dbc"], gbc=S[j]["gbc"], ident=ident, r_xin=r[src], r_xout=r[dst])
                ssd_layer(cx, cm, T, aps[src], aps[dst], w, scr)
            src, dst = seq[step], seq[step + 1]; step += 1
            w = dict(wup=Fw[i]["wup"].rearrange("(k p) n -> p k n", p=128), wdn=Fw[i]["wdn"].rearrange("(c p) n -> p c n", p=128),
                     cwb=Fw[i]["cwb"], gn=Fw[i]["gn"], r_xin=r[src], r_xout=r[dst])
            ffn_layer(cx, cm, T, aps[src], aps[dst], w)
        assert dst == "xout"
        cx.barrier(fresh=False)
    return nc


def host_layout(inp, T):
    f = lambda a: np.ascontiguousarray(np.asarray(a, dtype=np.float32))
    col = lambda v: f(np.asarray(v).reshape(KC, 128).T)
    m = {}
    m["bm"] = f(np.kron(np.eye(2), np.full((64, 64), 1.0 / 64)))
    rm = np.zeros((128, 128), np.float32)
    for blk in (0, 64):
        for q in range(8):
            rm[blk + q + 8, blk + q] = -1.0
            rm[blk + q, blk + q + 8] = 1.0
    m["rm"] = rm
    m["ident"] = np.eye(128, dtype=np.float32)
    pos = np.arange(T, dtype=np.float32)
    inv_freq = (np.float32(500000.0) ** (-(np.arange(0, ROT, 2, dtype=np.float32) / np.float32(ROT)))).astype(np.float32)
    ang = (pos[:, None] * inv_freq[None, :]).astype(np.float32)
    cosv, sinv = np.cos(ang).astype(np.float32), np.sin(ang).astype(np.float32)
    cosT = np.ones((128, T), np.float32); sinT = np.zeros((128, T), np.float32)
    for blk in (0, 64):
        for q in range(16):
            cosT[blk + q] = cosv[:, q % 8]; sinT[blk + q] = sinv[:, q % 8]
    m["cos"], m["sin"] = cosT, sinT
    for j in range(2):
        wq = np.asarray(inp["attn_w_qkv"][j])
        m[f"a{j}_wqkv"] = f(np.concatenate([wq[:, :QD]] + [np.tile(wq[:, QD + k * 64:QD + (k + 1) * 64], (1, 2)) for k in range(NKV)] + [wq[:, QD + 256:]], axis=1))
        m[f"a{j}_wo"] = f(inp["attn_w_o"][j])
        m[f"a{j}_gn"] = col(inp["attn_norm"][j])
        m[f"a{j}_gqk"] = f(np.stack([np.tile(np.asarray(inp["attn_q_norm"][j]), 2), np.tile(np.asarray(inp["attn_k_norm"][j]), 2)], 1))
        m[f"a{j}_sink"] = f(np.tile(np.asarray(inp["attn_sink"][j])[None, :], (128, 1)))
        m[f"s{j}_win"] = f(inp["ssd_w_in"][j])
        m[f"s{j}_wout"] = f(inp["ssd_w_out"][j])
        m[f"s{j}_gn"] = col(inp["ssd_norm"][j])
        cwb = np.zeros((128, 32, 6), np.float32)
        cwb[:, :, 0:5] = np.asarray(inp["ssd_conv_w"][j]).reshape(5, 32, 128).transpose(2, 1, 0)
        cwb[:, :, 5] = np.asarray(inp["ssd_conv_b"][j]).reshape(32, 128).T
        m[f"s{j}_cwb"] = cwb
        m[f"s{j}_dtp"] = f(np.stack([np.asarray(inp["ssd_dt_bias"][j]).reshape(64), np.asarray(inp["ssd_a_log"][j]).reshape(64)], 1))
        m[f"s{j}_dbc"] = f(np.tile(np.repeat(np.asarray(inp["ssd_d"][j]), 64)[None, :], (128, 1)))
        m[f"s{j}_gbc"] = f(np.tile(np.asarray(inp["ssd_gate_norm"][j])[None, :], (128, 1)))
    for i in range(DEPTH):
        m[f"f{i}_wup"] = f(inp["ffn_w_up"][i])
        m[f"f{i}_wdn"] = f(inp["ffn_w_down"][i])
        cwb = np.zeros((128, NFC, 4), np.float32)
        cwb[:, :, 0:3] = np.asarray(inp["ffn_conv_w"][i]).reshape(3, NFC, 128).transpose(2, 1, 0)
        cwb[:, :, 3] = np.asarray(inp["ffn_conv_b"][i]).reshape(NFC, 128).T
        m[f"f{i}_cwb"] = cwb
        m[f"f{i}_gn"] = col(inp["ffn_norm"][i])
    return m


def run_module(inp, n_cores=8):
    x = np.asarray(inp["x"], dtype=np.float32)
    B, T, _ = x.shape
    nc = build_program(T)
    shared = host_layout(inp, T)
    in_maps = []
    for c in range(n_cores):
        b = c % B
        xin = np.zeros((D, T + 2 * PAD), np.float32)
        xin[:, PAD:PAD + T] = x[b].T
        mp = dict(shared); mp["xin"] = xin
        in_maps.append(mp)
    res = run_bass_kernel_spmd(nc, in_maps, core_ids=list(range(n_cores)))
    out = np.stack([np.ascontiguousarray(res.results[b]["xout"][:, PAD:PAD + T].T) for b in range(B)], 0)
    return out.astype(np.float32)


def kernel(**inputs):
    return run_module(inputs)
```

```python
import numpy as np
from contextlib import ExitStack
import concourse.bass as bass
import concourse.mybir as mybir
from concourse.bass_utils import run_bass_kernel_spmd

F32 = mybir.dt.float32
BF16 = mybir.dt.bfloat16
AF = mybir.ActivationFunctionType
ALU = mybir.AluOpType

D = 1024
KC = D // 128
DFF = 2816
NFC = 2 * DFF // 128
NGC = DFF // 128
EPS = 1e-6
PAD = 128


class Sem:
    def __init__(self, handle, name):
        self.h = handle
        self.name = name
        self.count = 0


class Res:
    def __init__(self, name, excl=False):
        self.name = name
        self.excl = excl
        self.last_w = None
        self.readers = {}
        self.dsem = None


class Ctx:
    ENG = ("pe", "act", "dve", "pool", "sp")

    def __init__(self, nc, es):
        self.nc = nc
        self.es = es
        self.top_es = es
        self.fill0 = nc.gpsimd.to_reg(0.0)
        self.fillneg = nc.gpsimd.to_reg(-30000.0)
        self.eng = {"pe": nc.tensor, "act": nc.scalar, "dve": nc.vector, "pool": nc.gpsimd, "sp": nc.sync}
        self.sems = {}
        self.waited = {e: {} for e in self.ENG}
        self.gen = 0
        self._new_sems()
        self.uid = 0

    def _new_sems(self):
        self.gen += 1
        for e in self.ENG:
            self.sems[e] = Sem(self.es.enter_context(self.nc.semaphore(f"s_{e}_{self.gen}")), e)
        if not hasattr(self, "dfree"):
            self.dfree, self.dused, self.nd, self.dfresh = [], [], 0, []

    def name(self, base):
        self.uid += 1
        return f"{base}_{self.uid}"

    def sbuf(self, name, shape, dtype):
        t = self.es.enter_context(self.nc.sbuf_tensor(self.name(name), list(shape), dtype))
        return t

    def psum(self, name, shape, dtype):
        t = self.es.enter_context(self.nc.psum_tensor(self.name(name), list(shape), dtype))
        return t

    def _deps(self, eng, reads, writes):
        deps = []
        for r in reads:
            if r.last_w is not None:
                s, v, e = r.last_w
                if not (e == eng and eng == "pe"):
                    deps.append((s, v))
            if r.excl:
                for e, (s, v) in r.readers.items():
                    if e != eng:
                        deps.append((s, v))
        for w in writes:
            if w.last_w is not None:
                s, v, e = w.last_w
                if e != eng or eng != "pe":
                    deps.append((s, v))
            for e, (s, v) in w.readers.items():
                if e != eng or eng != "pe":
                    deps.append((s, v))
        return deps

    def _wait(self, eng, deps):
        wd = self.waited[eng]
        best = {}
        for s, v in deps:
            if wd.get(s, 0) >= v:
                continue
            if best.get(s, 0) < v:
                best[s] = v
        for s, v in best.items():
            assert v <= s.count, f"wait on un-emitted signal {s.name} {v}>{s.count} (engine {eng})"
            self.eng[eng].wait_ge(s.h, v)
            wd[s] = v

    def _stamp(self, eng, stamp, reads, writes):
        s, v = stamp
        for r in reads:
            r.readers[eng] = (s, v)
        for w in writes:
            w.last_w = (s, v, eng)
            w.readers = {}

    def op(self, eng, fn, reads=(), writes=(), signal=True):
        reads = [r for r in reads if r is not None]
        writes = [w for w in writes if w is not None]
        self._wait(eng, self._deps(eng, reads, writes))
        ins = fn(self.eng[eng])
        s = self.sems[eng]
        if signal:
            s.count += 1
            ins.then_inc(s.h, 1)
            stamp = (s, s.count)
        else:
            stamp = (s, s.count + 1)
        self._stamp(eng, stamp, reads, writes)
        return ins

    def _dsem(self, res, fresh=False):
        if res.dsem is None and fresh:
            self.nd += 1
            res.dsem = Sem(self.top_es.enter_context(self.nc.semaphore(f"s_dma_{self.nd}")), f"dma{self.nd}")
            self.dfresh.append(res.dsem)
        if res.dsem is None:
            while self.dfree and self.dfree[-1].count > 30000:
                self.dfree.pop()
            if self.dfree:
                res.dsem = self.dfree.pop()
            else:
                self.nd += 1
                res.dsem = Sem(self.top_es.enter_context(self.nc.semaphore(f"s_dma_{self.nd}")), f"dma{self.nd}")
            self.dused.append(res.dsem)
        return res.dsem

    def dma(self, queue, out, in_, reads=(), writes=(), slot=None):
        reads = [r for r in reads if r is not None]
        writes = [w for w in writes if w is not None]
        self._wait(queue, self._deps("dma", reads, writes))
        ins = self.eng[queue].dma_start(out=out, in_=in_)
        s = self._dsem(slot, fresh=(queue == "pool"))
        s.count += 16
        ins.then_inc(s.h, 16)
        self._stamp("dma", (s, s.count), reads, writes)
        return ins

    def barrier(self, fresh=True):
        for e in self.ENG:
            deps = [(s, s.count) for s in list(self.sems.values()) + self.dused + self.dfresh if s.count > 0]
            self._wait(e, deps)
        self.dfree.extend(self.dused)
        self.dused = []
        self.dfresh = []
        if fresh:
            old = dict(self.sems)
            self._new_sems()
            self._old = old


class Common:
    def __init__(self, cx):
        nc = cx.nc
        self.ones_mean = cx.sbuf("ones_mean", [128, 128], F32)
        self.r_ones = Res("ones_mean")
        cx.op("pool", lambda e: e.memset(self.ones_mean[:], 1.0 / D), writes=[self.r_ones])
        self.eps = cx.sbuf("eps", [128, 1], F32)
        self.r_eps = Res("eps")
        cx.op("pool", lambda e: e.memset(self.eps[:], EPS), writes=[self.r_eps])
        self.one = cx.sbuf("one", [128, 1], F32)
        self.r_one = Res("one")
        cx.op("pool", lambda e: e.memset(self.one[:], 1.0), writes=[self.r_one])


def rmsnorm_fm(cx, cm, xw, r_xw, W, g_ap, r_g, hn, r_hn, ps, r_ps, sq, r_sq, rstd, r_rstd):
    for k in range(KC):
        b = k % 2
        cx.op("act", lambda e, k=k, b=b: e.activation(out=sq[b][:, :W], in_=xw[:, k, :W], func=AF.Square),
              reads=[r_xw], writes=[r_sq[b]])
        cx.op("pe", lambda e, k=k, b=b: e.matmul(ps[:, :W], cm.ones_mean[:], sq[b][:, :W],
                                                    start=(k == 0), stop=(k == KC - 1)),
              reads=[cm.r_ones, r_sq[b]], writes=[r_ps], signal=True)
    cx.op("act", lambda e: e.activation(out=rstd[:, :W], in_=ps[:, :W], func=AF.Ln, bias=cm.eps[:, 0:1]),
          reads=[r_ps, cm.r_eps], writes=[r_rstd])
    cx.op("act", lambda e: e.activation(out=rstd[:, :W], in_=rstd[:, :W], func=AF.Exp, scale=-0.5),
          reads=[r_rstd], writes=[r_rstd])
    for k in range(KC):
        cx.op("dve", lambda e, k=k: e.scalar_tensor_tensor(out=hn[:, k, :W], in0=xw[:, k, :W], scalar=g_ap(k),
                                                        in1=rstd[:, :W], op0=ALU.mult, op1=ALU.mult),
              reads=[r_xw, r_rstd, r_g], writes=[r_hn[k]])


def load_weight_bf16(cx, dst, r_dst, src_ap, nk, split=1):
    for k in range(nk):
        cx.dma("pool", dst[:, k, :], src_ap[:, k, :], writes=[r_dst], slot=r_dst)


def ffn_layer(cx, cm, T, x_in, x_out, w):
    nc = cx.nc
    es = ExitStack()
    old_es = cx.es
    cx.es = es
    wup = cx.sbuf("wup", [128, KC, 2 * DFF], BF16); r_wup = Res("wup")
    wdn = cx.sbuf("wdn", [128, NGC, D], BF16); r_wdn = Res("wdn")
    cwb = cx.sbuf("cwb", [128, NFC, 4], F32); r_cwb = Res("cwb")
    gn = cx.sbuf("gn", [128, KC], F32); r_gn = Res("gn")
    xws = [cx.sbuf(f"xw{i}", [128, KC, 512], F32) for i in range(2)]; r_xws = [Res("xw0"), Res("xw1")]
    hn = cx.sbuf("hn", [128, KC, 512], BF16); r_hn = [Res(f"hn{k}") for k in range(KC)]
    gT = cx.sbuf("gT", [128, NGC, 512], BF16); r_gT = [Res(f"gT{c}") for c in range(NGC)]
    rstd = cx.sbuf("rstd", [128, 512], F32); r_rstd = Res("rstd")
    NT = 2
    t1 = [[cx.sbuf(f"t1_{i}_{j}", [128, 512], F32) for j in range(2)] for i in range(NT)]
    r_t1 = [[Res(f"t1_{i}_{j}") for j in range(2)] for i in range(NT)]
    sq = [t1[0][0], t1[0][1]]; r_sq = [r_t1[0][0], r_t1[0][1]]
    ps_st = cx.psum("ps_st", [128, 512], F32); r_ps_st = Res("ps_st", excl=True)
    ps_gv = [[cx.psum(f"ps_gv{i}{j}", [128, 512], F32) for j in range(2)] for i in range(2)]
    r_ps_gv = [[Res(f"ps_gv{i}{j}", excl=True) for j in range(2)] for i in range(2)]
    ps_o = [cx.psum(f"ps_o{i}", [128, 512], F32) for i in range(2)]
    r_ps_o = [Res(f"ps_o{i}", excl=True) for i in range(2)]

    cx.dma("sp", cwb[:], w["cwb"], writes=[r_cwb], slot=r_cwb)
    cx.dma("sp", gn[:], w["gn"], writes=[r_gn], slot=r_gn)
    load_weight_bf16(cx, wup, r_wup, w["wup"], KC)
    load_weight_bf16(cx, wdn, r_wdn, w["wdn"], NGC)

    x_in_v = x_in.rearrange("(k p) t -> p k t", p=128)
    x_out_v = x_out.rearrange("(k p) t -> p k t", p=128)
    STEP = 510
    tiles = [(t0, min(STEP, T - t0)) for t0 in range(0, T, STEP)]

    def load(i):
        t0, n_out = tiles[i]
        W = n_out + 2
        cx.dma("sp", xws[i % 2][:, :, :W], x_in_v[:, :, PAD + t0 - 1:PAD + t0 - 1 + W], reads=[w["r_xin"]], writes=[r_xws[i % 2]], slot=r_xws[i % 2])

    def norm(i):
        t0, n_out = tiles[i]
        rmsnorm_fm(cx, cm, xws[i % 2], r_xws[i % 2], n_out + 2, lambda k: gn[:, k:k + 1], r_gn, hn, r_hn, ps_st, r_ps_st, sq, r_sq, rstd, r_rstd)

    load(0)
    if len(tiles) > 1:
        load(1)
    norm(0)
    pair_i = 0
    for ti, (t0, n_out) in enumerate(tiles):
        W = n_out + 2
        xw = xws[ti % 2]; r_xw = r_xws[ti % 2]
        for cg in range(NGC):
            pb = pair_i % 2
            tb = pair_i % NT
            pair_i += 1
            for j, c in enumerate((cg, cg + NGC)):
                ps = ps_gv[pb][j]; r_ps = r_ps_gv[pb][j]
                for k in range(KC):
                    cx.op("pe", lambda e, k=k, c=c, ps=ps: e.matmul(ps[:, :W], wup[:, k, c * 128:(c + 1) * 128], hn[:, k, :W],
                                                                   start=(k == 0), stop=(k == KC - 1)),
                          reads=[r_wup, r_hn[k]], writes=[r_ps], signal=(k == KC - 1))
            for j, c in enumerate((cg, cg + NGC)):
                ps = ps_gv[pb][j]; r_ps = r_ps_gv[pb][j]
                tt = t1[tb][j]; r_tt = r_t1[tb][j]
                cx.op("act", lambda e, c=c, ps=ps, tt=tt: e.activation(out=tt[:, :n_out], in_=ps[:, 1:1 + n_out], func=AF.Identity,
                                                                        bias=cwb[:, c, 3:4], scale=cwb[:, c, 1:2]),
                      reads=[r_ps, r_cwb], writes=[r_tt])
                cx.op("dve", lambda e, c=c, ps=ps, tt=tt: e.scalar_tensor_tensor(out=tt[:, :n_out], in0=ps[:, 0:n_out], scalar=cwb[:, c, 0:1],
                                                                               in1=tt[:, :n_out], op0=ALU.mult, op1=ALU.add),
                      reads=[r_ps, r_cwb, r_tt], writes=[r_tt])
                cx.op("dve", lambda e, c=c, ps=ps, tt=tt: e.scalar_tensor_tensor(out=tt[:, :n_out], in0=ps[:, 2:2 + n_out], scalar=cwb[:, c, 2:3],
                                                                               in1=tt[:, :n_out], op0=ALU.mult, op1=ALU.add),
                      reads=[r_ps, r_cwb, r_tt], writes=[r_tt])
            tg = t1[tb][0]; tv = t1[tb][1]
            cx.op("act", lambda e, tg=tg: e.activation(out=tg[:, :n_out], in_=tg[:, :n_out], func=AF.Silu),
                  reads=[r_t1[tb][0]], writes=[r_t1[tb][0]])
            cx.op("pool", lambda e, tg=tg, tv=tv, cg=cg: e.tensor_tensor(out=gT[:, cg, :n_out], in0=tg[:, :n_out], in1=tv[:, :n_out], op=ALU.mult),
                  reads=[r_t1[tb][0], r_t1[tb][1]], writes=[r_gT[cg]])
        if ti + 1 < len(tiles):
            norm(ti + 1)
        for m in range(KC):
            ob = m % 2
            for c in range(NGC):
                cx.op("pe", lambda e, c=c, m=m, ob=ob: e.matmul(ps_o[ob][:, :n_out], wdn[:, c, m * 128:(m + 1) * 128], gT[:, c, :n_out],
                                                                 start=(c == 0), stop=(c == NGC - 1)),
                      reads=[r_wdn, r_gT[c]], writes=[r_ps_o[ob]], signal=(c == NGC - 1))
            cx.op("dve", lambda e, m=m, ob=ob: e.tensor_tensor(out=xw[:, m, 1:1 + n_out], in0=ps_o[ob][:, :n_out], in1=xw[:, m, 1:1 + n_out], op=ALU.add),
                  reads=[r_ps_o[ob], r_xw], writes=[r_xw])
        cx.dma("sp", x_out_v[:, :, PAD + t0:PAD + t0 + n_out], xw[:, :, 1:1 + n_out], reads=[r_xw], writes=[w["r_xout"]], slot=r_xw)
        if ti + 2 < len(tiles):
            load(ti + 2)
    cx.barrier()
    cx.es = old_es
    es.close()


NH, NKV, HD = 16, 4, 64
QD = NH * HD
WQKV = QD + 2 * NKV * HD + NKV * HD


def attn_layer(cx, cm, T, x_in, x_out, w):
    es = ExitStack(); old_es = cx.es; cx.es = es
    NB = T // 128
    wq = cx.sbuf("wqkv", [128, KC, WQKV], BF16); r_wq = Res("wqkv")
    wo = cx.sbuf("wo", [128, KC, D], BF16); r_wo = Res("wo")
    gn = cx.sbuf("gn", [128, KC], F32); r_gn = Res("gn")
    gqk = cx.sbuf("gqk", [128, 2], F32); r_gqk = Res("gqk")
    esk = cx.sbuf("esk", [128, NH], F32); r_esk = Res("esk")
    bm = cx.sbuf("bm", [128, 128], F32); r_bm = Res("bm")
    rmf = cx.sbuf("rmf", [128, 128], F32); r_rmf = Res("rmf")
    rm = cx.sbuf("rm", [128, 128], BF16); r_rm = Res("rm")
    onesb = cx.sbuf("onesb", [128, 128], BF16); r_onesb = Res("onesb")
    kT = cx.sbuf("kT", [128, NKV, T], BF16); r_kT = [Res(f"kT{i}") for i in range(T // 512)]
    V = cx.sbuf("V", [128, NB, NKV * HD], BF16); r_V = [Res(f"V{i}") for i in range(NB)]
    vv = cx.sbuf("vv", [128, 4, NKV, 2, HD], BF16); r_vv = [Res(f"vv{i}") for i in range(4)]
    xw = cx.sbuf("xw", [128, KC, 512], F32); r_xw = Res("xw")
    hn = cx.sbuf("hn", [128, KC, 512], BF16); r_hn = [Res(f"hn{k}") for k in range(KC)]
    qT = cx.sbuf("qT", [128, KC, 512], BF16); r_qT = [Res(f"qT{k}") for k in range(KC)]
    oT = hn; r_oT = r_hn
    cs = cx.sbuf("cs", [128, 2, 512], F32); r_cs = Res("cs")
    sq = [cx.sbuf(f"sq{i}", [128, 512], F32) for i in range(2)]; r_sq = [Res("sq0"), Res("sq1")]
    rstd = cx.sbuf("rstd", [128, 512], F32); r_rstd = Res("rstd")
    qnb = cx.sbuf("qnb", [128, 512], BF16); r_qnb = Res("qnb")
    ta = cx.sbuf("ta", [128, 512], F32); r_ta = Res("ta")
    tb = cx.sbuf("tb", [128, 512], F32); r_tb = Res("tb")
    PT = [cx.sbuf(f"PT{i}", [128, 512], BF16) for i in range(4)]; r_PT = [Res(f"PT{i}") for i in range(4)]
    rd = cx.sbuf("rd", [128, 512], F32); r_rd = Res("rd")
    ps_p1 = cx.psum("ps_p0", [128, 512], F32); r_ps_p1 = Res("ps_p0", excl=True)
    ps_p = [ps_p1, ps_p1]; r_ps_p = [r_ps_p1, r_ps_p1]
    ps_a = cx.psum("ps_a", [128, 512], F32); r_ps_a = Res("ps_a", excl=True)
    ps_b = ps_a; r_ps_b = r_ps_a
    ps_s = [[cx.psum(f"ps_s{i}{hf}", [128, 512], F32) for hf in range(2)] for i in range(2)]
    r_ps_s = [[Res(f"ps_s{i}{hf}", excl=True) for hf in range(2)] for i in range(2)]
    ps_o = cx.psum("ps_o", [128, 512], F32); r_ps_o = Res("ps_o", excl=True)
    ps_d = cx.psum("ps_d", [128, 512], F32); r_ps_d = Res("ps_d", excl=True)

    for dst, r, key in ((gn, r_gn, "gn"), (gqk, r_gqk, "gqk"), (esk, r_esk, "sink"), (bm, r_bm, "bm"), (rmf, r_rmf, "rm")):
        cx.dma("sp", dst[:], w[key], writes=[r], slot=r)
    load_weight_bf16(cx, wq, r_wq, w["wqkv"], KC)
    load_weight_bf16(cx, wo, r_wo, w["wo"], KC)
    cx.op("act", lambda e: e.activation(out=esk[:], in_=esk[:], func=AF.Exp), reads=[r_esk], writes=[r_esk])
    cx.op("dve", lambda e: e.tensor_copy(rm[:], rmf[:]), reads=[r_rmf], writes=[r_rm])
    cx.op("pool", lambda e: e.memset(onesb[:], 1.0), writes=[r_onesb])

    x_in_v = x_in.rearrange("(k p) t -> p k t", p=128)
    x_out_v = x_out.rearrange("(k p) t -> p k t", p=128)
    pp = [0]

    def load_norm(t0):
        cx.dma("sp", xw[:, :, :], x_in_v[:, :, PAD + t0:PAD + t0 + 512], reads=[w["r_xin"]], writes=[r_xw], slot=r_xw)
        cx.dma("sp", cs[:, 0, :], w["cos"][:, t0:t0 + 512], writes=[r_cs], slot=r_cs)
        cx.dma("sp", cs[:, 1, :], w["sin"][:, t0:t0 + 512], writes=[r_cs], slot=r_cs)
        rmsnorm_fm(cx, cm, xw, r_xw, 512, lambda k: gn[:, k:k + 1], r_gn, hn, r_hn, ps_a, r_ps_a, sq, r_sq, rstd, r_rstd)

    def proj_fm(col0):
        b = pp[0] % 2; pp[0] += 1
        for k in range(KC):
            cx.op("pe", lambda e, k=k: e.matmul(ps_p[b][:, :], wq[:, k, col0:col0 + 128], hn[:, k, :],
                                                start=(k == 0), stop=(k == KC - 1)),
                  reads=[r_wq, r_hn[k]], writes=[r_ps_p[b]], signal=(k == KC - 1))
        return ps_p[b], r_ps_p[b]

    def headnorm_rope(ps, r_ps, gcol, out_ap, r_out):
        cx.op("act", lambda e: e.activation(out=sq[0][:, :], in_=ps[:, :], func=AF.Square), reads=[r_ps], writes=[r_sq[0]])
        cx.op("pe", lambda e: e.matmul(ps_a[:, :], bm[:], sq[0][:, :], start=True, stop=True),
              reads=[r_bm, r_sq[0]], writes=[r_ps_a])
        cx.op("act", lambda e: e.activation(out=rstd[:, :], in_=ps_a[:, :], func=AF.Ln, bias=cm.eps[:, 0:1]),
              reads=[r_ps_a, cm.r_eps], writes=[r_rstd])
        cx.op("act", lambda e: e.activation(out=rstd[:, :], in_=rstd[:, :], func=AF.Exp, scale=-0.5),
              reads=[r_rstd], writes=[r_rstd])
        cx.op("dve", lambda e: e.scalar_tensor_tensor(out=qnb[:, :], in0=ps[:, :], scalar=gqk[:, gcol:gcol + 1], in1=rstd[:, :],
                                                      op0=ALU.mult, op1=ALU.mult),
              reads=[r_ps, r_gqk, r_rstd], writes=[r_qnb])
        cx.op("pe", lambda e: e.matmul(ps_b[:, :], rm[:], qnb[:, :], start=True, stop=True),
              reads=[r_rm, r_qnb], writes=[r_ps_b])
        cx.op("dve", lambda e: e.tensor_tensor(out=ta[:, :], in0=qnb[:, :], in1=cs[:, 0, :], op=ALU.mult),
              reads=[r_qnb, r_cs], writes=[r_ta])
        cx.op("dve", lambda e: e.tensor_tensor(out=tb[:, :], in0=ps_b[:, :], in1=cs[:, 1, :], op=ALU.mult),
              reads=[r_ps_b, r_cs], writes=[r_tb])
        cx.op("pool", lambda e: e.tensor_tensor(out=out_ap, in0=ta[:, :], in1=tb[:, :], op=ALU.add),
              reads=[r_ta, r_tb], writes=[r_out])

    for ti in range(T // 512):
        t0 = ti * 512
        load_norm(t0)
        for j in range(NKV):
            ps, r_ps = proj_fm(QD + j * 128)
            headnorm_rope(ps, r_ps, 1, kT[:, j, t0:t0 + 512], r_kT[ti])
        for bl in range(4):
            b = pp[0] % 2; pp[0] += 1
            for k in range(KC):
                cx.op("pe", lambda e, k=k: e.matmul(ps_p[b][:, :256], hn[:, k, bl * 128:(bl + 1) * 128], wq[:, k, QD + 512:QD + 768],
                                                    start=(k == 0), stop=(k == KC - 1)),
                      reads=[r_wq, r_hn[k]], writes=[r_ps_p[b]], signal=(k == KC - 1))
            cx.op("act", lambda e: e.copy(out=V[:, ti * 4 + bl, :], in_=ps_p[b][:, :256]), reads=[r_ps_p[b]], writes=[r_V[ti * 4 + bl]])

    vv_have = {}
    sc = [0]
    for ti in range(T // 512):
        t0 = ti * 512
        load_norm(t0)
        for c in range(KC):
            ps, r_ps = proj_fm(c * 128)
            headnorm_rope(ps, r_ps, 0, qT[:, c, :], r_qT[c])
        for bl in range(4):
            nb = ti * 4 + bl
            qs = slice(bl * 128, (bl + 1) * 128)
            kbs = [kb for kb in (nb - 1, nb, nb + 1) if 0 <= kb < NB]
            for kb in kbs:
                if vv_have.get(kb % 4) != kb:
                    for d2 in range(2):
                        cx.op("pool", lambda e, d2=d2: e.tensor_copy(vv[:, kb % 4, :, d2, :], V[:, kb, :].rearrange("p (j d) -> p j d", j=NKV)),
                              reads=[r_V[kb]], writes=[r_vv[kb % 4]])
                    vv_have[kb % 4] = kb
            for j in range(NKV):
                pts = []
                for kb in kbs:
                    sb = sc[0] % 2; pb = sc[0] % 4; sc[0] += 1
                    for half in range(2):
                        rows = slice(half * 64, half * 64 + 64)
                        cx.op("pe", lambda e: e.matmul(ps_s[sb][half][:, :256], kT[rows, j, kb * 128:(kb + 1) * 128],
                                                       qT[rows, 2 * j:2 * j + 2, qs], start=True, stop=True),
                              reads=[r_kT[kb // 4], r_qT[2 * j], r_qT[2 * j + 1]], writes=[r_ps_s[sb][half]])
                    for half in range(2):
                        cx.op("act", lambda e: e.activation(out=PT[pb][:, half * 256:(half + 1) * 256], in_=ps_s[sb][half][:, :256], func=AF.Exp, scale=HD ** -0.5),
                              reads=[r_ps_s[sb][half]], writes=[r_PT[pb]])
                    if kb != nb:
                        sgn = 1 if kb < nb else -1
                        cx.op("pool", lambda e: e.affine_select(out=PT[pb][:, :].rearrange("p (a q) -> p a q", a=4),
                                                                in_=PT[pb][:, :].rearrange("p (a q) -> p a q", a=4),
                                                                pattern=[[0, 4], [-sgn, 128]], compare_op=ALU.is_ge, fill=cx.fill0,
                                                                base=0, channel_multiplier=sgn),
                              reads=[r_PT[pb]], writes=[r_PT[pb]])
                    pts.append((kb, pb))
                for i, (kb, pb) in enumerate(pts):
                    cx.op("pe", lambda e: e.matmul(ps_o[:, :], vv[:, kb % 4, j, :, :], PT[pb][:, :], start=(i == 0), stop=(i == len(pts) - 1)),
                          reads=[r_vv[kb % 4], r_PT[pb]], writes=[r_ps_o], signal=(i == len(pts) - 1))
                for i, (kb, pb) in enumerate(pts):
                    cx.op("pe", lambda e: e.matmul(ps_d[:, :], onesb[:], PT[pb][:, :], start=(i == 0), stop=(i == len(pts) - 1)),
                          reads=[r_onesb, r_PT[pb]], writes=[r_ps_d], signal=(i == len(pts) - 1))
                heads = (4 * j, 4 * j + 2, 4 * j + 1, 4 * j + 3)
                for a, h in enumerate(heads):
                    cx.op("dve", lambda e: e.tensor_scalar(out=rd[:, a * 128:(a + 1) * 128], in0=ps_d[:, a * 128:(a + 1) * 128],
                                                           scalar1=esk[:, h:h + 1], scalar2=None, op0=ALU.add),
                          reads=[r_ps_d, r_esk], writes=[r_rd])
                cx.op("act", lambda e: e.activation(out=rd[:, :], in_=rd[:, :], func=AF.Ln), reads=[r_rd], writes=[r_rd])
                cx.op("act", lambda e: e.activation(out=rd[:, :], in_=rd[:, :], func=AF.Exp, scale=-1.0), reads=[r_rd], writes=[r_rd])
                for half in range(2):
                    rows = slice(half * 64, half * 64 + 64)
                    cx.op("dve", lambda e: e.tensor_tensor(out=oT[rows, 2 * j:2 * j + 2, qs],
                                                           in0=ps_o[rows, half * 256:(half + 1) * 256].rearrange("p (a q) -> p a q", a=2),
                                                           in1=rd[rows, half * 256:(half + 1) * 256].rearrange("p (a q) -> p a q", a=2), op=ALU.mult),
                          reads=[r_ps_o, r_rd], writes=[r_oT[2 * j], r_oT[2 * j + 1]])
        for m in range(KC):
            b = pp[0] % 2; pp[0] += 1
            for c in range(KC):
                cx.op("pe", lambda e, c=c: e.matmul(ps_p[b][:, :], wo[:, c, m * 128:(m + 1) * 128], oT[:, c, :],
                                                    start=(c == 0), stop=(c == KC - 1)),
                      reads=[r_wo, r_oT[c]], writes=[r_ps_p[b]], signal=(c == KC - 1))
            cx.op("dve", lambda e: e.tensor_tensor(out=xw[:, m, :], in0=ps_p[b][:, :], in1=xw[:, m, :], op=ALU.add),
                  reads=[r_ps_p[b], r_xw], writes=[r_xw])
        cx.dma("sp", x_out_v[:, :, PAD + t0:PAD + t0 + 512], xw[:, :, :], reads=[r_xw], writes=[w["r_xout"]], slot=r_xw)
    cx.barrier()
    cx.es = old_es
    es.close()


DI, NHS, NG, DS = 2048, 32, 8, 128
CONVD = DI + 2 * NG * DS
SIN = DI + CONVD + 2 * NHS


def ssd_layer(cx, cm, T, x_in, x_out, w, scr):
    NCH = T // 128
    x_in_v = x_in.rearrange("(k p) t -> p k t", p=128)
    x_out_v = x_out.rearrange("(k p) t -> p k t", p=128)
    es = ExitStack(); old_es = cx.es; cx.es = es
    win = cx.sbuf("win", [128, KC, SIN], BF16); r_win = Res("win")
    cwb = cx.sbuf("cwb", [128, 32, 6], F32); r_cwb = Res("cwb")
    gn = cx.sbuf("gn", [128, KC], F32); r_gn = Res("gn")
    dtp = cx.sbuf("dtp", [64, 2], F32); r_dtp = Res("dtp")
    idb = cx.sbuf("idb", [128, 128], BF16); r_idb = Res("idb")
    idf = cx.sbuf("idf", [128, 128], F32); r_idf = Res("idf")
    xws = [cx.sbuf(f"xw{i}", [128, KC, 512], F32) for i in range(2)]; r_xws = [Res("xw0"), Res("xw1")]
    hns = [cx.sbuf(f"hn{i}", [128, KC, 512], BF16) for i in range(2)]
    r_hns = [[Res(f"hn{i}_{k}") for k in range(KC)] for i in range(2)]
    sq = [cx.sbuf(f"sq{i}", [128, 512], F32) for i in range(2)]; r_sq = [Res("sq0"), Res("sq1")]
    rstd = cx.sbuf("rstd", [128, 512], F32); r_rstd = Res("rstd")
    tcv = [cx.sbuf(f"tcv{i}", [128, 384], F32) for i in range(2)]; r_tcv = [Res(f"tcv{i}") for i in range(2)]
    xbc = [cx.sbuf(f"xbc{i}", [128, 384], BF16) for i in range(2)]; r_xbc = [Res(f"xbc{i}") for i in range(2)]
    tok = cx.sbuf("tok", [128, 3, DI + NG * DS], BF16); r_tok = Res("tok")
    szt = cx.sbuf("szt", [128, DI], BF16); r_szt = Res("szt")
    aT = cx.sbuf("aT", [64, 384], F32); r_aT = Res("aT")
    dT = cx.sbuf("dT", [64, 384], F32); r_dT = Res("dT")
    acT = cx.sbuf("acT", [64, 384], F32); r_acT = Res("acT")
    onesf = cx.sbuf("onesf", [64, 384], F32); r_onesf = Res("onesf")
    tk2 = cx.sbuf("tk2", [128, 2, 64], F32); r_tk2 = Res("tk2")
    tot1 = cx.sbuf("tot1", [64, 1], F32); r_tot1 = Res("tot1")
    ps_st = cx.psum("ps_st", [128, 512], F32); r_ps_st = Res("ps_st", excl=True)
    ps_c = [cx.psum(f"ps_c{i}", [128, 512], F32) for i in range(2)]; r_ps_c = [Res(f"ps_c{i}", excl=True) for i in range(2)]
    ps_t = [cx.psum(f"ps_t{i}", [128, 1024], BF16) for i in range(2)]; r_ps_t = [Res(f"ps_t{i}", excl=True) for i in range(2)]
    ps_z = [cx.psum(f"ps_z{i}", [128, 512], F32) for i in range(2)]; r_ps_z = [Res(f"ps_z{i}", excl=True) for i in range(2)]
    ps_f = cx.psum("ps_f", [128, 512], F32); r_ps_f = Res("ps_f", excl=True)

    for dst, r, key in ((cwb, r_cwb, "cwb"), (gn, r_gn, "gn"), (dtp, r_dtp, "dtp"), (idf, r_idf, "ident")):
        cx.dma("sp", dst[:], w[key], writes=[r], slot=r)
    load_weight_bf16(cx, win, r_win, w["win"], KC)
    cx.op("dve", lambda e: e.tensor_copy(idb[:], idf[:]), reads=[r_idf], writes=[r_idb])
    cx.op("pool", lambda e: e.memset(onesf[:], 1.0), writes=[r_onesf])
    cx.op("act", lambda e: e.activation(out=dtp[:, 1:2], in_=dtp[:, 1:2], func=AF.Exp), reads=[r_dtp], writes=[r_dtp])
    cx.op("dve", lambda e: e.tensor_scalar(out=dtp[:, 1:2], in0=dtp[:, 1:2], scalar1=-1.0, scalar2=None, op0=ALU.mult),
          reads=[r_dtp], writes=[r_dtp])
    cc = [0]
    wins = [(t0, min(384, T - t0)) for t0 in range(0, T, 384)]

    def a_load(i):
        t0, n_out = wins[i]; W = n_out + 4
        cx.dma("sp", xws[i % 2][:, :, :W], x_in_v[:, :, PAD + t0 - 2:PAD + t0 - 2 + W], reads=[w["r_xin"]], writes=[r_xws[i % 2]], slot=r_xws[i % 2])

    def a_norm(i):
        t0, n_out = wins[i]
        rmsnorm_fm(cx, cm, xws[i % 2], r_xws[i % 2], n_out + 4, lambda k: gn[:, k:k + 1], r_gn, hns[i % 2], r_hns[i % 2], ps_st, r_ps_st, sq, r_sq, rstd, r_rstd)

    a_load(0)
    if len(wins) > 1:
        a_load(1)
    a_norm(0)
    for wi, (t0, n_out) in enumerate(wins):
        W = n_out + 4; nblk = n_out // 128
        hn = hns[wi % 2]; r_hn = r_hns[wi % 2]
        for c in range(32):
            if c == 20 and wi + 1 < len(wins):
                a_norm(wi + 1)
                if wi + 2 < len(wins):
                    a_load(wi + 2)
            b = cc[0] % 2; cc[0] += 1
            col0 = DI + c * 128
            for k in range(KC):
                cx.op("pe", lambda e, k=k: e.matmul(ps_c[b][:, :W], win[:, k, col0:col0 + 128], hn[:, k, :W], start=(k == 0), stop=(k == KC - 1)),
                      reads=[r_win, r_hn[k]], writes=[r_ps_c[b]], signal=(k == KC - 1))
            cx.op("act", lambda e: e.activation(out=tcv[b][:, :n_out], in_=ps_c[b][:, 2:2 + n_out], func=AF.Identity,
                                                bias=cwb[:, c, 5:6], scale=cwb[:, c, 2:3]),
                  reads=[r_ps_c[b], r_cwb], writes=[r_tcv[b]])
            for kk in (0, 1, 3, 4):
                cx.op("dve", lambda e, kk=kk: e.scalar_tensor_tensor(out=tcv[b][:, :n_out], in0=ps_c[b][:, kk:kk + n_out], scalar=cwb[:, c, kk:kk + 1],
                                                                     in1=tcv[b][:, :n_out], op0=ALU.mult, op1=ALU.add),
                      reads=[r_ps_c[b], r_cwb, r_tcv[b]], writes=[r_tcv[b]])
            cx.op("act", lambda e: e.activation(out=xbc[b][:, :n_out], in_=tcv[b][:, :n_out], func=AF.Silu), reads=[r_tcv[b]], writes=[r_xbc[b]])
            if c < 24:
                for bl in range(nblk):
                    cx.op("pe", lambda e, bl=bl: e.transpose(ps_t[b][:, bl * 128:(bl + 1) * 128], xbc[b][:, bl * 128:(bl + 1) * 128], idb[:]),
                          reads=[r_xbc[b], r_idb], writes=[r_ps_t[b]])
                cx.op("dve", lambda e: e.tensor_copy(tok[:, :nblk, c * 128:(c + 1) * 128], ps_t[b][:, :nblk * 128].rearrange("p (a q) -> p a q", a=nblk)),
                      reads=[r_ps_t[b]], writes=[r_tok])
            if c >= 16:
                dst = scr["BT"] if c < 24 else scr["CT"]
                g = (c - 16) % 8
                cx.dma("sp", dst[g * 128:(g + 1) * 128, t0:t0 + n_out], xbc[b][:, :n_out], reads=[r_xbc[b]], writes=[scr["r_BC"]], slot=r_xbc[b])
        for bl in range(nblk):
            r0 = t0 + bl * 128
            cx.dma("sp", scr["xtok"][r0:r0 + 128, :], tok[:, bl, :DI], reads=[r_tok], writes=[scr["r_tok"]], slot=r_tok)
            cx.dma("sp", scr["Btok"][r0:r0 + 128, :], tok[:, bl, DI:], reads=[r_tok], writes=[scr["r_tok"]], slot=r_tok)
        for bl in range(nblk):
            r0 = t0 + bl * 128
            for ct in range(4):
                b = cc[0] % 2; cc[0] += 1
                for k in range(KC):
                    cx.op("pe", lambda e, k=k: e.matmul(ps_z[b][:, :], hn[:, k, 2 + bl * 128:2 + (bl + 1) * 128], win[:, k, ct * 512:(ct + 1) * 512],
                                                        start=(k == 0), stop=(k == KC - 1)),
                          reads=[r_win, r_hn[k]], writes=[r_ps_z[b]], signal=(k == KC - 1))
                cx.op("act", lambda e: e.activation(out=szt[:, ct * 512:(ct + 1) * 512], in_=ps_z[b][:, :], func=AF.Silu),
                      reads=[r_ps_z[b]], writes=[r_szt])
            cx.dma("sp", scr["sz"][r0:r0 + 128, :], szt[:, :], reads=[r_szt], writes=[scr["r_sz"]], slot=r_szt)
        for k in range(KC):
            cx.op("pe", lambda e, k=k: e.matmul(ps_f[:64, :n_out], win[:, k, DI + CONVD:SIN], hn[:, k, 2:2 + n_out], start=(k == 0), stop=(k == KC - 1)),
                  reads=[r_win, r_hn[k]], writes=[r_ps_f], signal=(k == KC - 1))
        cx.op("act", lambda e: e.activation(out=dT[:, :n_out], in_=ps_f[:64, :n_out], func=AF.Exp, bias=dtp[:, 0:1]),
              reads=[r_ps_f, r_dtp], writes=[r_dT])
        cx.op("act", lambda e: e.activation(out=dT[:, :n_out], in_=dT[:, :n_out], func=AF.Ln, bias=cm.one[:64, 0:1]),
              reads=[r_dT, cm.r_one], writes=[r_dT])
        cx.op("dve", lambda e: e.tensor_scalar(out=aT[:, :n_out], in0=dT[:, :n_out], scalar1=dtp[:, 1:2], scalar2=None, op0=ALU.mult),
              reads=[r_dT, r_dtp], writes=[r_aT])
        for bl in range(nblk):
            cs_ = slice(bl * 128, (bl + 1) * 128)
            cx.op("dve", lambda e: e.tensor_tensor_scan(out=acT[:, cs_], data0=onesf[:, cs_], data1=aT[:, cs_], initial=0.0, op0=ALU.mult, op1=ALU.add),
                  reads=[r_onesf, r_aT], writes=[r_acT])
            cx.op("dve", lambda e: e.tensor_copy(tot1[32:64, :], acT[32:64, bl * 128 + 127:bl * 128 + 128]), reads=[r_acT], writes=[r_tot1])
            cx.op("dve", lambda e: e.scalar_tensor_tensor(out=acT[32:64, cs_], in0=acT[32:64, cs_], scalar=-1.0, in1=aT[32:64, cs_], op0=ALU.mult, op1=ALU.add),
                  reads=[r_acT, r_aT], writes=[r_acT])
            cx.op("dve", lambda e: e.tensor_scalar(out=acT[32:64, cs_], in0=acT[32:64, cs_], scalar1=tot1[32:64, 0:1], scalar2=None, op0=ALU.add),
                  reads=[r_acT, r_tot1], writes=[r_acT])
            r0 = t0 + bl * 128
            for i, src in enumerate((acT, dT)):
                cx.op("pe", lambda e, src=src: e.transpose(ps_f[:, 128 + i * 64:128 + (i + 1) * 64], src[:, cs_], idf[:64, :64]),
                      reads=[r_acT, r_dT, r_idf], writes=[r_ps_f])
            cx.op("act", lambda e: e.copy(out=tk2[:, :, :], in_=ps_f[:, 128:256].rearrange("p (a q) -> p a q", a=2)), reads=[r_ps_f], writes=[r_tk2])
            cx.dma("sp", scr["actok"][r0:r0 + 128, :], tk2[:, 0, :], reads=[r_tk2], writes=[scr["r_ac"]], slot=r_tk2)
            cx.dma("sp", scr["dttok"][r0:r0 + 128, :], tk2[:, 1, :], reads=[r_tk2], writes=[scr["r_ac"]], slot=r_tk2)
        cx.dma("sp", scr["acT"][:, t0:t0 + n_out], acT[:, :n_out], reads=[r_acT], writes=[scr["r_ac"]], slot=r_acT)
    cx.barrier()
    cx.es = old_es
    es.close()
    ssd_phase_b(cx, cm, T, x_in_v, x_out_v, w, scr)


def ssd_phase_b(cx, cm, T, x_in_v, x_out_v, w, scr):
    NCH = T // 128
    es = ExitStack(); old_es = cx.es; cx.es = es
    wout = cx.sbuf("wout", [128, 16, D], BF16); r_wout = Res("wout")
    dbc = cx.sbuf("dbc", [128, DI], F32); r_dbc = Res("dbc")
    gbc = cx.sbuf("gbc", [128, DI], F32); r_gbc = Res("gbc")
    idb = cx.sbuf("idb", [128, 128], BF16); r_idb = Res("idb")
    idf = cx.sbuf("idf", [128, 128], F32); r_idf = Res("idf")
    def two(name, shape, dt):
        return [cx.sbuf(f"{name}{i}", shape, dt) for i in range(2)], [Res(f"{name}{i}") for i in range(2)]
    xts, r_xts = two("xt", [128, DI], BF16)
    bts, r_bts = two("bt", [128, NG * DS], BF16)
    BTss, r_BTss = two("BTs", [128, NG, 128], BF16)
    CTss, r_CTss = two("CTs", [128, NG, 128], BF16)
    dtts, r_dtts = two("dtt", [128, 64], F32)
    acts, r_acts = two("act", [128, 64], F32)
    totbs, r_totbs = two("totb", [128, 32], F32)
    rowbs, r_rowbs = two("rowb", [128, 32, 128], F32)
    sm = cx.sbuf("sm", [128, 4, 32], F32); r_sm = Res("sm")
    cbt = cx.sbuf("cbt", [128, NG, 128], F32); r_cbt = [Res(f"cbt{g}") for g in range(NG)]
    dif = [cx.sbuf(f"dif{i}", [128, 4, 128], F32) for i in range(2)]; r_dif = [Res(f"dif{i}") for i in range(2)]
    MT = [cx.sbuf(f"MT{i}", [128, 4, 128], BF16) for i in range(2)]; r_MT = [Res(f"MT{i}") for i in range(2)]
    Bw = [cx.sbuf(f"Bw{i}", [128, 4, 128], BF16) for i in range(2)]; r_Bw = [Res(f"Bw{i}") for i in range(2)]
    HT = cx.sbuf("HT", [128, NHS, 64], F32); r_HT = [Res(f"HT{g}") for g in range(NG)]
    HTb = cx.sbuf("HTb", [128, NHS, 64], BF16); r_HTb = [Res(f"HTb{g}") for g in range(NG)]
    xdt = cx.sbuf("xdt", [128, DI], BF16); r_xdt = Res("xdt")
    yof = cx.sbuf("yof", [128, 512], F32); r_yof = Res("yof")
    dsk = cx.sbuf("dsk", [128, DI], F32); r_dsk = Res("dsk")
    ysb = cx.sbuf("ysb", [128, DI], F32); r_ysb = [Res(f"ysb{q}") for q in range(4)]
    y0 = cx.sbuf("y0", [128, DI], F32); r_y0 = Res("y0")
    szt = cx.sbuf("szt", [128, DI], BF16); r_szt = Res("szt")
    gss = cx.sbuf("gss", [128, NG], F32); r_gss = Res("gss")
    yb = cx.sbuf("yb", [128, DI], BF16); r_yb = Res("yb")
    yT = cx.sbuf("yT", [128, 16, 128], BF16); r_yT = Res("yT")
    xw = cx.sbuf("xw", [128, KC, 128], F32); r_xw = Res("xw")
    ps_cb = cx.psum("ps_cb", [128, 512], F32); r_ps_cb = Res("ps_cb", excl=True)
    ps_y = [cx.psum(f"ps_y{i}", [128, 512], F32) for i in range(2)]; r_ps_y = [Res(f"ps_y{i}", excl=True) for i in range(2)]
    ps_o = [cx.psum(f"ps_of{i}", [128, 512], F32) for i in range(2)]; r_ps_o = [Res(f"ps_of{i}", excl=True) for i in range(2)]
    ps_s = [cx.psum(f"ps_st{i}", [128, 512], F32) for i in range(2)]; r_ps_s = [Res(f"ps_st{i}", excl=True) for i in range(2)]
    ps_t = cx.psum("ps_tr", [128, 1024], BF16); r_ps_t = Res("ps_tr", excl=True)

    for dst, r, key in ((dbc, r_dbc, "dbc"), (gbc, r_gbc, "gbc"), (idf, r_idf, "ident")):
        cx.dma("sp", dst[:], w[key], writes=[r], slot=r)
    load_weight_bf16(cx, wout, r_wout, w["wout"], 16)
    cx.op("dve", lambda e: e.tensor_copy(idb[:], idf[:]), reads=[r_idf], writes=[r_idb])
    BTv = scr["BT"].rearrange("(g p) t -> p g t", p=128)
    CTv = scr["CT"].rearrange("(g p) t -> p g t", p=128)
    hc = [0]
    iters = [(0, c) for c in range(NCH)] + [(1, c) for c in range(NCH - 1, -1, -1)]

    def load_inputs(i):
        d, c = iters[i]; p = i % 2
        r0 = c * 128
        dc = slice(d * 32, d * 32 + 32)
        cx.dma("sp", xts[p][:], scr["xtok"][r0:r0 + 128, :], reads=[scr["r_tok"]], writes=[r_xts[p]], slot=r_xts[p])
        cx.dma("sp", bts[p][:], scr["Btok"][r0:r0 + 128, :], reads=[scr["r_tok"]], writes=[r_bts[p]], slot=r_bts[p])
        cx.dma("sp", BTss[p][:], BTv[:, :, r0:r0 + 128], reads=[scr["r_BC"]], writes=[r_BTss[p]], slot=r_BTss[p])
        cx.dma("sp", CTss[p][:], CTv[:, :, r0:r0 + 128], reads=[scr["r_BC"]], writes=[r_CTss[p]], slot=r_CTss[p])
        cx.dma("sp", dtts[p][:], scr["dttok"][r0:r0 + 128, :], reads=[scr["r_ac"]], writes=[r_dtts[p]], slot=r_dtts[p])
        cx.dma("sp", acts[p][:], scr["actok"][r0:r0 + 128, :], reads=[scr["r_ac"]], writes=[r_acts[p]], slot=r_acts[p])
        rl = r0 + 127 if d == 0 else r0
        cx.dma("sp", totbs[p][:], scr["actok"][rl:rl + 1, dc].partition_broadcast(128), reads=[scr["r_ac"]], writes=[r_totbs[p]], slot=r_totbs[p])
        cx.dma("sp", rowbs[p][:], scr["acT"][d * 32:d * 32 + 32, r0:r0 + 128].partition_broadcast(128), reads=[scr["r_ac"]], writes=[r_rowbs[p]], slot=r_rowbs[p])

    load_inputs(0)
    for it, (d, c) in enumerate(iters):
        if c == (0 if d == 0 else NCH - 1):
            cx.op("pool", lambda e: e.memset(HT[:, :, :], 0.0), writes=r_HT)
            cx.op("pool", lambda e: e.memset(HTb[:, :, :], 0.0), writes=r_HTb)
        dc = slice(d * 32, d * 32 + 32)
        if True:
            r0 = c * 128
            p = it % 2
            xt, r_xt, bt, r_bt, BTs, r_BTs, CTs, r_CTs = xts[p], r_xts[p], bts[p], r_bts[p], BTss[p], r_BTss[p], CTss[p], r_CTss[p]
            dtt, r_dtt, act, r_act, totb, r_totb, rowb, r_rowb = dtts[p], r_dtts[p], acts[p], r_acts[p], totbs[p], r_totbs[p], rowbs[p], r_rowbs[p]
            if it + 1 < len(iters):
                load_inputs(it + 1)
            if d == 1:
                cx.dma("sp", y0[:], scr["y0"][r0:r0 + 128, :], reads=[scr["r_y0"]], writes=[r_y0], slot=r_y0)
                cx.dma("sp", szt[:], scr["sz"][r0:r0 + 128, :], reads=[scr["r_sz"]], writes=[r_szt], slot=r_szt)
                cx.dma("sp", xw[:], x_in_v[:, :, PAD + r0:PAD + r0 + 128], reads=[w["r_xin"]], writes=[r_xw], slot=r_xw)
                cx.op("dve", lambda e: e.tensor_tensor(out=dsk[:, :], in0=xt[:, :], in1=dbc[:, :], op=ALU.mult), reads=[r_xt, r_dbc], writes=[r_dsk])
                cx.op("pool", lambda e: e.tensor_tensor(out=y0[:, :], in0=y0[:, :], in1=dsk[:, :], op=ALU.add), reads=[r_y0, r_dsk], writes=[r_y0])
            cx.op("dve", lambda e: e.tensor_scalar(out=sm[:, 0, :], in0=act[:, dc], scalar1=-1.0, scalar2=None, op0=ALU.mult), reads=[r_act], writes=[r_sm])
            cx.op("dve", lambda e: e.tensor_tensor(out=sm[:, 1, :], in0=totb[:, :], in1=act[:, dc], op=ALU.subtract), reads=[r_totb, r_act, r_sm], writes=[r_sm])
            cx.op("act", lambda e: e.activation(out=sm[:, 1, :], in_=sm[:, 1, :], func=AF.Exp), reads=[r_sm], writes=[r_sm])
            cx.op("act", lambda e: e.activation(out=sm[:, 2, :], in_=totb[:, :], func=AF.Exp), reads=[r_totb, r_sm], writes=[r_sm])
            cx.op("act", lambda e: e.activation(out=sm[:, 3, :], in_=act[:, dc], func=AF.Exp), reads=[r_act, r_sm], writes=[r_sm])
            for g4 in range(2):
                for gq in range(4):
                    g = g4 * 4 + gq
                    cx.op("pe", lambda e: e.matmul(ps_cb[:, gq * 128:(gq + 1) * 128], BTs[:, g, :], CTs[:, g, :], start=True, stop=True),
                          reads=[r_BTs, r_CTs], writes=[r_ps_cb])
                cx.op("act", lambda e: e.copy(out=cbt[:, g4 * 4:g4 * 4 + 4, :], in_=ps_cb[:, :].rearrange("p (g l) -> p g l", g=4)),
                      reads=[r_ps_cb], writes=r_cbt[g4 * 4:g4 * 4 + 4])
            cx.op("dve", lambda e: e.tensor_tensor(out=xdt[:, :].rearrange("p (h q) -> p h q", h=NHS), in0=xt[:, :].rearrange("p (h q) -> p h q", h=NHS),
                                                   in1=dtt[:, dc].unsqueeze(2).to_broadcast([128, NHS, 64]), op=ALU.mult),
                  reads=[r_xt, r_dtt], writes=[r_xdt])
            sgn = 1 if d == 0 else -1
            B4 = [128, 4, 128]

            def stage1(g):
                i2 = g % 2; hs = slice(4 * g, 4 * g + 4)
                cx.op("dve", lambda e: e.tensor_tensor(out=dif[i2][:, :, :], in0=rowb[:, hs, :], in1=sm[:, 0, hs].unsqueeze(2).to_broadcast(B4), op=ALU.add),
                      reads=[r_rowb, r_sm], writes=[r_dif[i2]])
                cx.op("pool", lambda e: e.affine_select(out=dif[i2][:, :, :], in_=dif[i2][:, :, :], pattern=[[0, 4], [sgn, 128]], compare_op=ALU.is_ge,
                                                        fill=cx.fillneg, base=0, channel_multiplier=-sgn),
                      reads=[r_dif[i2]], writes=[r_dif[i2]])
                cx.op("act", lambda e: e.activation(out=dif[i2][:, :, :], in_=dif[i2][:, :, :], func=AF.Exp), reads=[r_dif[i2]], writes=[r_dif[i2]])
                cx.op("pool", lambda e: e.tensor_tensor(out=Bw[i2][:, :, :], in0=bt[:, g * 128:(g + 1) * 128].unsqueeze(1).to_broadcast(B4),
                                                        in1=sm[:, 1, hs].unsqueeze(2).to_broadcast(B4), op=ALU.mult),
                      reads=[r_bt, r_sm], writes=[r_Bw[i2]])

            def stage2(g):
                i2 = g % 2; hs = slice(4 * g, 4 * g + 4); q = g // 2; gg = g % 2
                yb_, ob_ = ps_y[q % 2], ps_o[q % 2]
                r_yb_, r_ob_ = r_ps_y[q % 2], r_ps_o[q % 2]
                sb = ps_s[i2]; r_sb = r_ps_s[i2]
                for hq in range(4):
                    h = 4 * g + hq; hh = gg * 4 + hq
                    cx.op("pe", lambda e: e.matmul(ob_[:, hh * 64:(hh + 1) * 64], CTs[:, g, :], HTb[:, h, :], start=True, stop=True),
                          reads=[r_CTs, r_HTb[g]], writes=[r_ob_])
                    cx.op("pe", lambda e: e.matmul(sb[:, hq * 64:(hq + 1) * 64], Bw[i2][:, hq, :], xdt[:, h * 64:(h + 1) * 64], start=True, stop=True),
                          reads=[r_Bw[i2], r_xdt], writes=[r_sb])
                cx.op("dve", lambda e: e.tensor_tensor(out=MT[i2][:, :, :], in0=dif[i2][:, :, :], in1=cbt[:, g, :].unsqueeze(1).to_broadcast(B4), op=ALU.mult),
                      reads=[r_dif[i2], r_cbt[g]], writes=[r_MT[i2]])
                cx.op("dve", lambda e: e.tensor_tensor(out=HT[:, hs, :], in0=HT[:, hs, :], in1=sm[:, 2, hs].unsqueeze(2).to_broadcast([128, 4, 64]), op=ALU.mult),
                      reads=[r_HT[g], r_sm], writes=[r_HT[g]])
                for hq in range(4):
                    h = 4 * g + hq; hh = gg * 4 + hq
                    cx.op("pe", lambda e: e.matmul(yb_[:, hh * 64:(hh + 1) * 64], MT[i2][:, hq, :], xdt[:, h * 64:(h + 1) * 64], start=True, stop=True),
                          reads=[r_MT[i2], r_xdt], writes=[r_yb_])
                cx.op("dve", lambda e: e.tensor_tensor(out=HT[:, hs, :], in0=sb[:, :256].rearrange("p (h q) -> p h q", h=4), in1=HT[:, hs, :], op=ALU.add),
                      reads=[r_HT[g], r_sb], writes=[r_HT[g]])
                cx.op("act", lambda e: e.copy(out=HTb[:, hs, :], in_=HT[:, hs, :]), reads=[r_HT[g]], writes=[r_HTb[g]])
                if gg == 1:
                    qs = slice(q * 512, (q + 1) * 512)
                    h8 = slice(q * 8, q * 8 + 8)
                    cx.op("act", lambda e: e.copy(out=ysb[:, qs], in_=yb_[:, :]), reads=[r_yb_], writes=[r_ysb[q]])
                    cx.op("dve", lambda e: e.tensor_tensor(out=yof[:, :].rearrange("p (h q) -> p h q", h=8), in0=ob_[:, :].rearrange("p (h q) -> p h q", h=8),
                                                           in1=sm[:, 3, h8].unsqueeze(2).to_broadcast([128, 8, 64]), op=ALU.mult),
                          reads=[r_ob_, r_sm], writes=[r_yof])
                    cx.op("pool", lambda e: e.tensor_tensor(out=ysb[:, qs], in0=ysb[:, qs], in1=yof[:, :], op=ALU.add), reads=[r_yof, r_ysb[q]], writes=[r_ysb[q]])

            stage1(0)
            for g in range(NG):
                if g + 1 < NG:
                    stage1(g + 1)
                stage2(g)
            if d == 0:
                cx.dma("sp", scr["y0"][r0:r0 + 128, :], ysb[:, :], reads=r_ysb, writes=[scr["r_y0"]], slot=r_ysb[0])
                continue
            cx.op("pool", lambda e: e.tensor_tensor(out=y0[:, :], in0=y0[:, :], in1=ysb[:, :], op=ALU.add), reads=[r_y0] + r_ysb, writes=[r_y0])
            cx.op("dve", lambda e: e.tensor_tensor(out=y0[:, :], in0=y0[:, :], in1=szt[:, :], op=ALU.mult), reads=[r_y0, r_szt], writes=[r_y0])
            cx.op("pool", lambda e: e.tensor_tensor(out=ysb[:, :], in0=y0[:, :], in1=y0[:, :], op=ALU.mult), reads=[r_y0] + r_ysb, writes=r_ysb)
            cx.op("dve", lambda e: e.tensor_reduce(out=gss[:, :], in_=ysb[:, :].rearrange("p (g f) -> p g f", g=NG), axis=mybir.AxisListType.X, op=ALU.add),
                  reads=r_ysb, writes=[r_gss])
            cx.op("act", lambda e: e.activation(out=gss[:, :], in_=gss[:, :], func=AF.Ln, bias=cm.eps[:, 0:1], scale=1.0 / 256), reads=[r_gss, cm.r_eps], writes=[r_gss])
            cx.op("act", lambda e: e.activation(out=gss[:, :], in_=gss[:, :], func=AF.Exp, scale=-0.5), reads=[r_gss], writes=[r_gss])
            for g in range(NG):
                gs = slice(g * 256, (g + 1) * 256)
                cx.op("dve", lambda e: e.scalar_tensor_tensor(out=yb[:, gs], in0=y0[:, gs], scalar=gss[:, g:g + 1], in1=gbc[:, gs], op0=ALU.mult, op1=ALU.mult),
                      reads=[r_y0, r_gss, r_gbc], writes=[r_yb])
            for half in range(2):
                for cq in range(8):
                    cch = half * 8 + cq
                    cx.op("pe", lambda e: e.transpose(ps_t[:, cq * 128:(cq + 1) * 128], yb[:, cch * 128:(cch + 1) * 128], idb[:]),
                          reads=[r_yb, r_idb], writes=[r_ps_t])
                cx.op("act", lambda e: e.copy(out=yT[:, half * 8:(half + 1) * 8, :], in_=ps_t[:, :].rearrange("p (a q) -> p a q", a=8)),
                      reads=[r_ps_t], writes=[r_yT])
            for m in range(KC):
                pb = ps_y[m % 2]; r_pb = r_ps_y[m % 2]
                for cch in range(16):
                    cx.op("pe", lambda e, cch=cch: e.matmul(pb[:, :128], wout[:, cch, m * 128:(m + 1) * 128], yT[:, cch, :], start=(cch == 0), stop=(cch == 15)),
                          reads=[r_wout, r_yT], writes=[r_pb], signal=(cch == 15))
                cx.op("dve", lambda e: e.tensor_tensor(out=xw[:, m, :], in0=pb[:, :128], in1=xw[:, m, :], op=ALU.add), reads=[r_pb, r_xw], writes=[r_xw])
            cx.dma("sp", x_out_v[:, :, PAD + r0:PAD + r0 + 128], xw[:, :, :], reads=[r_xw], writes=[w["r_xout"]], slot=r_xw)
    cx.barrier()
    cx.es = old_es
    es.close()


def make_ssd_scratch(nc, T):
    d = lambda n, s, dt: nc.dram_tensor(n, s, dt, kind="Internal").ap()
    return dict(xtok=d("s_xtok", [T, DI], BF16), Btok=d("s_btok", [T, NG * DS], BF16), BT=d("s_BT", [NG * DS, T], BF16),
                CT=d("s_CT", [NG * DS, T], BF16), sz=d("s_sz", [T, DI], BF16), actok=d("s_actok", [T, 64], F32),
                dttok=d("s_dttok", [T, 64], F32), acT=d("s_acT", [64, T], F32), y0=d("s_y0", [T, DI], F32),
                r_tok=Res("s_tok"), r_BC=Res("s_BC"), r_sz=Res("s_sz"), r_ac=Res("s_ac"), r_y0=Res("s_y0"))


DEPTH = 4
ROT = 16


def build_program(T):
    TP = T + 2 * PAD
    nc = bass.Bass("TRN2", target_bir_lowering=False)
    di = lambda n, s: nc.dram_tensor(n, list(s), F32, kind="ExternalInput").ap()
    xin = di("xin", [D, TP])
    xout = nc.dram_tensor("xout", [D, TP], F32, kind="ExternalOutput").ap()
    xmid = nc.dram_tensor("xmid", [D, TP], F32, kind="Internal").ap()
    bm = di("bm", [128, 128]); rm = di("rm", [128, 128]); ident = di("ident", [128, 128])
    cos = di("cos", [128, T]); sin = di("sin", [128, T])
    A = {}
    for j in range(2):
        A[j] = dict(wqkv=di(f"a{j}_wqkv", [D, WQKV]), wo=di(f"a{j}_wo", [D, D]), gn=di(f"a{j}_gn", [128, KC]), gqk=di(f"a{j}_gqk", [128, 2]),
                    sink=di(f"a{j}_sink", [128, NH]))
    S = {}
    for j in range(2):
        S[j] = dict(win=di(f"s{j}_win", [D, SIN]), wout=di(f"s{j}_wout", [DI, D]), gn=di(f"s{j}_gn", [128, KC]), cwb=di(f"s{j}_cwb", [128, 32, 6]),
                    dtp=di(f"s{j}_dtp", [64, 2]), dbc=di(f"s{j}_dbc", [128, DI]), gbc=di(f"s{j}_gbc", [128, DI]))
    Fw = {}
    for i in range(DEPTH):
        Fw[i] = dict(wup=di(f"f{i}_wup", [D, 2 * DFF]), wdn=di(f"f{i}_wdn", [DFF, D]), cwb=di(f"f{i}_cwb", [128, NFC, 4]), gn=di(f"f{i}_gn", [128, KC]))
    scr = make_ssd_scratch(nc, T)
    with ExitStack() as es:
        cx = Ctx(nc, es)
        cm = Common(cx)
        r = {"xin": Res("xin"), "xout": Res("xout"), "xmid": Res("xmid")}
        aps = {"xin": xin, "xout": xout, "xmid": xmid}
        zes = ExitStack(); cx.es = zes
        zt = cx.sbuf("zt", [128, KC, PAD], F32); r_zt = Res("zt")
        cx.op("pool", lambda e: e.memset(zt[:], 0.0), writes=[r_zt])
        for nm in ("xout", "xmid"):
            v = aps[nm].rearrange("(k p) t -> p k t", p=128)
            cx.dma("sp", v[:, :, 0:PAD], zt[:], reads=[r_zt], writes=[r[nm]], slot=r_zt)
            cx.dma("sp", v[:, :, PAD + T:PAD + T + PAD], zt[:], reads=[r_zt], writes=[r[nm]], slot=r_zt)
        cx.barrier()
        cx.es = es
        zes.close()
        seq = ["xin"] + ["xmid", "xout"] * DEPTH
        step = 0
        for i in range(DEPTH):
            j = i // 2
            src, dst = seq[step], seq[step + 1]; step += 1
            if i % 2 == 0:
                w = dict(wqkv=A[j]["wqkv"].rearrange("(k p) n -> p k n", p=128), wo=A[j]["wo"].rearrange("(k p) n -> p k n", p=128),
                         gn=A[j]["gn"], gqk=A[j]["gqk"], sink=A[j]["sink"], bm=bm, rm=rm, cos=cos, sin=sin, r_xin=r[src], r_xout=r[dst])
                attn_layer(cx, cm, T, aps[src], aps[dst], w)
            else:
                w = dict(win=S[j]["win"].rearrange("(k p) n -> p k n", p=128), wout=S[j]["wout"].rearrange("(k p) n -> p k n", p=128),
                         gn=S[j]["gn"], cwb=S[j]["cwb"], dtp=S[j]["dtp"], dbc=S[j]["dbc"], gbc=S[j]["gbc"], ident=ident, r_xin=r[src], r_xout=r[dst])
                ssd_layer(cx, cm, T, aps[src], aps[dst], w, scr)
            src, dst = seq[step], seq[step + 1]; step += 1
            w = dict(wup=Fw[i]["wup"].rearrange("(k p) n -> p k n", p=128), wdn=Fw[i]["wdn"].rearrange("(c p) n -> p c n", p=128),
                     cwb=Fw[i]["cwb"], gn=Fw[i]["gn"], r_xin=r[src], r_xout=r[dst])
            ffn_layer(cx, cm, T, aps[src], aps[dst], w)
        assert dst == "xout"
        cx.barrier(fresh=False)
    return nc


def host_layout(inp, T):
    f = lambda a: np.ascontiguousarray(np.asarray(a, dtype=np.float32))
    col = lambda v: f(np.asarray(v).reshape(KC, 128).T)
    m = {}
    m["bm"] = f(np.kron(np.eye(2), np.full((64, 64), 1.0 / 64)))
    rm = np.zeros((128, 128), np.float32)
    for blk in (0, 64):
        for q in range(8):
            rm[blk + q + 8, blk + q] = -1.0
            rm[blk + q, blk + q + 8] = 1.0
    m["rm"] = rm
    m["ident"] = np.eye(128, dtype=np.float32)
    pos = np.arange(T, dtype=np.float32)
    inv_freq = (np.float32(500000.0) ** (-(np.arange(0, ROT, 2, dtype=np.float32) / np.float32(ROT)))).astype(np.float32)
    ang = (pos[:, None] * inv_freq[None, :]).astype(np.float32)
    cosv, sinv = np.cos(ang).astype(np.float32), np.sin(ang).astype(np.float32)
    cosT = np.ones((128, T), np.float32); sinT = np.zeros((128, T), np.float32)
    for blk in (0, 64):
        for q in range(16):
            cosT[blk + q] = cosv[:, q % 8]; sinT[blk + q] = sinv[:, q % 8]
    m["cos"], m["sin"] = cosT, sinT
    for j in range(2):
        wq = np.asarray(inp["attn_w_qkv"][j])
        m[f"a{j}_wqkv"] = f(np.concatenate([wq[:, :QD]] + [np.tile(wq[:, QD + k * 64:QD + (k + 1) * 64], (1, 2)) for k in range(NKV)] + [wq[:, QD + 256:]], axis=1))
        m[f"a{j}_wo"] = f(inp["attn_w_o"][j])
        m[f"a{j}_gn"] = col(inp["attn_norm"][j])
        m[f"a{j}_gqk"] = f(np.stack([np.tile(np.asarray(inp["attn_q_norm"][j]), 2), np.tile(np.asarray(inp["attn_k_norm"][j]), 2)], 1))
        m[f"a{j}_sink"] = f(np.tile(np.asarray(inp["attn_sink"][j])[None, :], (128, 1)))
        m[f"s{j}_win"] = f(inp["ssd_w_in"][j])
        m[f"s{j}_wout"] = f(inp["ssd_w_out"][j])
        m[f"s{j}_gn"] = col(inp["ssd_norm"][j])
        cwb = np.zeros((128, 32, 6), np.float32)
        cwb[:, :, 0:5] = np.asarray(inp["ssd_conv_w"][j]).reshape(5, 32, 128).transpose(2, 1, 0)
        cwb[:, :, 5] = np.asarray(inp["ssd_conv_b"][j]).reshape(32, 128).T
        m[f"s{j}_cwb"] = cwb
        m[f"s{j}_dtp"] = f(np.stack([np.asarray(inp["ssd_dt_bias"][j]).reshape(64), np.asarray(inp["ssd_a_log"][j]).reshape(64)], 1))
        m[f"s{j}_dbc"] = f(np.tile(np.repeat(np.asarray(inp["ssd_d"][j]), 64)[None, :], (128, 1)))
        m[f"s{j}_gbc"] = f(np.tile(np.asarray(inp["ssd_gate_norm"][j])[None, :], (128, 1)))
    for i in range(DEPTH):
        m[f"f{i}_wup"] = f(inp["ffn_w_up"][i])
        m[f"f{i}_wdn"] = f(inp["ffn_w_down"][i])
        cwb = np.zeros((128, NFC, 4), np.float32)
        cwb[:, :, 0:3] = np.asarray(inp["ffn_conv_w"][i]).reshape(3, NFC, 128).transpose(2, 1, 0)
        cwb[:, :, 3] = np.asarray(inp["ffn_conv_b"][i]).reshape(NFC, 128).T
        m[f"f{i}_cwb"] = cwb
        m[f"f{i}_gn"] = col(inp["ffn_norm"][i])
    return m


def run_module(inp, n_cores=8):
    x = np.asarray(inp["x"], dtype=np.float32)
    B, T, _ = x.shape
    nc = build_program(T)
    shared = host_layout(inp, T)
    in_maps = []
    for c in range(n_cores):
        b = c % B
        xin = np.zeros((D, T + 2 * PAD), np.float32)
        xin[:, PAD:PAD + T] = x[b].T
        mp = dict(shared); mp["xin"] = xin
        in_maps.append(mp)
    res = run_bass_kernel_spmd(nc, in_maps, core_ids=list(range(n_cores)))
    out = np.stack([np.ascontiguousarray(res.results[b]["xout"][:, PAD:PAD + T].T) for b in range(B)], 0)
    return out.astype(np.float32)


def kernel(**inputs):
    return run_module(inputs)
```

```python
import numpy as np
from contextlib import ExitStack
import concourse.bass as bass
import concourse.mybir as mybir
from concourse.bass_utils import run_bass_kernel_spmd

F32 = mybir.dt.float32
BF16 = mybir.dt.bfloat16
AF = mybir.ActivationFunctionType
ALU = mybir.AluOpType

D = 1024
KC = D // 128
DFF = 2816
NFC = 2 * DFF // 128
NGC = DFF // 128
EPS = 1e-6
PAD = 128


class Sem:
    def __init__(self, handle, name):
        self.h = handle
        self.name = name
        self.count = 0


class Res:
    def __init__(self, name, excl=False):
        self.name = name
        self.excl = excl
        self.last_w = None
        self.readers = {}
        self.dsem = None


class Ctx:
    ENG = ("pe", "act", "dve", "pool", "sp")

    def __init__(self, nc, es):
        self.nc = nc
        self.es = es
        self.top_es = es
        self.fill0 = nc.gpsimd.to_reg(0.0)
        self.fillneg = nc.gpsimd.to_reg(-30000.0)
        self.eng = {"pe": nc.tensor, "act": nc.scalar, "dve": nc.vector, "pool": nc.gpsimd, "sp": nc.sync}
        self.sems = {}
        self.waited = {e: {} for e in self.ENG}
        self.gen = 0
        self._new_sems()
        self.uid = 0

    def _new_sems(self):
        self.gen += 1
        for e in self.ENG:
            self.sems[e] = Sem(self.es.enter_context(self.nc.semaphore(f"s_{e}_{self.gen}")), e)
        if not hasattr(self, "dfree"):
            self.dfree, self.dused, self.nd, self.dfresh = [], [], 0, []

    def name(self, base):
        self.uid += 1
        return f"{base}_{self.uid}"

    def sbuf(self, name, shape, dtype):
        t = self.es.enter_context(self.nc.sbuf_tensor(self.name(name), list(shape), dtype))
        return t

    def psum(self, name, shape, dtype):
        t = self.es.enter_context(self.nc.psum_tensor(self.name(name), list(shape), dtype))
        return t

    def _deps(self, eng, reads, writes):
        deps = []
        for r in reads:
            if r.last_w is not None:
                s, v, e = r.last_w
                if not (e == eng and eng == "pe"):
                    deps.append((s, v))
            if r.excl:
                for e, (s, v) in r.readers.items():
                    if e != eng:
                        deps.append((s, v))
        for w in writes:
            if w.last_w is not None:
                s, v, e = w.last_w
                if e != eng or eng != "pe":
                    deps.append((s, v))
            for e, (s, v) in w.readers.items():
                if e != eng or eng != "pe":
                    deps.append((s, v))
        return deps

    def _wait(self, eng, deps):
        wd = self.waited[eng]
        best = {}
        for s, v in deps:
            if wd.get(s, 0) >= v:
                continue
            if best.get(s, 0) < v:
                best[s] = v
        for s, v in best.items():
            assert v <= s.count, f"wait on un-emitted signal {s.name} {v}>{s.count} (engine {eng})"
            self.eng[eng].wait_ge(s.h, v)
            wd[s] = v

    def _stamp(self, eng, stamp, reads, writes):
        s, v = stamp
        for r in reads:
            r.readers[eng] = (s, v)
        for w in writes:
            w.last_w = (s, v, eng)
            w.readers = {}

    def op(self, eng, fn, reads=(), writes=(), signal=True):
        reads = [r for r in reads if r is not None]
        writes = [w for w in writes if w is not None]
        self._wait(eng, self._deps(eng, reads, writes))
        ins = fn(self.eng[eng])
        s = self.sems[eng]
        if signal:
            s.count += 1
            ins.then_inc(s.h, 1)
            stamp = (s, s.count)
        else:
            stamp = (s, s.count + 1)
        self._stamp(eng, stamp, reads, writes)
        return ins

    def _dsem(self, res, fresh=False):
        if res.dsem is None and fresh:
            self.nd += 1
            res.dsem = Sem(self.top_es.enter_context(self.nc.semaphore(f"s_dma_{self.nd}")), f"dma{self.nd}")
            self.dfresh.append(res.dsem)
        if res.dsem is None:
            while self.dfree and self.dfree[-1].count > 30000:
                self.dfree.pop()
            if self.dfree:
                res.dsem = self.dfree.pop()
            else:
                self.nd += 1
                res.dsem = Sem(self.top_es.enter_context(self.nc.semaphore(f"s_dma_{self.nd}")), f"dma{self.nd}")
            self.dused.append(res.dsem)
        return res.dsem

    def dma(self, queue, out, in_, reads=(), writes=(), slot=None):
        reads = [r for r in reads if r is not None]
        writes = [w for w in writes if w is not None]
        self._wait(queue, self._deps("dma", reads, writes))
        ins = self.eng[queue].dma_start(out=out, in_=in_)
        s = self._dsem(slot, fresh=(queue == "pool"))
        s.count += 16
        ins.then_inc(s.h, 16)
        self._stamp("dma", (s, s.count), reads, writes)
        return ins

    def barrier(self, fresh=True):
        for e in self.ENG:
            deps = [(s, s.count) for s in list(self.sems.values()) + self.dused + self.dfresh if s.count > 0]
            self._wait(e, deps)
        self.dfree.extend(self.dused)
        self.dused = []
        self.dfresh = []
        if fresh:
            old = dict(self.sems)
            self._new_sems()
            self._old = old


class Common:
    def __init__(self, cx):
        nc = cx.nc
        self.ones_mean = cx.sbuf("ones_mean", [128, 128], F32)
        self.r_ones = Res("ones_mean")
        cx.op("pool", lambda e: e.memset(self.ones_mean[:], 1.0 / D), writes=[self.r_ones])
        self.eps = cx.sbuf("eps", [128, 1], F32)
        self.r_eps = Res("eps")
        cx.op("pool", lambda e: e.memset(self.eps[:], EPS), writes=[self.r_eps])
        self.one = cx.sbuf("one", [128, 1], F32)
        self.r_one = Res("one")
        cx.op("pool", lambda e: e.memset(self.one[:], 1.0), writes=[self.r_one])


def rmsnorm_fm(cx, cm, xw, r_xw, W, g_ap, r_g, hn, r_hn, ps, r_ps, sq, r_sq, rstd, r_rstd):
    for k in range(KC):
        b = k % 2
        cx.op("act", lambda e, k=k, b=b: e.activation(out=sq[b][:, :W], in_=xw[:, k, :W], func=AF.Square),
              reads=[r_xw], writes=[r_sq[b]])
        cx.op("pe", lambda e, k=k, b=b: e.matmul(ps[:, :W], cm.ones_mean[:], sq[b][:, :W],
                                                    start=(k == 0), stop=(k == KC - 1)),
              reads=[cm.r_ones, r_sq[b]], writes=[r_ps], signal=True)
    cx.op("act", lambda e: e.activation(out=rstd[:, :W], in_=ps[:, :W], func=AF.Ln, bias=cm.eps[:, 0:1]),
          reads=[r_ps, cm.r_eps], writes=[r_rstd])
    cx.op("act", lambda e: e.activation(out=rstd[:, :W], in_=rstd[:, :W], func=AF.Exp, scale=-0.5),
          reads=[r_rstd], writes=[r_rstd])
    for k in range(KC):
        cx.op("dve", lambda e, k=k: e.scalar_tensor_tensor(out=hn[:, k, :W], in0=xw[:, k, :W], scalar=g_ap(k),
                                                        in1=rstd[:, :W], op0=ALU.mult, op1=ALU.mult),
              reads=[r_xw, r_rstd, r_g], writes=[r_hn[k]])


def load_weight_bf16(cx, dst, r_dst, src_ap, nk, split=1):
    for k in range(nk):
        cx.dma("pool", dst[:, k, :], src_ap[:, k, :], writes=[r_dst], slot=r_dst)


def ffn_layer(cx, cm, T, x_in, x_out, w):
    nc = cx.nc
    es = ExitStack()
    old_es = cx.es
    cx.es = es
    wup = cx.sbuf("wup", [128, KC, 2 * DFF], BF16); r_wup = Res("wup")
    wdn = cx.sbuf("wdn", [128, NGC, D], BF16); r_wdn = Res("wdn")
    cwb = cx.sbuf("cwb", [128, NFC, 4], F32); r_cwb = Res("cwb")
    gn = cx.sbuf("gn", [128, KC], F32); r_gn = Res("gn")
    xws = [cx.sbuf(f"xw{i}", [128, KC, 512], F32) for i in range(2)]; r_xws = [Res("xw0"), Res("xw1")]
    hn = cx.sbuf("hn", [128, KC, 512], BF16); r_hn = [Res(f"hn{k}") for k in range(KC)]
    gT = cx.sbuf("gT", [128, NGC, 512], BF16); r_gT = [Res(f"gT{c}") for c in range(NGC)]
    rstd = cx.sbuf("rstd", [128, 512], F32); r_rstd = Res("rstd")
    NT = 2
    t1 = [[cx.sbuf(f"t1_{i}_{j}", [128, 512], F32) for j in range(2)] for i in range(NT)]
    r_t1 = [[Res(f"t1_{i}_{j}") for j in range(2)] for i in range(NT)]
    sq = [t1[0][0], t1[0][1]]; r_sq = [r_t1[0][0], r_t1[0][1]]
    ps_st = cx.psum("ps_st", [128, 512], F32); r_ps_st = Res("ps_st", excl=True)
    ps_gv = [[cx.psum(f"ps_gv{i}{j}", [128, 512], F32) for j in range(2)] for i in range(2)]
    r_ps_gv = [[Res(f"ps_gv{i}{j}", excl=True) for j in range(2)] for i in range(2)]
    ps_o = [cx.psum(f"ps_o{i}", [128, 512], F32) for i in range(2)]
    r_ps_o = [Res(f"ps_o{i}", excl=True) for i in range(2)]

    cx.dma("sp", cwb[:], w["cwb"], writes=[r_cwb], slot=r_cwb)
    cx.dma("sp", gn[:], w["gn"], writes=[r_gn], slot=r_gn)
    load_weight_bf16(cx, wup, r_wup, w["wup"], KC)
    load_weight_bf16(cx, wdn, r_wdn, w["wdn"], NGC)

    x_in_v = x_in.rearrange("(k p) t -> p k t", p=128)
    x_out_v = x_out.rearrange("(k p) t -> p k t", p=128)
    STEP = 510
    tiles = [(t0, min(STEP, T - t0)) for t0 in range(0, T, STEP)]

    def load(i):
        t0, n_out = tiles[i]
        W = n_out + 2
        cx.dma("sp", xws[i % 2][:, :, :W], x_in_v[:, :, PAD + t0 - 1:PAD + t0 - 1 + W], reads=[w["r_xin"]], writes=[r_xws[i % 2]], slot=r_xws[i % 2])

    def norm(i):
        t0, n_out = tiles[i]
        rmsnorm_fm(cx, cm, xws[i % 2], r_xws[i % 2], n_out + 2, lambda k: gn[:, k:k + 1], r_gn, hn, r_hn, ps_st, r_ps_st, sq, r_sq, rstd, r_rstd)

    load(0)
    if len(tiles) > 1:
        load(1)
    norm(0)
    pair_i = 0
    for ti, (t0, n_out) in enumerate(tiles):
        W = n_out + 2
        xw = xws[ti % 2]; r_xw = r_xws[ti % 2]
        def front(cg, pi):
            pb = pi % 2; tb = pi % NT
            for j, c in enumerate((cg, cg + NGC)):
                ps = ps_gv[pb][j]; r_ps = r_ps_gv[pb][j]
                for k in range(KC):
                    cx.op("pe", lambda e, k=k, c=c, ps=ps: e.matmul(ps[:, :W], wup[:, k, c * 128:(c + 1) * 128], hn[:, k, :W],
                                                                   start=(k == 0), stop=(k == KC - 1)),
                          reads=[r_wup, r_hn[k]], writes=[r_ps], signal=(k == KC - 1))
            for j, c in enumerate((cg, cg + NGC)):
                ps = ps_gv[pb][j]; r_ps = r_ps_gv[pb][j]
                tt = t1[tb][j]; r_tt = r_t1[tb][j]
                cx.op("act", lambda e, c=c, ps=ps, tt=tt: e.activation(out=tt[:, :n_out], in_=ps[:, 1:1 + n_out], func=AF.Identity,
                                                                        bias=cwb[:, c, 3:4], scale=cwb[:, c, 1:2]),
                      reads=[r_ps, r_cwb], writes=[r_tt])
                cx.op("dve", lambda e, c=c, ps=ps, tt=tt: e.scalar_tensor_tensor(out=tt[:, :n_out], in0=ps[:, 0:n_out], scalar=cwb[:, c, 0:1],
                                                                               in1=tt[:, :n_out], op0=ALU.mult, op1=ALU.add),
                      reads=[r_ps, r_cwb, r_tt], writes=[r_tt])
                cx.op("dve", lambda e, c=c, ps=ps, tt=tt: e.scalar_tensor_tensor(out=tt[:, :n_out], in0=ps[:, 2:2 + n_out], scalar=cwb[:, c, 2:3],
                                                                               in1=tt[:, :n_out], op0=ALU.mult, op1=ALU.add),
                      reads=[r_ps, r_cwb, r_tt], writes=[r_tt])

        def back(cg, pi):
            tb = pi % NT
            tg = t1[tb][0]; tv = t1[tb][1]
            cx.op("act", lambda e, tg=tg: e.activation(out=tg[:, :n_out], in_=tg[:, :n_out], func=AF.Silu),
                  reads=[r_t1[tb][0]], writes=[r_t1[tb][0]])
            cx.op("pool", lambda e, tg=tg, tv=tv, cg=cg: e.tensor_tensor(out=gT[:, cg, :n_out], in0=tg[:, :n_out], in1=tv[:, :n_out], op=ALU.mult),
                  reads=[r_t1[tb][0], r_t1[tb][1]], writes=[r_gT[cg]])

        front(0, pair_i)
        for cg in range(NGC):
            if cg + 1 < NGC:
                front(cg + 1, pair_i + 1)
            back(cg, pair_i)
            pair_i += 1
        if ti + 1 < len(tiles):
            norm(ti + 1)
        for m in range(KC):
            ob = m % 2
            for c in range(NGC):
                cx.op("pe", lambda e, c=c, m=m, ob=ob: e.matmul(ps_o[ob][:, :n_out], wdn[:, c, m * 128:(m + 1) * 128], gT[:, c, :n_out],
                                                                 start=(c == 0), stop=(c == NGC - 1)),
                      reads=[r_wdn, r_gT[c]], writes=[r_ps_o[ob]], signal=(c == NGC - 1))
            cx.op("dve", lambda e, m=m, ob=ob: e.tensor_tensor(out=xw[:, m, 1:1 + n_out], in0=ps_o[ob][:, :n_out], in1=xw[:, m, 1:1 + n_out], op=ALU.add),
                  reads=[r_ps_o[ob], r_xw], writes=[r_xw])
        cx.dma("sp", x_out_v[:, :, PAD + t0:PAD + t0 + n_out], xw[:, :, 1:1 + n_out], reads=[r_xw], writes=[w["r_xout"]], slot=r_xw)
        if ti + 2 < len(tiles):
            load(ti + 2)
    cx.barrier()
    cx.es = old_es
    es.close()


NH, NKV, HD = 16, 4, 64
QD = NH * HD
WQKV = QD + 2 * NKV * HD + NKV * HD


def attn_layer(cx, cm, T, x_in, x_out, w):
    es = ExitStack(); old_es = cx.es; cx.es = es
    NB = T // 128
    wq = cx.sbuf("wqkv", [128, KC, WQKV], BF16); r_wq = Res("wqkv")
    wo = cx.sbuf("wo", [128, KC, D], BF16); r_wo = Res("wo")
    gn = cx.sbuf("gn", [128, KC], F32); r_gn = Res("gn")
    gqk = cx.sbuf("gqk", [128, 2], F32); r_gqk = Res("gqk")
    esk = cx.sbuf("esk", [128, NH], F32); r_esk = Res("esk")
    bm = cx.sbuf("bm", [128, 128], F32); r_bm = Res("bm")
    rmf = cx.sbuf("rmf", [128, 128], F32); r_rmf = Res("rmf")
    rm = cx.sbuf("rm", [128, 128], BF16); r_rm = Res("rm")
    onesb = cx.sbuf("onesb", [128, 128], BF16); r_onesb = Res("onesb")
    kT = cx.sbuf("kT", [128, NKV, T], BF16); r_kT = [Res(f"kT{i}") for i in range(T // 512)]
    V = cx.sbuf("V", [128, NB, NKV * HD], BF16); r_V = [Res(f"V{i}") for i in range(NB)]
    vv = cx.sbuf("vv", [128, 4, NKV, 2, HD], BF16); r_vv = [Res(f"vv{i}") for i in range(4)]
    xw = cx.sbuf("xw", [128, KC, 512], F32); r_xw = Res("xw")
    hn = cx.sbuf("hn", [128, KC, 512], BF16); r_hn = [Res(f"hn{k}") for k in range(KC)]
    qT = cx.sbuf("qT", [128, KC, 512], BF16); r_qT = [Res(f"qT{k}") for k in range(KC)]
    oT = hn; r_oT = r_hn
    cs = cx.sbuf("cs", [128, 2, 512], F32); r_cs = Res("cs")
    sq = [cx.sbuf(f"sq{i}", [128, 512], F32) for i in range(2)]; r_sq = [Res("sq0"), Res("sq1")]
    rstd = cx.sbuf("rstd", [128, 512], F32); r_rstd = Res("rstd")
    qnb = cx.sbuf("qnb", [128, 512], BF16); r_qnb = Res("qnb")
    ta = cx.sbuf("ta", [128, 512], F32); r_ta = Res("ta")
    tb = cx.sbuf("tb", [128, 512], F32); r_tb = Res("tb")
    PT = [cx.sbuf(f"PT{i}", [128, 512], BF16) for i in range(4)]; r_PT = [Res(f"PT{i}") for i in range(4)]
    rd = cx.sbuf("rd", [128, 512], F32); r_rd = Res("rd")
    ps_p1 = cx.psum("ps_p0", [128, 512], F32); r_ps_p1 = Res("ps_p0", excl=True)
    ps_p = [ps_p1, ps_p1]; r_ps_p = [r_ps_p1, r_ps_p1]
    ps_a = cx.psum("ps_a", [128, 512], F32); r_ps_a = Res("ps_a", excl=True)
    ps_b = ps_a; r_ps_b = r_ps_a
    ps_s = [[cx.psum(f"ps_s{i}{hf}", [128, 512], F32) for hf in range(2)] for i in range(2)]
    r_ps_s = [[Res(f"ps_s{i}{hf}", excl=True) for hf in range(2)] for i in range(2)]
    ps_o = cx.psum("ps_o", [128, 512], F32); r_ps_o = Res("ps_o", excl=True)
    ps_d = cx.psum("ps_d", [128, 512], F32); r_ps_d = Res("ps_d", excl=True)
    ta2 = cx.sbuf("ta2", [128, 512], F32); r_ta2 = Res("ta2")
    tb2 = cx.sbuf("tb2", [128, 512], F32); r_tb2 = Res("tb2")
    CH = [dict(ps=ps_p[0], r_ps=r_ps_p[0], pa=ps_a, r_pa=r_ps_a, sq=sq[0], r_sq=r_sq[0], rstd=rstd, r_rstd=r_rstd, qnb=qnb, r_qnb=r_qnb,
               ta=ta, r_ta=r_ta, tb=tb, r_tb=r_tb),
          dict(ps=ps_s[0][0], r_ps=r_ps_s[0][0], pa=ps_s[0][1], r_pa=r_ps_s[0][1], sq=sq[1], r_sq=r_sq[1], rstd=rd, r_rstd=r_rd, qnb=PT[0], r_qnb=r_PT[0],
               ta=ta2, r_ta=r_ta2, tb=tb2, r_tb=r_tb2)]

    for dst, r, key in ((gn, r_gn, "gn"), (gqk, r_gqk, "gqk"), (esk, r_esk, "sink"), (bm, r_bm, "bm"), (rmf, r_rmf, "rm")):
        cx.dma("sp", dst[:], w[key], writes=[r], slot=r)
    load_weight_bf16(cx, wq, r_wq, w["wqkv"], KC)
    load_weight_bf16(cx, wo, r_wo, w["wo"], KC)
    cx.op("act", lambda e: e.activation(out=esk[:], in_=esk[:], func=AF.Exp), reads=[r_esk], writes=[r_esk])
    cx.op("dve", lambda e: e.tensor_copy(rm[:], rmf[:]), reads=[r_rmf], writes=[r_rm])
    cx.op("pool", lambda e: e.memset(onesb[:], 1.0), writes=[r_onesb])

    x_in_v = x_in.rearrange("(k p) t -> p k t", p=128)
    x_out_v = x_out.rearrange("(k p) t -> p k t", p=128)
    pp = [0]

    def load_norm(t0):
        cx.dma("sp", xw[:, :, :], x_in_v[:, :, PAD + t0:PAD + t0 + 512], reads=[w["r_xin"]], writes=[r_xw], slot=r_xw)
        cx.dma("sp", cs[:, 0, :], w["cos"][:, t0:t0 + 512], writes=[r_cs], slot=r_cs)
        cx.dma("sp", cs[:, 1, :], w["sin"][:, t0:t0 + 512], writes=[r_cs], slot=r_cs)
        rmsnorm_fm(cx, cm, xw, r_xw, 512, lambda k: gn[:, k:k + 1], r_gn, hn, r_hn, ps_a, r_ps_a, sq, r_sq, rstd, r_rstd)

    def proj_fm(col0):
        S = CH[pp[0] % 2]; pp[0] += 1
        for k in range(KC):
            cx.op("pe", lambda e, k=k: e.matmul(S["ps"][:, :], wq[:, k, col0:col0 + 128], hn[:, k, :],
                                                start=(k == 0), stop=(k == KC - 1)),
                  reads=[r_wq, r_hn[k]], writes=[S["r_ps"]], signal=(k == KC - 1))
        return S

    def headnorm_rope_gen(S, gcol, out_ap, r_out):
        ps, r_ps, pa, r_pa = S["ps"], S["r_ps"], S["pa"], S["r_pa"]
        sq_, r_sq_, rs_, r_rs_, qn_, r_qn_, ta_, r_ta_, tb_, r_tb_ = (S["sq"], S["r_sq"], S["rstd"], S["r_rstd"], S["qnb"], S["r_qnb"],
                                                                    S["ta"], S["r_ta"], S["tb"], S["r_tb"])
        cx.op("act", lambda e: e.activation(out=sq_[:, :], in_=ps[:, :], func=AF.Square), reads=[r_ps], writes=[r_sq_])
        yield

        cx.op("pe", lambda e: e.matmul(pa[:, :], bm[:], sq_[:, :], start=True, stop=True),
              reads=[r_bm, r_sq_], writes=[r_pa])
        yield

        cx.op("act", lambda e: e.activation(out=rs_[:, :], in_=pa[:, :], func=AF.Ln, bias=cm.eps[:, 0:1]),
              reads=[r_pa, cm.r_eps], writes=[r_rs_])
        yield

        cx.op("act", lambda e: e.activation(out=rs_[:, :], in_=rs_[:, :], func=AF.Exp, scale=-0.5),
              reads=[r_rs_], writes=[r_rs_])
        yield

        cx.op("dve", lambda e: e.scalar_tensor_tensor(out=qn_[:, :], in0=ps[:, :], scalar=gqk[:, gcol:gcol + 1], in1=rs_[:, :],
                                                      op0=ALU.mult, op1=ALU.mult),
              reads=[r_ps, r_gqk, r_rs_], writes=[r_qn_])
        yield

        cx.op("pe", lambda e: e.matmul(pa[:, :], rm[:], qn_[:, :], start=True, stop=True),
              reads=[r_rm, r_qn_], writes=[r_pa])
        yield

        cx.op("dve", lambda e: e.tensor_tensor(out=ta_[:, :], in0=qn_[:, :], in1=cs[:, 0, :], op=ALU.mult),
              reads=[r_qn_, r_cs], writes=[r_ta_])
        yield

        cx.op("dve", lambda e: e.tensor_tensor(out=tb_[:, :], in0=pa[:, :], in1=cs[:, 1, :], op=ALU.mult),
              reads=[r_pa, r_cs], writes=[r_tb_])
        yield

        cx.op("pool", lambda e: e.tensor_tensor(out=out_ap, in0=ta_[:, :], in1=tb_[:, :], op=ALU.add),
              reads=[r_ta_, r_tb_], writes=[r_out])
        yield

    def run_chains(specs):
        for i in range(0, len(specs), 2):
            gens = []
            for col0, gcol, out_ap, r_out in specs[i:i + 2]:
                S = proj_fm(col0)
                gens.append(headnorm_rope_gen(S, gcol, out_ap, r_out))
            live = list(gens)
            while live:
                for g_ in list(live):
                    try:
                        next(g_)
                    except StopIteration:
                        live.remove(g_)

    for ti in range(T // 512):
        t0 = ti * 512
        load_norm(t0)
        run_chains([(QD + j * 128, 1, kT[:, j, t0:t0 + 512], r_kT[ti]) for j in range(NKV)])
        for bl in range(4):
            b = pp[0] % 2; pp[0] += 1
            for k in range(KC):
                cx.op("pe", lambda e, k=k: e.matmul(ps_p[b][:, :256], hn[:, k, bl * 128:(bl + 1) * 128], wq[:, k, QD + 512:QD + 768],
                                                    start=(k == 0), stop=(k == KC - 1)),
                      reads=[r_wq, r_hn[k]], writes=[r_ps_p[b]], signal=(k == KC - 1))
            cx.op("act", lambda e: e.copy(out=V[:, ti * 4 + bl, :], in_=ps_p[b][:, :256]), reads=[r_ps_p[b]], writes=[r_V[ti * 4 + bl]])

    vv_have = {}
    sc = [0]
    for ti in range(T // 512):
        t0 = ti * 512
        load_norm(t0)
        run_chains([(c * 128, 0, qT[:, c, :], r_qT[c]) for c in range(KC)])
        for bl in range(4):
            nb = ti * 4 + bl
            qs = slice(bl * 128, (bl + 1) * 128)
            kbs = [kb for kb in (nb - 1, nb, nb + 1) if 0 <= kb < NB]
            for kb in kbs:
                if vv_have.get(kb % 4) != kb:
                    for d2 in range(2):
                        cx.op("pool", lambda e, d2=d2: e.tensor_copy(vv[:, kb % 4, :, d2, :], V[:, kb, :].rearrange("p (j d) -> p j d", j=NKV)),
                              reads=[r_V[kb]], writes=[r_vv[kb % 4]])
                    vv_have[kb % 4] = kb
            for j in range(NKV):
                pts = []
                for kb in kbs:
                    sb = sc[0] % 2; pb = sc[0] % 4; sc[0] += 1
                    for half in range(2):
                        rows = slice(half * 64, half * 64 + 64)
                        cx.op("pe", lambda e: e.matmul(ps_s[sb][half][:, :256], kT[rows, j, kb * 128:(kb + 1) * 128],
                                                       qT[rows, 2 * j:2 * j + 2, qs], start=True, stop=True),
                              reads=[r_kT[kb // 4], r_qT[2 * j], r_qT[2 * j + 1]], writes=[r_ps_s[sb][half]])
                    for half in range(2):
                        cx.op("act", lambda e: e.activation(out=PT[pb][:, half * 256:(half + 1) * 256], in_=ps_s[sb][half][:, :256], func=AF.Exp, scale=HD ** -0.5),
                              reads=[r_ps_s[sb][half]], writes=[r_PT[pb]])
                    if kb != nb:
                        sgn = 1 if kb < nb else -1
                        cx.op("pool", lambda e: e.affine_select(out=PT[pb][:, :].rearrange("p (a q) -> p a q", a=4),
                                                                in_=PT[pb][:, :].rearrange("p (a q) -> p a q", a=4),
                                                                pattern=[[0, 4], [-sgn, 128]], compare_op=ALU.is_ge, fill=cx.fill0,
                                                                base=0, channel_multiplier=sgn),
                              reads=[r_PT[pb]], writes=[r_PT[pb]])
                    pts.append((kb, pb))
                for i, (kb, pb) in enumerate(pts):
                    cx.op("pe", lambda e: e.matmul(ps_o[:, :], vv[:, kb % 4, j, :, :], PT[pb][:, :], start=(i == 0), stop=(i == len(pts) - 1)),
                          reads=[r_vv[kb % 4], r_PT[pb]], writes=[r_ps_o], signal=(i == len(pts) - 1))
                for i, (kb, pb) in enumerate(pts):
                    cx.op("pe", lambda e: e.matmul(ps_d[:, :], onesb[:], PT[pb][:, :], start=(i == 0), stop=(i == len(pts) - 1)),
                          reads=[r_onesb, r_PT[pb]], writes=[r_ps_d], signal=(i == len(pts) - 1))
                heads = (4 * j, 4 * j + 2, 4 * j + 1, 4 * j + 3)
                for a, h in enumerate(heads):
                    cx.op("dve", lambda e: e.tensor_scalar(out=rd[:, a * 128:(a + 1) * 128], in0=ps_d[:, a * 128:(a + 1) * 128],
                                                           scalar1=esk[:, h:h + 1], scalar2=None, op0=ALU.add),
                          reads=[r_ps_d, r_esk], writes=[r_rd])
                cx.op("act", lambda e: e.activation(out=rd[:, :], in_=rd[:, :], func=AF.Ln), reads=[r_rd], writes=[r_rd])
                cx.op("act", lambda e: e.activation(out=rd[:, :], in_=rd[:, :], func=AF.Exp, scale=-1.0), reads=[r_rd], writes=[r_rd])
                for half in range(2):
                    rows = slice(half * 64, half * 64 + 64)
                    cx.op("dve", lambda e: e.tensor_tensor(out=oT[rows, 2 * j:2 * j + 2, qs],
                                                           in0=ps_o[rows, half * 256:(half + 1) * 256].rearrange("p (a q) -> p a q", a=2),
                                                           in1=rd[rows, half * 256:(half + 1) * 256].rearrange("p (a q) -> p a q", a=2), op=ALU.mult),
                          reads=[r_ps_o, r_rd], writes=[r_oT[2 * j], r_oT[2 * j + 1]])
        for m in range(KC):
            b = pp[0] % 2; pp[0] += 1
            for c in range(KC):
                cx.op("pe", lambda e, c=c: e.matmul(ps_p[b][:, :], wo[:, c, m * 128:(m + 1) * 128], oT[:, c, :],
                                                    start=(c == 0), stop=(c == KC - 1)),
                      reads=[r_wo, r_oT[c]], writes=[r_ps_p[b]], signal=(c == KC - 1))
            cx.op("dve", lambda e: e.tensor_tensor(out=xw[:, m, :], in0=ps_p[b][:, :], in1=xw[:, m, :], op=ALU.add),
                  reads=[r_ps_p[b], r_xw], writes=[r_xw])
        cx.dma("sp", x_out_v[:, :, PAD + t0:PAD + t0 + 512], xw[:, :, :], reads=[r_xw], writes=[w["r_xout"]], slot=r_xw)
    cx.barrier()
    cx.es = old_es
    es.close()


DI, NHS, NG, DS = 2048, 32, 8, 128
CONVD = DI + 2 * NG * DS
SIN = DI + CONVD + 2 * NHS


def ssd_layer(cx, cm, T, x_in, x_out, w, scr):
    NCH = T // 128
    x_in_v = x_in.rearrange("(k p) t -> p k t", p=128)
    x_out_v = x_out.rearrange("(k p) t -> p k t", p=128)
    es = ExitStack(); old_es = cx.es; cx.es = es
    win = cx.sbuf("win", [128, KC, SIN], BF16); r_win = Res("win")
    cwb = cx.sbuf("cwb", [128, 32, 6], F32); r_cwb = Res("cwb")
    gn = cx.sbuf("gn", [128, KC], F32); r_gn = Res("gn")
    dtp = cx.sbuf("dtp", [64, 2], F32); r_dtp = Res("dtp")
    idb = cx.sbuf("idb", [128, 128], BF16); r_idb = Res("idb")
    idf = cx.sbuf("idf", [128, 128], F32); r_idf = Res("idf")
    xws = [cx.sbuf(f"xw{i}", [128, KC, 512], F32) for i in range(2)]; r_xws = [Res("xw0"), Res("xw1")]
    hns = [cx.sbuf(f"hn{i}", [128, KC, 512], BF16) for i in range(2)]
    r_hns = [[Res(f"hn{i}_{k}") for k in range(KC)] for i in range(2)]
    sq = [cx.sbuf(f"sq{i}", [128, 512], F32) for i in range(2)]; r_sq = [Res("sq0"), Res("sq1")]
    rstd = cx.sbuf("rstd", [128, 512], F32); r_rstd = Res("rstd")
    tcv = [cx.sbuf(f"tcv{i}", [128, 384], F32) for i in range(2)]; r_tcv = [Res(f"tcv{i}") for i in range(2)]
    xbc = [cx.sbuf(f"xbc{i}", [128, 384], BF16) for i in range(2)]; r_xbc = [Res(f"xbc{i}") for i in range(2)]
    tok = cx.sbuf("tok", [128, 3, DI + NG * DS], BF16); r_tok = Res("tok")
    szt = cx.sbuf("szt", [128, DI], BF16); r_szt = Res("szt")
    aT = cx.sbuf("aT", [64, 384], F32); r_aT = Res("aT")
    dT = cx.sbuf("dT", [64, 384], F32); r_dT = Res("dT")
    acT = cx.sbuf("acT", [64, 384], F32); r_acT = Res("acT")
    onesf = cx.sbuf("onesf", [64, 384], F32); r_onesf = Res("onesf")
    tk2 = cx.sbuf("tk2", [128, 2, 64], F32); r_tk2 = Res("tk2")
    tot1 = cx.sbuf("tot1", [64, 1], F32); r_tot1 = Res("tot1")
    ps_st = cx.psum("ps_st", [128, 512], F32); r_ps_st = Res("ps_st", excl=True)
    ps_c = [cx.psum(f"ps_c{i}", [128, 512], F32) for i in range(2)]; r_ps_c = [Res(f"ps_c{i}", excl=True) for i in range(2)]
    ps_t = [cx.psum(f"ps_t{i}", [128, 1024], BF16) for i in range(2)]; r_ps_t = [Res(f"ps_t{i}", excl=True) for i in range(2)]
    ps_z = [cx.psum(f"ps_z{i}", [128, 512], F32) for i in range(2)]; r_ps_z = [Res(f"ps_z{i}", excl=True) for i in range(2)]
    ps_f = cx.psum("ps_f", [128, 512], F32); r_ps_f = Res("ps_f", excl=True)

    for dst, r, key in ((cwb, r_cwb, "cwb"), (gn, r_gn, "gn"), (dtp, r_dtp, "dtp"), (idf, r_idf, "ident")):
        cx.dma("sp", dst[:], w[key], writes=[r], slot=r)
    load_weight_bf16(cx, win, r_win, w["win"], KC)
    cx.op("dve", lambda e: e.tensor_copy(idb[:], idf[:]), reads=[r_idf], writes=[r_idb])
    cx.op("pool", lambda e: e.memset(onesf[:], 1.0), writes=[r_onesf])
    cx.op("act", lambda e: e.activation(out=dtp[:, 1:2], in_=dtp[:, 1:2], func=AF.Exp), reads=[r_dtp], writes=[r_dtp])
    cx.op("dve", lambda e: e.tensor_scalar(out=dtp[:, 1:2], in0=dtp[:, 1:2], scalar1=-1.0, scalar2=None, op0=ALU.mult),
          reads=[r_dtp], writes=[r_dtp])
    cc = [0]
    wins = [(t0, min(384, T - t0)) for t0 in range(0, T, 384)]

    def a_load(i):
        t0, n_out = wins[i]; W = n_out + 4
        cx.dma("sp", xws[i % 2][:, :, :W], x_in_v[:, :, PAD + t0 - 2:PAD + t0 - 2 + W], reads=[w["r_xin"]], writes=[r_xws[i % 2]], slot=r_xws[i % 2])

    def a_norm(i):
        t0, n_out = wins[i]
        rmsnorm_fm(cx, cm, xws[i % 2], r_xws[i % 2], n_out + 4, lambda k: gn[:, k:k + 1], r_gn, hns[i % 2], r_hns[i % 2], ps_st, r_ps_st, sq, r_sq, rstd, r_rstd)

    a_load(0)
    if len(wins) > 1:
        a_load(1)
    a_norm(0)
    for wi, (t0, n_out) in enumerate(wins):
        W = n_out + 4; nblk = n_out // 128
        hn = hns[wi % 2]; r_hn = r_hns[wi % 2]
        def front(c):
            b = c % 2
            col0 = DI + c * 128
            for k in range(KC):
                cx.op("pe", lambda e, k=k: e.matmul(ps_c[b][:, :W], win[:, k, col0:col0 + 128], hn[:, k, :W], start=(k == 0), stop=(k == KC - 1)),
                      reads=[r_win, r_hn[k]], writes=[r_ps_c[b]], signal=(k == KC - 1))
            cx.op("act", lambda e: e.activation(out=tcv[b][:, :n_out], in_=ps_c[b][:, 2:2 + n_out], func=AF.Identity,
                                                bias=cwb[:, c, 5:6], scale=cwb[:, c, 2:3]),
                  reads=[r_ps_c[b], r_cwb], writes=[r_tcv[b]])
            for kk in (0, 1, 3, 4):
                cx.op("dve", lambda e, kk=kk: e.scalar_tensor_tensor(out=tcv[b][:, :n_out], in0=ps_c[b][:, kk:kk + n_out], scalar=cwb[:, c, kk:kk + 1],
                                                                     in1=tcv[b][:, :n_out], op0=ALU.mult, op1=ALU.add),
                      reads=[r_ps_c[b], r_cwb, r_tcv[b]], writes=[r_tcv[b]])
            cx.op("act", lambda e: e.activation(out=xbc[b][:, :n_out], in_=tcv[b][:, :n_out], func=AF.Silu), reads=[r_tcv[b]], writes=[r_xbc[b]])

        def back(c):
            b = c % 2
            if c < 24:
                for bl in range(nblk):
                    cx.op("pe", lambda e, bl=bl: e.transpose(ps_t[b][:, bl * 128:(bl + 1) * 128], xbc[b][:, bl * 128:(bl + 1) * 128], idb[:]),
                          reads=[r_xbc[b], r_idb], writes=[r_ps_t[b]])
                cx.op("dve", lambda e: e.tensor_copy(tok[:, :nblk, c * 128:(c + 1) * 128], ps_t[b][:, :nblk * 128].rearrange("p (a q) -> p a q", a=nblk)),
                      reads=[r_ps_t[b]], writes=[r_tok])
            if c >= 16:
                dst = scr["BT"] if c < 24 else scr["CT"]
                g = (c - 16) % 8
                cx.dma("sp", dst[g * 128:(g + 1) * 128, t0:t0 + n_out], xbc[b][:, :n_out], reads=[r_xbc[b]], writes=[scr["r_BC"]], slot=r_xbc[b])

        front(0)
        for c in range(32):
            if c == 20 and wi + 1 < len(wins):
                a_norm(wi + 1)
                if wi + 2 < len(wins):
                    a_load(wi + 2)
            if c + 1 < 32:
                front(c + 1)
            back(c)
        for bl in range(nblk):
            r0 = t0 + bl * 128
            cx.dma("sp", scr["xtok"][r0:r0 + 128, :], tok[:, bl, :DI], reads=[r_tok], writes=[scr["r_tok"]], slot=r_tok)
            cx.dma("sp", scr["Btok"][r0:r0 + 128, :], tok[:, bl, DI:], reads=[r_tok], writes=[scr["r_tok"]], slot=r_tok)
        for bl in range(nblk):
            r0 = t0 + bl * 128
            for ct in range(4):
                b = cc[0] % 2; cc[0] += 1
                for k in range(KC):
                    cx.op("pe", lambda e, k=k: e.matmul(ps_z[b][:, :], hn[:, k, 2 + bl * 128:2 + (bl + 1) * 128], win[:, k, ct * 512:(ct + 1) * 512],
                                                        start=(k == 0), stop=(k == KC - 1)),
                          reads=[r_win, r_hn[k]], writes=[r_ps_z[b]], signal=(k == KC - 1))
                cx.op("act", lambda e: e.activation(out=szt[:, ct * 512:(ct + 1) * 512], in_=ps_z[b][:, :], func=AF.Silu),
                      reads=[r_ps_z[b]], writes=[r_szt])
            cx.dma("sp", scr["sz"][r0:r0 + 128, :], szt[:, :], reads=[r_szt], writes=[scr["r_sz"]], slot=r_szt)
        for k in range(KC):
            cx.op("pe", lambda e, k=k: e.matmul(ps_f[:64, :n_out], win[:, k, DI + CONVD:SIN], hn[:, k, 2:2 + n_out], start=(k == 0), stop=(k == KC - 1)),
                  reads=[r_win, r_hn[k]], writes=[r_ps_f], signal=(k == KC - 1))
        cx.op("act", lambda e: e.activation(out=dT[:, :n_out], in_=ps_f[:64, :n_out], func=AF.Exp, bias=dtp[:, 0:1]),
              reads=[r_ps_f, r_dtp], writes=[r_dT])
        cx.op("act", lambda e: e.activation(out=dT[:, :n_out], in_=dT[:, :n_out], func=AF.Ln, bias=cm.one[:64, 0:1]),
              reads=[r_dT, cm.r_one], writes=[r_dT])
        cx.op("dve", lambda e: e.tensor_scalar(out=aT[:, :n_out], in0=dT[:, :n_out], scalar1=dtp[:, 1:2], scalar2=None, op0=ALU.mult),
              reads=[r_dT, r_dtp], writes=[r_aT])
        for bl in range(nblk):
            cs_ = slice(bl * 128, (bl + 1) * 128)
            cx.op("dve", lambda e: e.tensor_tensor_scan(out=acT[:, cs_], data0=onesf[:, cs_], data1=aT[:, cs_], initial=0.0, op0=ALU.mult, op1=ALU.add),
                  reads=[r_onesf, r_aT], writes=[r_acT])
            cx.op("dve", lambda e: e.tensor_copy(tot1[32:64, :], acT[32:64, bl * 128 + 127:bl * 128 + 128]), reads=[r_acT], writes=[r_tot1])
            cx.op("dve", lambda e: e.scalar_tensor_tensor(out=acT[32:64, cs_], in0=acT[32:64, cs_], scalar=-1.0, in1=aT[32:64, cs_], op0=ALU.mult, op1=ALU.add),
                  reads=[r_acT, r_aT], writes=[r_acT])
            cx.op("dve", lambda e: e.tensor_scalar(out=acT[32:64, cs_], in0=acT[32:64, cs_], scalar1=tot1[32:64, 0:1], scalar2=None, op0=ALU.add),
                  reads=[r_acT, r_tot1], writes=[r_acT])
            r0 = t0 + bl * 128
            for i, src in enumerate((acT, dT)):
                cx.op("pe", lambda e, src=src: e.transpose(ps_f[:, 128 + i * 64:128 + (i + 1) * 64], src[:, cs_], idf[:64, :64]),
                      reads=[r_acT, r_dT, r_idf], writes=[r_ps_f])
            cx.op("act", lambda e: e.copy(out=tk2[:, :, :], in_=ps_f[:, 128:256].rearrange("p (a q) -> p a q", a=2)), reads=[r_ps_f], writes=[r_tk2])
            cx.dma("sp", scr["actok"][r0:r0 + 128, :], tk2[:, 0, :], reads=[r_tk2], writes=[scr["r_ac"]], slot=r_tk2)
            cx.dma("sp", scr["dttok"][r0:r0 + 128, :], tk2[:, 1, :], reads=[r_tk2], writes=[scr["r_ac"]], slot=r_tk2)
        cx.dma("sp", scr["acT"][:, t0:t0 + n_out], acT[:, :n_out], reads=[r_acT], writes=[scr["r_ac"]], slot=r_acT)
    cx.barrier()
    cx.es = old_es
    es.close()
    ssd_phase_b(cx, cm, T, x_in_v, x_out_v, w, scr)


def ssd_phase_b(cx, cm, T, x_in_v, x_out_v, w, scr):
    NCH = T // 128
    es = ExitStack(); old_es = cx.es; cx.es = es
    wout = cx.sbuf("wout", [128, 16, D], BF16); r_wout = Res("wout")
    dbc = cx.sbuf("dbc", [128, DI], F32); r_dbc = Res("dbc")
    gbc = cx.sbuf("gbc", [128, DI], F32); r_gbc = Res("gbc")
    idb = cx.sbuf("idb", [128, 128], BF16); r_idb = Res("idb")
    idf = cx.sbuf("idf", [128, 128], F32); r_idf = Res("idf")
    def two(name, shape, dt):
        return [cx.sbuf(f"{name}{i}", shape, dt) for i in range(2)], [Res(f"{name}{i}") for i in range(2)]
    xts, r_xts = two("xt", [128, DI], BF16)
    bts, r_bts = two("bt", [128, NG * DS], BF16)
    BTss, r_BTss = two("BTs", [128, NG, 128], BF16)
    CTss, r_CTss = two("CTs", [128, NG, 128], BF16)
    dtts, r_dtts = two("dtt", [128, 64], F32)
    acts, r_acts = two("act", [128, 64], F32)
    totbs, r_totbs = two("totb", [128, 32], F32)
    rowbs, r_rowbs = two("rowb", [128, 32, 128], F32)
    sm = cx.sbuf("sm", [128, 4, 32], F32); r_sm = Res("sm")
    cbt = cx.sbuf("cbt", [128, NG, 128], F32); r_cbt = [Res(f"cbt{g}") for g in range(NG)]
    dif = [cx.sbuf(f"dif{i}", [128, 4, 128], F32) for i in range(2)]; r_dif = [Res(f"dif{i}") for i in range(2)]
    MT = [cx.sbuf(f"MT{i}", [128, 4, 128], BF16) for i in range(2)]; r_MT = [Res(f"MT{i}") for i in range(2)]
    Bw = [cx.sbuf(f"Bw{i}", [128, 4, 128], BF16) for i in range(2)]; r_Bw = [Res(f"Bw{i}") for i in range(2)]
    HT = cx.sbuf("HT", [128, NHS, 64], F32); r_HT = [Res(f"HT{g}") for g in range(NG)]
    HTb = cx.sbuf("HTb", [128, NHS, 64], BF16); r_HTb = [Res(f"HTb{g}") for g in range(NG)]
    xdt = cx.sbuf("xdt", [128, DI], BF16); r_xdt = Res("xdt")
    yof = cx.sbuf("yof", [128, 512], F32); r_yof = Res("yof")
    dsk = cx.sbuf("dsk", [128, DI], F32); r_dsk = Res("dsk")
    ysb = cx.sbuf("ysb", [128, DI], F32); r_ysb = [Res(f"ysb{q}") for q in range(4)]
    y0 = cx.sbuf("y0", [128, DI], F32); r_y0 = Res("y0")
    szt = cx.sbuf("szt", [128, DI], BF16); r_szt = Res("szt")
    gss = cx.sbuf("gss", [128, NG], F32); r_gss = Res("gss")
    yb = cx.sbuf("yb", [128, DI], BF16); r_yb = Res("yb")
    yT = cx.sbuf("yT", [128, 16, 128], BF16); r_yT = Res("yT")
    xw = cx.sbuf("xw", [128, KC, 128], F32); r_xw = Res("xw")
    ps_cb = cx.psum("ps_cb", [128, 512], F32); r_ps_cb = Res("ps_cb", excl=True)
    ps_y = [cx.psum(f"ps_y{i}", [128, 512], F32) for i in range(2)]; r_ps_y = [Res(f"ps_y{i}", excl=True) for i in range(2)]
    ps_o = [cx.psum(f"ps_of{i}", [128, 512], F32) for i in range(2)]; r_ps_o = [Res(f"ps_of{i}", excl=True) for i in range(2)]
    ps_s = [cx.psum(f"ps_st{i}", [128, 512], F32) for i in range(2)]; r_ps_s = [Res(f"ps_st{i}", excl=True) for i in range(2)]
    ps_t = cx.psum("ps_tr", [128, 1024], BF16); r_ps_t = Res("ps_tr", excl=True)

    for dst, r, key in ((dbc, r_dbc, "dbc"), (gbc, r_gbc, "gbc"), (idf, r_idf, "ident")):
        cx.dma("sp", dst[:], w[key], writes=[r], slot=r)
    load_weight_bf16(cx, wout, r_wout, w["wout"], 16)
    cx.op("dve", lambda e: e.tensor_copy(idb[:], idf[:]), reads=[r_idf], writes=[r_idb])
    BTv = scr["BT"].rearrange("(g p) t -> p g t", p=128)
    CTv = scr["CT"].rearrange("(g p) t -> p g t", p=128)
    hc = [0]
    iters = [(0, c) for c in range(NCH)] + [(1, c) for c in range(NCH - 1, -1, -1)]

    def load_inputs(i):
        d, c = iters[i]; p = i % 2
        r0 = c * 128
        dc = slice(d * 32, d * 32 + 32)
        cx.dma("sp", xts[p][:], scr["xtok"][r0:r0 + 128, :], reads=[scr["r_tok"]], writes=[r_xts[p]], slot=r_xts[p])
        cx.dma("sp", bts[p][:], scr["Btok"][r0:r0 + 128, :], reads=[scr["r_tok"]], writes=[r_bts[p]], slot=r_bts[p])
        cx.dma("sp", BTss[p][:], BTv[:, :, r0:r0 + 128], reads=[scr["r_BC"]], writes=[r_BTss[p]], slot=r_BTss[p])
        cx.dma("sp", CTss[p][:], CTv[:, :, r0:r0 + 128], reads=[scr["r_BC"]], writes=[r_CTss[p]], slot=r_CTss[p])
        cx.dma("sp", dtts[p][:], scr["dttok"][r0:r0 + 128, :], reads=[scr["r_ac"]], writes=[r_dtts[p]], slot=r_dtts[p])
        cx.dma("sp", acts[p][:], scr["actok"][r0:r0 + 128, :], reads=[scr["r_ac"]], writes=[r_acts[p]], slot=r_acts[p])
        rl = r0 + 127 if d == 0 else r0
        cx.dma("sp", totbs[p][:], scr["actok"][rl:rl + 1, dc].partition_broadcast(128), reads=[scr["r_ac"]], writes=[r_totbs[p]], slot=r_totbs[p])
        cx.dma("sp", rowbs[p][:], scr["acT"][d * 32:d * 32 + 32, r0:r0 + 128].partition_broadcast(128), reads=[scr["r_ac"]], writes=[r_rowbs[p]], slot=r_rowbs[p])

    load_inputs(0)
    for it, (d, c) in enumerate(iters):
        if c == (0 if d == 0 else NCH - 1):
            cx.op("pool", lambda e: e.memset(HT[:, :, :], 0.0), writes=r_HT)
            cx.op("pool", lambda e: e.memset(HTb[:, :, :], 0.0), writes=r_HTb)
        dc = slice(d * 32, d * 32 + 32)
        if True:
            r0 = c * 128
            p = it % 2
            xt, r_xt, bt, r_bt, BTs, r_BTs, CTs, r_CTs = xts[p], r_xts[p], bts[p], r_bts[p], BTss[p], r_BTss[p], CTss[p], r_CTss[p]
            dtt, r_dtt, act, r_act, totb, r_totb, rowb, r_rowb = dtts[p], r_dtts[p], acts[p], r_acts[p], totbs[p], r_totbs[p], rowbs[p], r_rowbs[p]
            if it + 1 < len(iters):
                load_inputs(it + 1)
            if d == 1:
                cx.dma("sp", y0[:], scr["y0"][r0:r0 + 128, :], reads=[scr["r_y0"]], writes=[r_y0], slot=r_y0)
                cx.dma("sp", szt[:], scr["sz"][r0:r0 + 128, :], reads=[scr["r_sz"]], writes=[r_szt], slot=r_szt)
                cx.dma("sp", xw[:], x_in_v[:, :, PAD + r0:PAD + r0 + 128], reads=[w["r_xin"]], writes=[r_xw], slot=r_xw)
                cx.op("dve", lambda e: e.tensor_tensor(out=dsk[:, :], in0=xt[:, :], in1=dbc[:, :], op=ALU.mult), reads=[r_xt, r_dbc], writes=[r_dsk])
                cx.op("pool", lambda e: e.tensor_tensor(out=y0[:, :], in0=y0[:, :], in1=dsk[:, :], op=ALU.add), reads=[r_y0, r_dsk], writes=[r_y0])
            cx.op("dve", lambda e: e.tensor_scalar(out=sm[:, 0, :], in0=act[:, dc], scalar1=-1.0, scalar2=None, op0=ALU.mult), reads=[r_act], writes=[r_sm])
            cx.op("dve", lambda e: e.tensor_tensor(out=sm[:, 1, :], in0=totb[:, :], in1=act[:, dc], op=ALU.subtract), reads=[r_totb, r_act, r_sm], writes=[r_sm])
            cx.op("act", lambda e: e.activation(out=sm[:, 1, :], in_=sm[:, 1, :], func=AF.Exp), reads=[r_sm], writes=[r_sm])
            cx.op("act", lambda e: e.activation(out=sm[:, 2, :], in_=totb[:, :], func=AF.Exp), reads=[r_totb, r_sm], writes=[r_sm])
            cx.op("act", lambda e: e.activation(out=sm[:, 3, :], in_=act[:, dc], func=AF.Exp), reads=[r_act, r_sm], writes=[r_sm])
            for g4 in range(2):
                for gq in range(4):
                    g = g4 * 4 + gq
                    cx.op("pe", lambda e: e.matmul(ps_cb[:, gq * 128:(gq + 1) * 128], BTs[:, g, :], CTs[:, g, :], start=True, stop=True),
                          reads=[r_BTs, r_CTs], writes=[r_ps_cb])
                cx.op("act", lambda e: e.copy(out=cbt[:, g4 * 4:g4 * 4 + 4, :], in_=ps_cb[:, :].rearrange("p (g l) -> p g l", g=4)),
                      reads=[r_ps_cb], writes=r_cbt[g4 * 4:g4 * 4 + 4])
            cx.op("dve", lambda e: e.tensor_tensor(out=xdt[:, :].rearrange("p (h q) -> p h q", h=NHS), in0=xt[:, :].rearrange("p (h q) -> p h q", h=NHS),
                                                   in1=dtt[:, dc].unsqueeze(2).to_broadcast([128, NHS, 64]), op=ALU.mult),
                  reads=[r_xt, r_dtt], writes=[r_xdt])
            sgn = 1 if d == 0 else -1
            B4 = [128, 4, 128]

            def stage1(g):
                i2 = g % 2; hs = slice(4 * g, 4 * g + 4)
                cx.op("dve", lambda e: e.tensor_tensor(out=dif[i2][:, :, :], in0=rowb[:, hs, :], in1=sm[:, 0, hs].unsqueeze(2).to_broadcast(B4), op=ALU.add),
                      reads=[r_rowb, r_sm], writes=[r_dif[i2]])
                cx.op("pool", lambda e: e.affine_select(out=dif[i2][:, :, :], in_=dif[i2][:, :, :], pattern=[[0, 4], [sgn, 128]], compare_op=ALU.is_ge,
                                                        fill=cx.fillneg, base=0, channel_multiplier=-sgn),
                      reads=[r_dif[i2]], writes=[r_dif[i2]])
                cx.op("act", lambda e: e.activation(out=dif[i2][:, :, :], in_=dif[i2][:, :, :], func=AF.Exp), reads=[r_dif[i2]], writes=[r_dif[i2]])
                cx.op("pool", lambda e: e.tensor_tensor(out=Bw[i2][:, :, :], in0=bt[:, g * 128:(g + 1) * 128].unsqueeze(1).to_broadcast(B4),
                                                        in1=sm[:, 1, hs].unsqueeze(2).to_broadcast(B4), op=ALU.mult),
                      reads=[r_bt, r_sm], writes=[r_Bw[i2]])

            def stage2(g):
                i2 = g % 2; hs = slice(4 * g, 4 * g + 4); q = g // 2; gg = g % 2
                yb_, ob_ = ps_y[q % 2], ps_o[q % 2]
                r_yb_, r_ob_ = r_ps_y[q % 2], r_ps_o[q % 2]
                sb = ps_s[i2]; r_sb = r_ps_s[i2]
                for hq in range(4):
                    h = 4 * g + hq; hh = gg * 4 + hq
                    cx.op("pe", lambda e: e.matmul(ob_[:, hh * 64:(hh + 1) * 64], CTs[:, g, :], HTb[:, h, :], start=True, stop=True),
                          reads=[r_CTs, r_HTb[g]], writes=[r_ob_])
                    cx.op("pe", lambda e: e.matmul(sb[:, hq * 64:(hq + 1) * 64], Bw[i2][:, hq, :], xdt[:, h * 64:(h + 1) * 64], start=True, stop=True),
                          reads=[r_Bw[i2], r_xdt], writes=[r_sb])
                cx.op("dve", lambda e: e.tensor_tensor(out=MT[i2][:, :, :], in0=dif[i2][:, :, :], in1=cbt[:, g, :].unsqueeze(1).to_broadcast(B4), op=ALU.mult),
                      reads=[r_dif[i2], r_cbt[g]], writes=[r_MT[i2]])
                cx.op("dve", lambda e: e.tensor_tensor(out=HT[:, hs, :], in0=HT[:, hs, :], in1=sm[:, 2, hs].unsqueeze(2).to_broadcast([128, 4, 64]), op=ALU.mult),
                      reads=[r_HT[g], r_sm], writes=[r_HT[g]])
                for hq in range(4):
                    h = 4 * g + hq; hh = gg * 4 + hq
                    cx.op("pe", lambda e: e.matmul(yb_[:, hh * 64:(hh + 1) * 64], MT[i2][:, hq, :], xdt[:, h * 64:(h + 1) * 64], start=True, stop=True),
                          reads=[r_MT[i2], r_xdt], writes=[r_yb_])
                cx.op("dve", lambda e: e.tensor_tensor(out=HT[:, hs, :], in0=sb[:, :256].rearrange("p (h q) -> p h q", h=4), in1=HT[:, hs, :], op=ALU.add),
                      reads=[r_HT[g], r_sb], writes=[r_HT[g]])
                cx.op("act", lambda e: e.copy(out=HTb[:, hs, :], in_=HT[:, hs, :]), reads=[r_HT[g]], writes=[r_HTb[g]])
                if gg == 1:
                    qs = slice(q * 512, (q + 1) * 512)
                    h8 = slice(q * 8, q * 8 + 8)
                    cx.op("act", lambda e: e.copy(out=ysb[:, qs], in_=yb_[:, :]), reads=[r_yb_], writes=[r_ysb[q]])
                    cx.op("dve", lambda e: e.tensor_tensor(out=yof[:, :].rearrange("p (h q) -> p h q", h=8), in0=ob_[:, :].rearrange("p (h q) -> p h q", h=8),
                                                           in1=sm[:, 3, h8].unsqueeze(2).to_broadcast([128, 8, 64]), op=ALU.mult),
                          reads=[r_ob_, r_sm], writes=[r_yof])
                    cx.op("pool", lambda e: e.tensor_tensor(out=ysb[:, qs], in0=ysb[:, qs], in1=yof[:, :], op=ALU.add), reads=[r_yof, r_ysb[q]], writes=[r_ysb[q]])

            stage1(0)
            for g in range(NG):
                if g + 1 < NG:
                    stage1(g + 1)
                stage2(g)
            if d == 0:
                cx.dma("sp", scr["y0"][r0:r0 + 128, :], ysb[:, :], reads=r_ysb, writes=[scr["r_y0"]], slot=r_ysb[0])
                continue
            cx.op("pool", lambda e: e.tensor_tensor(out=y0[:, :], in0=y0[:, :], in1=ysb[:, :], op=ALU.add), reads=[r_y0] + r_ysb, writes=[r_y0])
            cx.op("dve", lambda e: e.tensor_tensor(out=y0[:, :], in0=y0[:, :], in1=szt[:, :], op=ALU.mult), reads=[r_y0, r_szt], writes=[r_y0])
            cx.op("pool", lambda e: e.tensor_tensor(out=ysb[:, :], in0=y0[:, :], in1=y0[:, :], op=ALU.mult), reads=[r_y0] + r_ysb, writes=r_ysb)
            cx.op("dve", lambda e: e.tensor_reduce(out=gss[:, :], in_=ysb[:, :].rearrange("p (g f) -> p g f", g=NG), axis=mybir.AxisListType.X, op=ALU.add),
                  reads=r_ysb, writes=[r_gss])
            cx.op("act", lambda e: e.activation(out=gss[:, :], in_=gss[:, :], func=AF.Ln, bias=cm.eps[:, 0:1], scale=1.0 / 256), reads=[r_gss, cm.r_eps], writes=[r_gss])
            cx.op("act", lambda e: e.activation(out=gss[:, :], in_=gss[:, :], func=AF.Exp, scale=-0.5), reads=[r_gss], writes=[r_gss])
            for g in range(NG):
                gs = slice(g * 256, (g + 1) * 256)
                cx.op("dve", lambda e: e.scalar_tensor_tensor(out=yb[:, gs], in0=y0[:, gs], scalar=gss[:, g:g + 1], in1=gbc[:, gs], op0=ALU.mult, op1=ALU.mult),
                      reads=[r_y0, r_gss, r_gbc], writes=[r_yb])
            for half in range(2):
                for cq in range(8):
                    cch = half * 8 + cq
                    cx.op("pe", lambda e: e.transpose(ps_t[:, cq * 128:(cq + 1) * 128], yb[:, cch * 128:(cch + 1) * 128], idb[:]),
                          reads=[r_yb, r_idb], writes=[r_ps_t])
                cx.op("act", lambda e: e.copy(out=yT[:, half * 8:(half + 1) * 8, :], in_=ps_t[:, :].rearrange("p (a q) -> p a q", a=8)),
                      reads=[r_ps_t], writes=[r_yT])
            for m in range(KC):
                pb = ps_y[m % 2]; r_pb = r_ps_y[m % 2]
                for cch in range(16):
                    cx.op("pe", lambda e, cch=cch: e.matmul(pb[:, :128], wout[:, cch, m * 128:(m + 1) * 128], yT[:, cch, :], start=(cch == 0), stop=(cch == 15)),
                          reads=[r_wout, r_yT], writes=[r_pb], signal=(cch == 15))
                cx.op("dve", lambda e: e.tensor_tensor(out=xw[:, m, :], in0=pb[:, :128], in1=xw[:, m, :], op=ALU.add), reads=[r_pb, r_xw], writes=[r_xw])
            cx.dma("sp", x_out_v[:, :, PAD + r0:PAD + r0 + 128], xw[:, :, :], reads=[r_xw], writes=[w["r_xout"]], slot=r_xw)
    cx.barrier()
    cx.es = old_es
    es.close()


def make_ssd_scratch(nc, T):
    d = lambda n, s, dt: nc.dram_tensor(n, s, dt, kind="Internal").ap()
    return dict(xtok=d("s_xtok", [T, DI], BF16), Btok=d("s_btok", [T, NG * DS], BF16), BT=d("s_BT", [NG * DS, T], BF16),
                CT=d("s_CT", [NG * DS, T], BF16), sz=d("s_sz", [T, DI], BF16), actok=d("s_actok", [T, 64], F32),
                dttok=d("s_dttok", [T, 64], F32), acT=d("s_acT", [64, T], F32), y0=d("s_y0", [T, DI], F32),
                r_tok=Res("s_tok"), r_BC=Res("s_BC"), r_sz=Res("s_sz"), r_ac=Res("s_ac"), r_y0=Res("s_y0"))


DEPTH = 4
ROT = 16


def build_program(T):
    TP = T + 2 * PAD
    nc = bass.Bass("TRN2", target_bir_lowering=False)
    di = lambda n, s: nc.dram_tensor(n, list(s), F32, kind="ExternalInput").ap()
    xin = di("xin", [D, TP])
    xout = nc.dram_tensor("xout", [D, TP], F32, kind="ExternalOutput").ap()
    xmid = nc.dram_tensor("xmid", [D, TP], F32, kind="Internal").ap()
    bm = di("bm", [128, 128]); rm = di("rm", [128, 128]); ident = di("ident", [128, 128])
    cos = di("cos", [128, T]); sin = di("sin", [128, T])
    A = {}
    for j in range(2):
        A[j] = dict(wqkv=di(f"a{j}_wqkv", [D, WQKV]), wo=di(f"a{j}_wo", [D, D]), gn=di(f"a{j}_gn", [128, KC]), gqk=di(f"a{j}_gqk", [128, 2]),
                    sink=di(f"a{j}_sink", [128, NH]))
    S = {}
    for j in range(2):
        S[j] = dict(win=di(f"s{j}_win", [D, SIN]), wout=di(f"s{j}_wout", [DI, D]), gn=di(f"s{j}_gn", [128, KC]), cwb=di(f"s{j}_cwb", [128, 32, 6]),
                    dtp=di(f"s{j}_dtp", [64, 2]), dbc=di(f"s{j}_dbc", [128, DI]), gbc=di(f"s{j}_gbc", [128, DI]))
    Fw = {}
    for i in range(DEPTH):
        Fw[i] = dict(wup=di(f"f{i}_wup", [D, 2 * DFF]), wdn=di(f"f{i}_wdn", [DFF, D]), cwb=di(f"f{i}_cwb", [128, NFC, 4]), gn=di(f"f{i}_gn", [128, KC]))
    scr = make_ssd_scratch(nc, T)
    with ExitStack() as es:
        cx = Ctx(nc, es)
        cm = Common(cx)
        r = {"xin": Res("xin"), "xout": Res("xout"), "xmid": Res("xmid")}
        aps = {"xin": xin, "xout": xout, "xmid": xmid}
        zes = ExitStack(); cx.es = zes
        zt = cx.sbuf("zt", [128, KC, PAD], F32); r_zt = Res("zt")
        cx.op("pool", lambda e: e.memset(zt[:], 0.0), writes=[r_zt])
        for nm in ("xout", "xmid"):
            v = aps[nm].rearrange("(k p) t -> p k t", p=128)
            cx.dma("sp", v[:, :, 0:PAD], zt[:], reads=[r_zt], writes=[r[nm]], slot=r_zt)
            cx.dma("sp", v[:, :, PAD + T:PAD + T + PAD], zt[:], reads=[r_zt], writes=[r[nm]], slot=r_zt)
        cx.barrier()
        cx.es = es
        zes.close()
        seq = ["xin"] + ["xmid", "xout"] * DEPTH
        step = 0
        for i in range(DEPTH):
            j = i // 2
            src, dst = seq[step], seq[step + 1]; step += 1
            if i % 2 == 0:
                w = dict(wqkv=A[j]["wqkv"].rearrange("(k p) n -> p k n", p=128), wo=A[j]["wo"].rearrange("(k p) n -> p k n", p=128),
                         gn=A[j]["gn"], gqk=A[j]["gqk"], sink=A[j]["sink"], bm=bm, rm=rm, cos=cos, sin=sin, r_xin=r[src], r_xout=r[dst])
                attn_layer(cx, cm, T, aps[src], aps[dst], w)
            else:
                w = dict(win=S[j]["win"].rearrange("(k p) n -> p k n", p=128), wout=S[j]["wout"].rearrange("(k p) n -> p k n", p=128),
                         gn=S[j]["gn"], cwb=S[j]["cwb"], dtp=S[j]["dtp"], dbc=S[j]["dbc"], gbc=S[j]["gbc"], ident=ident, r_xin=r[src], r_xout=r[dst])
                ssd_layer(cx, cm, T, aps[src], aps[dst], w, scr)
            src, dst = seq[step], seq[step + 1]; step += 1
            w = dict(wup=Fw[i]["wup"].rearrange("(k p) n -> p k n", p=128), wdn=Fw[i]["wdn"].rearrange("(c p) n -> p c n", p=128),
                     cwb=Fw[i]["cwb"], gn=Fw[i]["gn"], r_xin=r[src], r_xout=r[dst])
            ffn_layer(cx, cm, T, aps[src], aps[dst], w)
        assert dst == "xout"
        cx.barrier(fresh=False)
    return nc


def host_layout(inp, T):
    f = lambda a: np.ascontiguousarray(np.asarray(a, dtype=np.float32))
    col = lambda v: f(np.asarray(v).reshape(KC, 128).T)
    m = {}
    m["bm"] = f(np.kron(np.eye(2), np.full((64, 64), 1.0 / 64)))
    rm = np.zeros((128, 128), np.float32)
    for blk in (0, 64):
        for q in range(8):
            rm[blk + q + 8, blk + q] = -1.0
            rm[blk + q, blk + q + 8] = 1.0
    m["rm"] = rm
    m["ident"] = np.eye(128, dtype=np.float32)
    pos = np.arange(T, dtype=np.float32)
    inv_freq = (np.float32(500000.0) ** (-(np.arange(0, ROT, 2, dtype=np.float32) / np.float32(ROT)))).astype(np.float32)
    ang = (pos[:, None] * inv_freq[None, :]).astype(np.float32)
    cosv, sinv = np.cos(ang).astype(np.float32), np.sin(ang).astype(np.float32)
    cosT = np.ones((128, T), np.float32); sinT = np.zeros((128, T), np.float32)
    for blk in (0, 64):
        for q in range(16):
            cosT[blk + q] = cosv[:, q % 8]; sinT[blk + q] = sinv[:, q % 8]
    m["cos"], m["sin"] = cosT, sinT
    for j in range(2):
        wq = np.asarray(inp["attn_w_qkv"][j])
        m[f"a{j}_wqkv"] = f(np.concatenate([wq[:, :QD]] + [np.tile(wq[:, QD + k * 64:QD + (k + 1) * 64], (1, 2)) for k in range(NKV)] + [wq[:, QD + 256:]], axis=1))
        m[f"a{j}_wo"] = f(inp["attn_w_o"][j])
        m[f"a{j}_gn"] = col(inp["attn_norm"][j])
        m[f"a{j}_gqk"] = f(np.stack([np.tile(np.asarray(inp["attn_q_norm"][j]), 2), np.tile(np.asarray(inp["attn_k_norm"][j]), 2)], 1))
        m[f"a{j}_sink"] = f(np.tile(np.asarray(inp["attn_sink"][j])[None, :], (128, 1)))
        m[f"s{j}_win"] = f(inp["ssd_w_in"][j])
        m[f"s{j}_wout"] = f(inp["ssd_w_out"][j])
        m[f"s{j}_gn"] = col(inp["ssd_norm"][j])
        cwb = np.zeros((128, 32, 6), np.float32)
        cwb[:, :, 0:5] = np.asarray(inp["ssd_conv_w"][j]).reshape(5, 32, 128).transpose(2, 1, 0)
        cwb[:, :, 5] = np.asarray(inp["ssd_conv_b"][j]).reshape(32, 128).T
        m[f"s{j}_cwb"] = cwb
        m[f"s{j}_dtp"] = f(np.stack([np.asarray(inp["ssd_dt_bias"][j]).reshape(64), np.asarray(inp["ssd_a_log"][j]).reshape(64)], 1))
        m[f"s{j}_dbc"] = f(np.tile(np.repeat(np.asarray(inp["ssd_d"][j]), 64)[None, :], (128, 1)))
        m[f"s{j}_gbc"] = f(np.tile(np.asarray(inp["ssd_gate_norm"][j])[None, :], (128, 1)))
    for i in range(DEPTH):
        m[f"f{i}_wup"] = f(inp["ffn_w_up"][i])
        m[f"f{i}_wdn"] = f(inp["ffn_w_down"][i])
        cwb = np.zeros((128, NFC, 4), np.float32)
        cwb[:, :, 0:3] = np.asarray(inp["ffn_conv_w"][i]).reshape(3, NFC, 128).transpose(2, 1, 0)
        cwb[:, :, 3] = np.asarray(inp["ffn_conv_b"][i]).reshape(NFC, 128).T
        m[f"f{i}_cwb"] = cwb
        m[f"f{i}_gn"] = col(inp["ffn_norm"][i])
    return m


def run_module(inp, n_cores=8):
    x = np.asarray(inp["x"], dtype=np.float32)
    B, T, _ = x.shape
    nc = build_program(T)
    shared = host_layout(inp, T)
    in_maps = []
    for c in range(n_cores):
        b = c % B
        xin = np.zeros((D, T + 2 * PAD), np.float32)
        xin[:, PAD:PAD + T] = x[b].T
        mp = dict(shared); mp["xin"] = xin
        in_maps.append(mp)
    res = run_bass_kernel_spmd(nc, in_maps, core_ids=list(range(n_cores)))
    out = np.stack([np.ascontiguousarray(res.results[b]["xout"][:, PAD:PAD + T].T) for b in range(B)], 0)
    return out.astype(np.float32)


def kernel(**inputs):
    return run_module(inputs)
```
